# Optimizing a Trainium2 kernel written in Bass

```python
import math
import jax, jax.numpy as jnp
from jax import lax
import numpy as np

D_MODEL = 1024
BATCH = 8
SEQ = 2048
DEPTH = 4

CHUNK = 64
N_MEM = 256
N_A_LAYERS = DEPTH // 2
N_B_LAYERS = DEPTH - N_A_LAYERS
MEM_HEADS = 4
MEM_HEAD_DIM = 64
MEM_WIDTH = MEM_HEADS * MEM_HEAD_DIM
TOK_WIDTH = D_MODEL - MEM_WIDTH
SSM_GROUP = 16
SSM_GROUPS = TOK_WIDTH // SSM_GROUP
SSM_STATE = 64
DIFF_HEAD_DIM = 64
DIFF_HEADS = TOK_WIDTH // (2 * DIFF_HEAD_DIM)
DIFF_V_DIM = 2 * DIFF_HEAD_DIM
D_FF = ((math.ceil(8 * D_MODEL / 3) + 255) // 256) * 256
Q_BLOCK = 128
EPS = 1e-6
DT_MIN = 0.001
DT_MAX = 0.1

kernel_name = "hybrid_s5_diffattn_yoco_encoder"


def rms_norm(x, g):
    xf = x.astype(jnp.float32)
    y = xf * lax.rsqrt(jnp.mean(xf * xf, axis=-1, keepdims=True) + EPS)
    return (y * g.astype(jnp.float32)).astype(x.dtype)


def s5_mixer(u, a_re, a_im, log_dt, b_re, b_im, c_re, c_im, d_skip, w_glu):
    bsz, L, _ = u.shape
    f32 = jnp.float32
    uf = u.astype(f32).reshape(bsz, L, SSM_GROUPS, SSM_GROUP)
    a_re = a_re.astype(f32); a_im = a_im.astype(f32)
    dt = jnp.exp(log_dt.astype(f32))[:, None]
    mag = jnp.exp(dt * a_re)
    abar_re = mag * jnp.cos(dt * a_im)
    abar_im = mag * jnp.sin(dt * a_im)
    den = a_re * a_re + a_im * a_im
    nr = abar_re - 1.0
    coef_re = (nr * a_re + abar_im * a_im) / den
    coef_im = (abar_im * a_re - nr * a_im) / den
    b_re = b_re.astype(f32); b_im = b_im.astype(f32)
    bbar_re = coef_re[..., None] * b_re - coef_im[..., None] * b_im
    bbar_im = coef_re[..., None] * b_im + coef_im[..., None] * b_re
    bu_re = jnp.einsum('blgp,gnp->blgn', uf, bbar_re)
    bu_im = jnp.einsum('blgp,gnp->blgn', uf, bbar_im)
    a_seq_re = jnp.broadcast_to(abar_re, (1, L, SSM_GROUPS, SSM_STATE))
    a_seq_im = jnp.broadcast_to(abar_im, (1, L, SSM_GROUPS, SSM_STATE))

    def combine(left, right):
        al_re, al_im, bl_re, bl_im = left
        ar_re, ar_im, br_re, br_im = right
        return (al_re * ar_re - al_im * ar_im,
                al_re * ar_im + al_im * ar_re,
                ar_re * bl_re - ar_im * bl_im + br_re,
                ar_re * bl_im + ar_im * bl_re + br_im)

    _, _, s_re, s_im = lax.associative_scan(combine, (a_seq_re, a_seq_im, bu_re, bu_im), axis=1)
    y = (jnp.einsum('blgn,gpn->blgp', s_re, c_re.astype(f32))
         - jnp.einsum('blgn,gpn->blgp', s_im, c_im.astype(f32)))
    y = y.reshape(bsz, L, TOK_WIDTH) + d_skip.astype(f32) * uf.reshape(bsz, L, TOK_WIDTH)
    h = jax.nn.gelu(y).astype(u.dtype)
    return h * jax.nn.sigmoid(h @ w_glu)


def memory_attention(q, mem_n, wk, wv, q_g, k_g):
    bsz, L, _ = q.shape
    M = mem_n.shape[1]
    qh = rms_norm(q.reshape(bsz, L, MEM_HEADS, MEM_HEAD_DIM), q_g)
    kh = rms_norm((mem_n @ wk).reshape(bsz, M, MEM_HEADS, MEM_HEAD_DIM), k_g)
    vh = (mem_n @ wv).reshape(bsz, M, MEM_HEADS, MEM_HEAD_DIM)
    s = jnp.einsum('blhd,bmhd->bhlm', qh, kh).astype(jnp.float32) * (MEM_HEAD_DIM ** -0.5)
    p = jax.nn.softmax(s, axis=-1)
    o = jnp.einsum('bhlm,bmhd->blhd', p.astype(vh.dtype), vh)
    return o.reshape(bsz, L, MEM_WIDTH)


def diff_attention(q, k, v, lam, sub_g, lambda_init):
    bsz, L = q.shape[0], q.shape[1]
    nb = L // Q_BLOCK
    qb = q.reshape(bsz, nb, Q_BLOCK, 2, DIFF_HEADS, DIFF_HEAD_DIM).transpose(1, 0, 2, 3, 4, 5)
    key_chunk = jnp.arange(L) // CHUNK
    scale = DIFF_HEAD_DIM ** -0.5

    def one_block(args):
        qi, idx = args
        q_chunk = (idx * Q_BLOCK + jnp.arange(Q_BLOCK)) // CHUNK
        mask = key_chunk[None, :] <= q_chunk[:, None]
        s = jnp.einsum('bqchd,bkchd->bchqk', qi, k).astype(jnp.float32) * scale
        s = jnp.where(mask, s, -jnp.inf)
        p = jax.nn.softmax(s, axis=-1)
        a = p[:, 0] - lam * p[:, 1]
        return jnp.einsum('bhqk,bkhe->bqhe', a.astype(v.dtype), v)

    out = lax.map(one_block, (qb, jnp.arange(nb)))
    out = out.transpose(1, 0, 2, 3, 4).reshape(bsz, L, DIFF_HEADS, DIFF_V_DIM)
    out = rms_norm(out, sub_g) * (1.0 - lambda_init)
    return out.reshape(bsz, L, TOK_WIDTH)


def swiglu(h, wg, wu, wd):
    return (jax.nn.silu(h @ wg) * (h @ wu)) @ wd


def setup_inputs(seed: int = 0) -> dict:
    key = jax.random.key(seed)
    ks = iter(jax.random.split(key, 40))
    f32 = jnp.float32

    def nrm(shape, scale):
        return jax.random.normal(next(ks), shape, f32) * scale

    def gain(shape):
        return 1.0 + nrm(shape, 0.02)

    G, N, P = SSM_GROUPS, SSM_STATE, SSM_GROUP
    a_im_init = jnp.broadcast_to(jnp.pi * jnp.arange(N, dtype=f32), (N_A_LAYERS, G, N))
    return {
        "x": nrm((BATCH, SEQ, D_MODEL), 1.0),
        "mem": nrm((BATCH, N_MEM, D_MODEL), 1.0),
        "norm_mix": gain((DEPTH, D_MODEL)),
        "norm_ffn": gain((DEPTH, D_MODEL)),
        "norm_mem": gain((DEPTH, D_MODEL)),
        "w_in": nrm((DEPTH, D_MODEL, D_MODEL), D_MODEL ** -0.5),
        "w_out": nrm((DEPTH, D_MODEL, D_MODEL), D_MODEL ** -0.5),
        "mem_wk": nrm((DEPTH, D_MODEL, MEM_WIDTH), D_MODEL ** -0.5),
        "mem_wv": nrm((DEPTH, D_MODEL, MEM_WIDTH), D_MODEL ** -0.5),
        "mem_q_norm": gain((DEPTH, MEM_HEAD_DIM)),
        "mem_k_norm": gain((DEPTH, MEM_HEAD_DIM)),
        "ffn_w_gate": nrm((DEPTH, D_MODEL, D_FF), D_MODEL ** -0.5),
        "ffn_w_up": nrm((DEPTH, D_MODEL, D_FF), D_MODEL ** -0.5),
        "ffn_w_down": nrm((DEPTH, D_FF, D_MODEL), D_FF ** -0.5),
        "ssm_a_re": -0.5 + nrm((N_A_LAYERS, G, N), 0.01),
        "ssm_a_im": a_im_init + nrm((N_A_LAYERS, G, N), 0.01),
        "ssm_log_dt": jax.random.uniform(next(ks), (N_A_LAYERS, G), f32,
                                         minval=math.log(DT_MIN), maxval=math.log(DT_MAX)),
        "ssm_b_re": nrm((N_A_LAYERS, G, N, P), (2 * P) ** -0.5),
        "ssm_b_im": nrm((N_A_LAYERS, G, N, P), (2 * P) ** -0.5),
        "ssm_c_re": nrm((N_A_LAYERS, G, P, N), (2 * N) ** -0.5),
        "ssm_c_im": nrm((N_A_LAYERS, G, P, N), (2 * N) ** -0.5),
        "ssm_d": nrm((N_A_LAYERS, TOK_WIDTH), 0.5),
        "ssm_w_glu": nrm((N_A_LAYERS, TOK_WIDTH, TOK_WIDTH), TOK_WIDTH ** -0.5),
        "kv_norm": gain((D_MODEL,)),
        "diff_wk": nrm((D_MODEL, TOK_WIDTH), D_MODEL ** -0.5),
        "diff_wv": nrm((D_MODEL, TOK_WIDTH), D_MODEL ** -0.5),
        "diff_k_norm": gain((DIFF_HEAD_DIM,)),
        "diff_q_norm": gain((N_B_LAYERS, DIFF_HEAD_DIM)),
        "diff_lambda_q1": nrm((N_B_LAYERS, DIFF_HEAD_DIM), 0.1),
        "diff_lambda_k1": nrm((N_B_LAYERS, DIFF_HEAD_DIM), 0.1),
        "diff_lambda_q2": nrm((N_B_LAYERS, DIFF_HEAD_DIM), 0.1),
        "diff_lambda_k2": nrm((N_B_LAYERS, DIFF_HEAD_DIM), 0.1),
        "diff_sub_norm": gain((N_B_LAYERS, DIFF_V_DIM)),
    }


def reference(x, mem, norm_mix, norm_ffn, norm_mem, w_in, w_out, mem_wk, mem_wv,
              mem_q_norm, mem_k_norm, ffn_w_gate, ffn_w_up, ffn_w_down,
              ssm_a_re, ssm_a_im, ssm_log_dt, ssm_b_re, ssm_b_im, ssm_c_re, ssm_c_im,
              ssm_d, ssm_w_glu, kv_norm, diff_wk, diff_wv, diff_k_norm, diff_q_norm,
              diff_lambda_q1, diff_lambda_k1, diff_lambda_q2, diff_lambda_k2, diff_sub_norm):
    bsz, L, _ = x.shape
    k_shared = None
    v_shared = None
    for layer in range(DEPTH):
        h = rms_norm(x, norm_mix[layer])
        z = h @ w_in[layer]
        z_tok, z_mem = z[..., :TOK_WIDTH], z[..., TOK_WIDTH:]
        mem_n = rms_norm(mem, norm_mem[layer])
        o_mem = memory_attention(z_mem, mem_n, mem_wk[layer], mem_wv[layer],
                                 mem_q_norm[layer], mem_k_norm[layer])
        if layer < N_A_LAYERS:
            i = layer
            o_tok = s5_mixer(z_tok, ssm_a_re[i], ssm_a_im[i], ssm_log_dt[i], ssm_b_re[i],
                             ssm_b_im[i], ssm_c_re[i], ssm_c_im[i], ssm_d[i], ssm_w_glu[i])
        else:
            j = layer - N_A_LAYERS
            lambda_init = 0.8 - 0.6 * math.exp(-0.3 * layer)
            lam = (jnp.exp(jnp.sum(diff_lambda_q1[j].astype(jnp.float32) * diff_lambda_k1[j].astype(jnp.float32)))
                   - jnp.exp(jnp.sum(diff_lambda_q2[j].astype(jnp.float32) * diff_lambda_k2[j].astype(jnp.float32)))
                   + lambda_init)
            q = rms_norm(z_tok.reshape(bsz, L, 2, DIFF_HEADS, DIFF_HEAD_DIM), diff_q_norm[j])
            o_tok = diff_attention(q, k_shared, v_shared, lam, diff_sub_norm[j], lambda_init)
        x = x + jnp.concatenate([o_tok.astype(x.dtype), o_mem.astype(x.dtype)], axis=-1) @ w_out[layer]
        h = rms_norm(x, norm_ffn[layer])
        x = x + swiglu(h, ffn_w_gate[layer], ffn_w_up[layer], ffn_w_down[layer])
        if layer == N_A_LAYERS - 1:
            hk = rms_norm(x, kv_norm)
            k_shared = rms_norm((hk @ diff_wk).reshape(bsz, L, 2, DIFF_HEADS, DIFF_HEAD_DIM), diff_k_norm)
            v_shared = (hk @ diff_wv).reshape(bsz, L, DIFF_HEADS, DIFF_V_DIM)
    return x
```

```python
import contextlib
import numpy as np
import concourse.bass as bass
import concourse.mybir as mybir
from concourse.bass_utils import run_bass_kernel_spmd

F32 = mybir.dt.float32
BF16 = mybir.dt.bfloat16
AF = mybir.ActivationFunctionType
ALU = mybir.AluOpType
AX = mybir.AxisListType

SEM_ROT = 20000


class Buf:
    __slots__ = ("name", "w", "r")

    def __init__(self, name=""):
        self.name = name
        self.w = None
        self.r = []


class Prog:
    ENGS = ("tensor", "vector", "scalar", "gpsimd", "sync")

    def __init__(self):
        self.nc = bass.Bass("TRN2", target_bir_lowering=False)
        self.stack = contextlib.ExitStack()
        self.rec = {e: [] for e in self.ENGS}
        self.sems = {}
        self.cur = {}
        self.epoch = {e: 0 for e in self.ENGS}
        self.known = {e: {} for e in self.ENGS}
        self.dma_cnt = {}
        for e in self.ENGS:
            self._new_epoch(e)

    def sem(self, key):
        if key not in self.sems:
            self.sems[key] = self.stack.enter_context(self.nc.semaphore(str(key)))
        return self.sems[key]

    def _new_epoch(self, e):
        self.epoch[e] += 1
        key = (e, self.epoch[e])
        self.sem(key)
        self.cur[e] = [key, 0]

    def sbuf(self, name, shape, dt):
        return self.stack.enter_context(self.nc.sbuf_tensor(name, list(shape), dt))

    def psum(self, name, shape, dt=F32):
        return self.stack.enter_context(self.nc.psum_tensor(name, list(shape), dt))

    def dram(self, name, shape, dt, kind):
        return self.nc.dram_tensor(name, list(shape), dt, kind=kind).ap()

    def _deps(self, eng, reads, writes):
        toks = []
        for b in reads:
            if b.w is not None:
                toks.append(b.w)
        for b in writes:
            if b.w is not None:
                toks.append(b.w)
            toks.extend(b.r)
        need = {}
        for (k, v) in toks:
            if eng == "tensor" and k[0] == "tensor":
                continue
            if self.known[eng].get(k, 0) >= v:
                continue
            if need.get(k, 0) < v:
                need[k] = v
        for k, v in need.items():
            self.known[eng][k] = v
        return [(self.sems[k], v) for k, v in need.items()]

    def op(self, eng, fn, reads=(), writes=()):
        waits = self._deps(eng, reads, writes)
        cur = self.cur[eng]
        if cur[1] >= SEM_ROT:
            self._new_epoch(eng)
            cur = self.cur[eng]
        cur[1] += 1
        tok = (cur[0], cur[1])
        self.rec[eng].append((waits, fn, self.sems[cur[0]], 1))
        for b in writes:
            b.w = tok
            b.r = []
        for b in reads:
            if b not in writes:
                b.r.append(tok)
        return tok

    def dma(self, fn, semkey, reads=(), writes=(), eng="sync"):
        waits = self._deps(eng, reads, writes)
        self.sem(semkey)
        self.dma_cnt[semkey] = self.dma_cnt.get(semkey, 0) + 16
        tok = (semkey, self.dma_cnt[semkey])
        self.rec[eng].append((waits, fn, self.sems[semkey], 16))
        for b in writes:
            b.w = tok
            b.r = []
        for b in reads:
            if b not in writes:
                b.r.append(tok)
        return tok

    def wait_all(self, eng, bufs):
        waits = self._deps(eng, bufs, ())
        self.rec[eng].append((waits, None, None, 0))


    def barrier(self):
        toks = []
        for e in self.ENGS:
            for ep in range(1, self.epoch[e] + 1):
                k = (e, ep)
                v = self.cur[e][1] if ep == self.epoch[e] else None
                if v is None:
                    continue
                if v > 0:
                    toks.append((k, v))
        for k, v in self.dma_cnt.items():
            toks.append((k, v))
        for e in self.ENGS:
            waits = []
            for (k, v) in toks:
                if k[0] == e and isinstance(k, tuple) and k[0] in self.ENGS and e != "sync":
                    pass
                if self.known[e].get(k, 0) >= v:
                    continue
                self.known[e][k] = v
                waits.append((self.sems[k], v))
            if waits:
                self.rec[e].append((waits, None, None, 0))

    def finish(self):
        nc = self.nc
        rec = self.rec

        def replay(e, name):
            for (waits, fn, sem, inc) in rec[name]:
                for (s, v) in waits:
                    e.wait_ge(s, v)
                if fn is not None:
                    ins = fn(e)
                    ins.then_inc(sem, inc)

        with nc.Block() as block:
            @block.tensor
            def _(e):
                replay(e, "tensor")

            @block.vector
            def _(e):
                replay(e, "vector")

            @block.scalar
            def _(e):
                replay(e, "scalar")

            @block.gpsimd
            def _(e):
                replay(e, "gpsimd")

            @block.sync
            def _(e):
                replay(e, "sync")
        self.stack.close()
        return nc

import math

EPS = 1e-6
NL = 4
BLK = 256
NBLK = 8
DFF = 2816
NF = 22
FC = 4


def _cst_layout():
    off = {}
    n = 0
    for name, w in [("gmix", 32), ("gffn", 32), ("gmem", 32), ("gkv", 8), ("mqn", 4), ("mkn", 4),
                    ("dqn", 2), ("dkn", 1), ("dsn", 2), ("lam", 512), ("dF", 12)]:
        off[name] = (n, w)
        n += w
    return off, n


CST_OFF, NCST = _cst_layout()


def build(n_layers=NL):
    P = Prog()
    nc = P.nc
    dI = lambda name, shape: P.dram(name, shape, F32, "ExternalInput")
    xT_d = dI("xT", [1024, 2048])
    memT_d = dI("memT", [1024, 256])
    cst_d = dI("cst", [128, NCST])
    w_in_d = dI("w_in", [4, 1024, 1024])
    w_out_d = dI("w_out", [4, 1024, 1024])
    wk_d = dI("mem_wk", [4, 1024, 256])
    wv_d = dI("mem_wv", [4, 1024, 256])
    wg_d = dI("ffn_w_gate", [4, 1024, DFF])
    wu_d = dI("ffn_w_up", [4, 1024, DFF])
    wd_d = dI("ffn_w_down", [4, DFF, 1024])
    wglu_d = dI("ssm_w_glu", [2, 768, 768])
    dwk_d = dI("diff_wk", [1024, 768])
    dwv_d = dI("diff_wv", [1024, 768])
    s5F_d = dI("s5F", [2, 128, 6, 4, 128])
    s5dtF_d = dI("s5dtF", [2, 128, 6])
    s5M_d = dI("s5M", [2, 128, 3, 24])
    s5C_d = dI("s5C", [2, 128, 2, 24, 32])
    yT_d = P.dram("yT", [1024, 2048], F32, "ExternalOutput")

    xT = P.sbuf("xT_sb", [128, 8, 2048], F32)
    MIX = P.sbuf("MIX", [128, 24576], BF16)
    ARENA = P.sbuf("ARENA", [128, 28672], BF16)
    stg = [P.sbuf(f"stg{i}", [128, 1024], F32) for i in range(2)]
    wgb = [P.sbuf(f"wgb{i}", [128, 8, 128], BF16) for i in range(4)]
    cst = P.sbuf("cst_sb", [128, NCST], F32)
    ones_bf = P.sbuf("ones_bf", [128, 128], BF16)
    blk64 = P.sbuf("blk64", [128, 128], BF16)
    epsT = P.sbuf("epsT", [128, 2], F32)
    KmT = P.sbuf("KmT", [128, 2, 256], BF16)
    Vm = P.sbuf("Vm", [128, 2, 256], BF16)
    sq = [P.sbuf(f"sq{i}", [128, 256], BF16) for i in range(2)]
    rstd = [P.sbuf(f"rstd{i}", [128, 256], F32) for i in range(2)]
    sgt = [P.sbuf(f"sgt{i}", [128, 512], BF16) for i in range(2)]
    eM = [P.sbuf(f"eM{i}", [128, 256], BF16) for i in range(2)]
    rrm = P.sbuf("rrm", [128, 256], F32)
    TMP = P.sbuf("TMP", [128, 2320], F32)
    small = P.sbuf("small", [128, 64], F32)
    PS = [P.psum(f"ps{i}", [128, 512]) for i in range(8)]

    B = {}

    def bf(name):
        if name not in B:
            B[name] = Buf(name)
        return B[name]

    def cs(name, i=0, n=1):
        o, w = CST_OFF[name]
        return cst[:, o + i:o + i + n]

    w_in_bf = ARENA[:, 0:8192].rearrange("p (k m) -> p k m", k=8)
    w_out_bf = ARENA[:, 8192:16384].rearrange("p (k m) -> p k m", k=8)
    h_blk = ARENA[:, 16384:18432].rearrange("p (k n) -> p k n", k=8)
    z_blk = ARENA[:, 18432:20480].rearrange("p (k n) -> p k n", k=8)
    cat_blk = ARENA[:, 20480:22528].rearrange("p (k n) -> p k n", k=8)
    SPARE = ARENA[:, 22528:28672]
    hT = ARENA[:, 0:16384].rearrange("p (k n) -> p k n", k=8)
    aT = ARENA[:, 16384:24576].rearrange("p (f n) -> p f n", f=FC)
    wdb = ARENA[:, 24576:28672].rearrange("p (f n) -> p f n", f=FC)
    KdT = MIX[:, 0:12288].rearrange("p (t n) -> p t n", t=6)
    Vd = MIX[:, 12288:24576].rearrange("p (t n) -> p t n", t=16)
    WB = MIX[:, 0:6144].rearrange("p (t k r n) -> p t k r n", t=6, k=4, r=2)
    CA = MIX[:, 6144:13824].rearrange("p (a e r n) -> p a e r n", a=24, e=5, r=2)
    Rtab = MIX[:, 13824:19968].bitcast(F32).rearrange("p (r a c) -> p r a c", r=2, a=24)
    wglu_bf = MIX[:, 19968:24576].rearrange("p (k m) -> p k m", k=6)
    memT_sb = SPARE[:, 0:4096].bitcast(F32).rearrange("p (k n) -> p k n", k=8)
    mem_n = SPARE[:, 4096:6144].rearrange("p (k n) -> p k n", k=8)
    wk_bf = ARENA[:, 16384:18432].rearrange("p (k n) -> p k n", k=8)
    wv_bf = ARENA[:, 18432:20480].rearrange("p (k n) -> p k n", k=8)

    PAR = {"p": 0, "mbank": None}
    h_sets = [h_blk, SPARE[:, 0:2048].rearrange("p (k n) -> p k n", k=8)]
    z_sets = [z_blk, SPARE[:, 2048:4096].rearrange("p (k n) -> p k n", k=8)]
    catm_sets = [cat_blk[:, 6:8, :], SPARE[:, 5632:6144].rearrange("p (k n) -> p k n", k=2)]

    def HB():
        return h_sets[PAR["p"]]

    def ZB():
        return z_sets[PAR["p"]]

    def CATM():
        return catm_sets[PAR["p"]]

    def sfx():
        return "" if PAR["p"] == 0 else "_1"

    def V(fn, reads=(), writes=(), eng="vector"):
        return P.op(eng, fn, reads=reads, writes=writes)

    def tt(out, a, b, op, reads, writes, eng="vector"):
        return P.op(eng, lambda e: e.tensor_tensor(out=out, in0=a, in1=b, op=op), reads=reads, writes=writes)

    def stt(out, a, s, b, op0, op1, reads, writes):
        return P.op("vector", lambda e: e.scalar_tensor_tensor(out=out, in0=a, scalar=s, in1=b, op0=op0, op1=op1),
                    reads=reads, writes=writes)

    def ts(out, a, s1, s2, op0, op1, reads, writes, eng="vector"):
        return P.op(eng, lambda e: e.tensor_scalar(out=out, in0=a, scalar1=s1, scalar2=s2, op0=op0, op1=op1),
                    reads=reads, writes=writes)

    def act(out, a, func, reads, writes, scale=1.0, bias=None):
        if bias is None:
            return P.op("scalar", lambda e: e.activation(out=out, in_=a, func=func, scale=scale), reads=reads, writes=writes)
        return P.op("scalar", lambda e: e.activation(out=out, in_=a, func=func, scale=scale, bias=bias), reads=reads, writes=writes)

    def mm(out, lhsT, rhs, start, stop, reads, writes):
        return P.op("tensor", lambda e: e.matmul(out, lhsT=lhsT, rhs=rhs, start=start, stop=stop), reads=reads, writes=writes)

    stg_i = [0]

    cast_rot = [0]

    def wload(src, dst, dstbuf, shape3=None, engs=("gpsimd",)):
        i = stg_i[0] % 2
        stg_i[0] += 1
        n = 1
        for d in src.shape[1:]:
            n *= d
        sv = stg[i][:, 0:n]
        if len(src.shape) == 3:
            sv = sv.rearrange("p (a b) -> p a b", a=src.shape[1])
        P.dma(lambda e: e.dma_start(out=sv, in_=src), ("stg", i), writes=[bf(f"stg{i}")])
        ce = engs[cast_rot[0] % len(engs)]
        cast_rot[0] += 1
        if ce == "scalar":
            act(dst, sv, AF.Copy, [bf(f"stg{i}")], [dstbuf])
        else:
            P.op(ce, lambda e: e.tensor_copy(out=dst, in_=sv), reads=[bf(f"stg{i}")], writes=[dstbuf])

    psn = {}

    def psb(lo, hi):
        k = (lo, hi)
        psn[k] = psn.get(k, -1) + 1
        return lo + psn[k] % (hi - lo)

    rot = {}

    def rr(name, n):
        rot[name] = rot.get(name, -1) + 1
        return rot[name] % n

    def rstd_from(ps_ap, ncols, scale, reads):
        i = rr("rstd", 2)
        r = rstd[i][:, 0:ncols]
        act(r, ps_ap, AF.Ln, reads, [bf(f"rstd{i}")], scale=scale, bias=epsT[:, 0:1])
        act(r, r, AF.Exp, [bf(f"rstd{i}")], [bf(f"rstd{i}")], scale=-0.5)
        return r, bf(f"rstd{i}")

    def group_norm_evac(ps_ap, psbuf, ncols, gain, dst, dstbufs, full, split=None, defer=False):
        i = rr("sq", 2)
        s = sq[i][:, 0:ncols]
        act(s, ps_ap, AF.Square, [psbuf], [bf(f"sq{i}")])
        if defer:
            return lambda: _gne2(i, s, ps_ap, psbuf, ncols, gain, dst, dstbufs, full, split)
        _gne2(i, s, ps_ap, psbuf, ncols, gain, dst, dstbufs, full, split)

    def _gne2(i, s, ps_ap, psbuf, ncols, gain, dst, dstbufs, full, split):
        b2 = psb(2, 4)
        mm(PS[b2][:, 0:ncols], ones_bf[:] if full else blk64[:], s, True, True, [bf(f"sq{i}"), bf("consts")], [bf(f"ps{b2}")])
        r, rb = rstd_from(PS[b2][:, 0:ncols], ncols, 1.0 / (128 if full else 64), [bf(f"ps{b2}")])
        if split is None:
            stt(dst, ps_ap, gain, r, ALU.mult, ALU.mult, [psbuf, rb, bf("cst")], dstbufs)
        else:
            lo_, hi_ = split
            stt(lo_[0:64, :], ps_ap[0:64, :], gain[0:64, :], r[0:64, :], ALU.mult, ALU.mult, [psbuf, rb, bf("cst")], dstbufs)
            stt(hi_[64:128, :], ps_ap[64:128, :], gain[64:128, :], r[64:128, :], ALU.mult, ALU.mult, [psbuf, rb, bf("cst")], dstbufs)

    zhi_sets = [[TMP[:, 1280 + 128 * m_:1408 + 128 * m_].bitcast(BF16) for m_ in range(6)],
                [SPARE[:, 4096 + 256 * m_:4352 + 256 * m_] for m_ in range(6)]]

    def ZHI(m_):
        return zhi_sets[PAR["p"]][m_]

    def xnorm(blk, gname, gl, dst, dstbuf):
        c0 = blk * BLK
        b2 = psb(2, 4)
        for k in range(8):
            i = rr("sq", 2)
            act(sq[i][:], xT[:, k, c0:c0 + BLK], AF.Square, [bf(f"x{k}_{blk}")], [bf(f"sq{i}")])
            mm(PS[b2][:, 0:BLK], ones_bf[:], sq[i][:], k == 0, k == 7, [bf(f"sq{i}"), bf("consts")], [bf(f"ps{b2}")])
        r, rb = rstd_from(PS[b2][:, 0:BLK], BLK, 1.0 / 1024, [bf(f"ps{b2}")])
        for k in range(8):
            stt(dst[:, k, :], xT[:, k, c0:c0 + BLK], cs(gname, gl * 8 + k), r, ALU.mult, ALU.mult,
                [bf(f"x{k}_{blk}"), rb, bf("cst")], [dstbuf])

    P.dma(lambda e: e.dma_start(out=cst[:], in_=cst_d[:, :]), "cstld", writes=[bf("cst")])
    V(lambda e: e.memset(ones_bf[:], 1.0), writes=[bf("consts")])
    V(lambda e: e.memset(blk64[:], 0.0), writes=[bf("consts")])
    V(lambda e: e.memset(blk64[0:64, 0:64], 1.0), writes=[bf("consts")])
    V(lambda e: e.memset(blk64[64:128, 64:128], 1.0), writes=[bf("consts")])
    V(lambda e: e.memset(epsT[:, 0:1], EPS), writes=[bf("consts")])
    V(lambda e: e.memset(epsT[:, 1:2], math.pi / 2), writes=[bf("consts")])
    xsrc = xT_d.rearrange("(k p) n -> p k n", p=128)
    for k in range(8):
        for hh in range(2):
            P.dma(lambda e, k=k, hh=hh: e.dma_start(out=xT[:, k, hh * 1024:(hh + 1) * 1024], in_=xsrc[:, k, hh * 1024:(hh + 1) * 1024]),
                  "xld", writes=[bf(f"x{k}_{b}") for b in range(hh * 4, hh * 4 + 4)])

    def lam_prep():
        lamv = cs("lam", 0, 512).rearrange("p (a j d) -> p a j d", a=4, j=2)
        tmp = TMP[:, 0:64]
        for j in range(2):
            layer = 2 + j
            linit = 0.8 - 0.6 * math.exp(-0.3 * layer)
            for a in range(2):
                tt(tmp, lamv[:, 2 * a, j, :], lamv[:, 2 * a + 1, j, :], ALU.mult, [bf("cst")], [bf("lamtmp")])
                V(lambda e, a=a, j=j: e.reduce_sum(out=small[:, 8 + a:9 + a], in_=tmp, axis=AX.X), [bf("lamtmp")], [bf("small")])
                act(small[:, 8 + a:9 + a], small[:, 8 + a:9 + a], AF.Exp, [bf("small")], [bf("small")])
            tt(small[:, 10:11], small[:, 9:10], small[:, 8:9], ALU.subtract, [bf("small")], [bf("small")])
            ts(small[:, j:j + 1], small[:, 10:11], -linit, None, ALU.add, ALU.bypass, [bf("small")], [bf("small")])
            ts(small[:, 2 + j:3 + j], cs("dsn", j), 1.0 - linit, None, ALU.mult, ALU.bypass, [bf("cst")], [bf("small")])

    lam_prep()

    def layer_start(l):
        wi = w_in_d[l].rearrange("(k p) m -> p k m", p=128)
        wo = w_out_d[l].rearrange("(k p) m -> p k m", p=128)
        for m in range(8):
            wload(wi[:, :, m * 128:(m + 1) * 128], w_in_bf[:, :, m * 128:(m + 1) * 128], bf("w_in"), engs=("scalar", "vector", "gpsimd"))
        for m in range(8):
            wload(wo[:, :, m * 128:(m + 1) * 128], w_out_bf[:, :, m * 128:(m + 1) * 128], bf("w_out"), engs=("scalar", "vector", "gpsimd"))
        msrc = memT_d.rearrange("(k p) n -> p k n", p=128)
        P.dma(lambda e: e.dma_start(out=memT_sb, in_=msrc), "memld", writes=[bf("memT")])
        wks = wk_d[l].rearrange("(k p) m -> p k m", p=128)
        wvs = wv_d[l].rearrange("(k p) m -> p k m", p=128)
        for j in range(2):
            wload(wks[:, :, j * 128:(j + 1) * 128], wk_bf[:, :, j * 128:(j + 1) * 128], bf("wk"), engs=("scalar", "vector"))
            wload(wvs[:, :, j * 128:(j + 1) * 128], wv_bf[:, :, j * 128:(j + 1) * 128], bf("wv"), engs=("scalar", "vector"))
        b2 = psb(2, 4)
        for k in range(8):
            i = rr("sq", 2)
            act(sq[i][:], memT_sb[:, k, :], AF.Square, [bf("memT")], [bf(f"sq{i}")])
            mm(PS[b2][:, 0:256], ones_bf[:], sq[i][:], k == 0, k == 7, [bf(f"sq{i}"), bf("consts")], [bf(f"ps{b2}")])
        r, rb = rstd_from(PS[b2][:, 0:256], 256, 1.0 / 1024, [bf(f"ps{b2}")])
        for k in range(8):
            stt(mem_n[:, k, :], memT_sb[:, k, :], cs("gmem", l * 8 + k), r, ALU.mult, ALU.mult,
                [bf("memT"), rb, bf("cst")], [bf("mem_n")])
        for j in range(2):
            b0 = psb(0, 2)
            for k in range(8):
                mm(PS[b0][:, 0:256], wk_bf[:, k, j * 128:(j + 1) * 128], mem_n[:, k, :], k == 0, k == 7,
                   [bf("wk"), bf("mem_n")], [bf(f"ps{b0}")])
            group_norm_evac(PS[b0][:, 0:256], bf(f"ps{b0}"), 256, cs("mkn", l), KmT[:, j, :], [bf("KmT")], False)
        for i2 in range(2):
            b0 = psb(0, 2)
            for k in range(8):
                mm(PS[b0][:, 0:256], mem_n[:, k, i2 * 128:(i2 + 1) * 128], wv_bf[:, k, :], k == 0, k == 7,
                   [bf("wv"), bf("mem_n")], [bf(f"ps{b0}")])
            act(Vm[:, i2, :], PS[b0][:, 0:256], AF.Copy, [bf(f"ps{b0}")], [bf("Vm")])

    def win_block(l, blk):
        hb_, zb_ = HB(), ZB()
        hbuf = bf("h_blk" + sfx())
        xnorm(blk, "gmix", l, hb_, hbuf)
        for m in range(8):
            b0 = psb(0, 2)
            for k in range(8):
                mm(PS[b0][:, 0:BLK], w_in_bf[:, k, m * 128:(m + 1) * 128], hb_[:, k, :], k == 0, k == 7,
                   [bf("w_in"), hbuf], [bf(f"ps{b0}")])
            zbuf = bf(f"z{m}" + sfx())
            if m >= 6:
                group_norm_evac(PS[b0][:, 0:BLK], bf(f"ps{b0}"), BLK, cs("mqn", l), zb_[:, m, :], [zbuf], False)
            elif l >= 2:
                group_norm_evac(PS[b0][:, 0:BLK], bf(f"ps{b0}"), BLK, cs("dqn", l - 2), None, [zbuf], False,
                                split=(zb_[:, m, :], ZHI(m)))
            else:
                act(zb_[:, m, :], PS[b0][:, 0:BLK], AF.Copy, [bf(f"ps{b0}")], [zbuf])

    def memattn_block(l, blk):
        zb_, cm_ = ZB(), CATM()
        for h in range(4):
            j = h // 2
            po = (h % 2) * 64
            if PAR["mbank"] is None:
                nb, dbk = 4 + (h % 2), 6 + (h % 2)
            else:
                nb, dbk = PAR["mbank"]
            zbuf = bf(f"z{6 + j}" + sfx())
            eis = []
            for i2 in range(2):
                b0 = psb(0, 2)
                mm(PS[b0][:, 0:BLK], KmT[po:po + 64, j, i2 * 128:(i2 + 1) * 128], zb_[po:po + 64, 6 + j, :], True, True,
                   [bf("KmT"), zbuf], [bf(f"ps{b0}")])
                ei = rr("eM", 2)
                act(eM[ei][:], PS[b0][:, 0:BLK], AF.Exp, [bf(f"ps{b0}")], [bf(f"eM{ei}")], scale=0.125)
                eis.append(ei)
            for i2 in range(2):
                ei = eis[i2]
                mm(PS[nb][po:po + 64, 0:BLK], Vm[:, i2, h * 64:(h + 1) * 64], eM[ei][:], i2 == 0, i2 == 1,
                   [bf("Vm"), bf(f"eM{ei}")], [bf(f"ps{nb}")])
                mm(PS[dbk][po:po + 64, 0:BLK], ones_bf[:, 0:64], eM[ei][:], i2 == 0, i2 == 1,
                   [bf("consts"), bf(f"eM{ei}")], [bf(f"ps{dbk}")])
            V(lambda e, po=po, dbk=dbk: e.reciprocal(out=rrm[po:po + 64, :], in_=PS[dbk][po:po + 64, 0:BLK]), [bf(f"ps{dbk}")], [bf(f"rrm{po}")])
            tt(cm_[po:po + 64, j, :], PS[nb][po:po + 64, 0:BLK], rrm[po:po + 64, :], ALU.mult,
               [bf(f"ps{nb}"), bf(f"rrm{po}")], [bf(f"cat{6 + j}" + sfx())])

    def wout_block(l, blk):
        c0 = blk * BLK
        cm_ = CATM()
        for m in range(8):
            b0 = psb(0, 2)
            for k in range(8):
                rhs = cat_blk[:, k, :] if k < 6 else cm_[:, k - 6, :]
                cbuf = bf(f"cat{k}") if k < 6 else bf(f"cat{k}" + sfx())
                mm(PS[b0][:, 0:BLK], w_out_bf[:, k, m * 128:(m + 1) * 128], rhs, k == 0, k == 7,
                   [bf("w_out"), cbuf], [bf(f"ps{b0}")])
            tt(xT[:, m, c0:c0 + BLK], xT[:, m, c0:c0 + BLK], PS[b0][:, 0:BLK], ALU.add,
               [bf(f"ps{b0}"), bf(f"x{m}_{blk}")], [bf(f"x{m}_{blk}")])

    def ffn(l):
        P.barrier()
        gsrc = wg_d[l].rearrange("(k p) f -> p k f", p=128)
        usrc = wu_d[l].rearrange("(k p) f -> p k f", p=128)
        dsrc = wd_d[l].rearrange("(f p) m -> p f m", p=128)
        f = 0
        while f < NF:
            nfc = min(FC, NF - f)
            for fi in range(nfc):
                ff = f + fi
                gi = rr("wgb", 2)
                wload(gsrc[:, :, ff * 128:(ff + 1) * 128], wgb[gi][:], bf(f"wg{gi}"))
                wload(usrc[:, :, ff * 128:(ff + 1) * 128], wgb[2 + gi][:], bf(f"wu{gi}"))
                wload(dsrc[:, ff, :], wdb[:, fi, :], bf(f"wd{fi}"))
                for tb in range(4):
                    if ff == 0:
                        for blk in (2 * tb, 2 * tb + 1):
                            xnorm(blk, "gffn", l, hT[:, :, blk * BLK:(blk + 1) * BLK], bf(f"hT{blk}"))
                    bg = psb(0, 2)
                    bu = psb(2, 4)
                    for k in range(8):
                        mm(PS[bg][:], wgb[gi][:, k, :], hT[:, k, tb * 512:(tb + 1) * 512], k == 0, k == 7,
                           [bf(f"wg{gi}"), bf(f"hT{2 * tb}"), bf(f"hT{2 * tb + 1}")], [bf(f"ps{bg}")])
                    for k in range(8):
                        mm(PS[bu][:], wgb[2 + gi][:, k, :], hT[:, k, tb * 512:(tb + 1) * 512], k == 0, k == 7,
                           [bf(f"wu{gi}"), bf(f"hT{2 * tb}"), bf(f"hT{2 * tb + 1}")], [bf(f"ps{bu}")])
                    si = rr("sgt", 2)
                    act(sgt[si][:], PS[bg][:], AF.Silu, [bf(f"ps{bg}")], [bf(f"sgt{si}")])
                    tt(aT[:, fi, tb * 512:(tb + 1) * 512], sgt[si][:], PS[bu][:], ALU.mult,
                       [bf(f"sgt{si}"), bf(f"ps{bu}")], [bf(f"aT{fi}_{tb}")])
            for tb in range(4):
                for m in range(8):
                    bd = psb(4, 8)
                    for fi in range(nfc):
                        mm(PS[bd][:], wdb[:, fi, m * 128:(m + 1) * 128], aT[:, fi, tb * 512:(tb + 1) * 512], fi == 0, fi == nfc - 1,
                           [bf(f"wd{fi}"), bf(f"aT{fi}_{tb}")], [bf(f"ps{bd}")])
                    tt(xT[:, m, tb * 512:(tb + 1) * 512], xT[:, m, tb * 512:(tb + 1) * 512], PS[bd][:], ALU.add,
                       [bf(f"ps{bd}"), bf(f"x{m}_{2 * tb}"), bf(f"x{m}_{2 * tb + 1}")], [bf(f"x{m}_{2 * tb}"), bf(f"x{m}_{2 * tb + 1}")])
            f += nfc
        P.barrier()

    def kv_shared():
        for blk in range(NBLK):
            xnorm(blk, "gkv", 0, hT[:, :, blk * BLK:(blk + 1) * BLK], bf(f"hT{blk // 2}"))
        ksrc = dwk_d.rearrange("(k p) f -> p k f", p=128)
        vsrc = dwv_d.rearrange("(k p) f -> p k f", p=128)
        for t in range(6):
            gi = rr("wgb", 2)
            wload(ksrc[:, :, t * 128:(t + 1) * 128], wgb[gi][:], bf(f"wg{gi}"), engs=("scalar", "gpsimd"))
            wload(vsrc[:, :, t * 128:(t + 1) * 128], wgb[2 + gi][:], bf(f"wu{gi}"), engs=("scalar", "gpsimd"))
            for blk in range(NBLK):
                b0 = psb(0, 2)
                for k in range(8):
                    mm(PS[b0][:, 0:BLK], wgb[gi][:, k, :], hT[:, k, blk * BLK:(blk + 1) * BLK], k == 0, k == 7,
                       [bf(f"wg{gi}"), bf(f"hT{blk // 2}")], [bf(f"ps{b0}")])
                group_norm_evac(PS[b0][:, 0:BLK], bf(f"ps{b0}"), BLK, cs("dkn", 0), KdT[:, t, blk * BLK:(blk + 1) * BLK],
                                [bf(f"KdT{blk}")], False)
                for tt_ in (2 * blk, 2 * blk + 1):
                    b4 = psb(4, 8)
                    for k in range(8):
                        mm(PS[b4][:, 0:128], hT[:, k, tt_ * 128:(tt_ + 1) * 128], wgb[2 + gi][:, k, :], k == 0, k == 7,
                           [bf(f"wu{gi}"), bf(f"hT{tt_ // 4}")], [bf(f"ps{b4}")])
                    V(lambda e, tt_=tt_, t=t, b4=b4: e.tensor_copy(out=Vd[:, tt_, t * 128:(t + 1) * 128], in_=PS[b4][:, 0:128]),
                      [bf(f"ps{b4}")], [bf(f"Vd{tt_}")])
        P.barrier()

    def diff_block(l, blk, hook=None):
        j = l - 2
        zb_ = ZB()
        zhi_ = [ZHI(m_) for m_ in range(6)]
        sf_ = sfx()
        TF = TMP
        r0 = TF[:, 0:256]
        r1 = TF[:, 256:512]
        t0 = TF[:, 512:768]
        t1 = TF[:, 768:1024]
        eT = [TF[:, 1024:1152].bitcast(BF16), TF[:, 1152:1280].bitcast(BF16)]
        nkt = 2 * blk + 2

        def loops(h, c):
            hp = h % 2
            idx = c * 6 + h
            zt = idx // 2
            po = (idx % 2) * 64
            ob, db = 4 + hp, 6 + hp
            pendq = []
            Es = [sgt[0], sgt[1], TF[:, 1024:1280].bitcast(BF16)]
            Eb = [bf("sgt0"), bf("sgt1"), bf("e3")]

            def pv(unit, ei):
                for ik, kt in enumerate(unit):
                    n0 = max(0, kt - 2 * blk) * 128
                    N = BLK - n0
                    rhs = Es[ei][:, ik * 256:ik * 256 + N]
                    mm(PS[ob][:, c * 256 + n0:c * 256 + BLK], Vd[:, kt, h * 128:(h + 1) * 128], rhs, kt == 0, kt == nkt - 1,
                       [bf(f"Vd{kt}"), Eb[ei]], [bf(f"ps{ob}")])
                    mm(PS[db][:, c * 256 + n0:c * 256 + BLK], ones_bf[:], rhs, kt == 0, kt == nkt - 1,
                       [bf("consts"), Eb[ei]], [bf(f"ps{db}")])

            units = [(2 * p_, 2 * p_ + 1) for p_ in range(blk)] + [(2 * blk, 2 * blk + 1)]
            zq = zb_[:, zt, :] if po == 0 else zhi_[zt]
            for unit in units:
                diag = unit[0] >= 2 * blk
                b0 = psb(0, 2)
                for ik, kt in enumerate(unit):
                    n0 = max(0, kt - 2 * blk) * 128
                    N = BLK - n0
                    mm(PS[b0][:, ik * 256:ik * 256 + N], KdT[:, zt, kt * 128:(kt + 1) * 128], zq[:, n0:BLK], True, True,
                       [bf(f"KdT{kt // 2}"), bf(f"z{zt}" + sf_)], [bf(f"ps{b0}")])
                W = 384 if diag else 512
                ei = rr("eT3", 3)
                act(Es[ei][:, 0:W], PS[b0][:, 0:W], AF.Exp, [bf(f"ps{b0}")], [Eb[ei]], scale=0.125)
                if diag:
                    P.op("gpsimd", lambda e, ei=ei: e.memset(Es[ei][64:128, 0:64], 0.0), writes=[Eb[ei]])
                    P.op("gpsimd", lambda e, ei=ei: e.memset(Es[ei][64:128, 256:320], 0.0), writes=[Eb[ei]])
                pendq.append((unit, ei))
                if len(pendq) > 2:
                    pv(*pendq.pop(0))
            while pendq:
                pv(*pendq.pop(0))

        st = {}

        def postA(h):
            hp = h % 2
            ob, db = 4 + hp, 6 + hp
            V(lambda e: e.reciprocal(out=r0, in_=PS[db][:, 0:256]), [bf(f"ps{db}")], [bf("r0")])
            V(lambda e: e.reciprocal(out=r1, in_=PS[db][:, 256:512]), [bf(f"ps{db}")], [bf("r1")])
            tt(t0, PS[ob][:, 0:256], r0, ALU.mult, [bf(f"ps{ob}"), bf("r0")], [bf("t0")])
            tt(t1, PS[ob][:, 256:512], r1, ALU.mult, [bf(f"ps{ob}"), bf("r1")], [bf("t1")])
            stt(t0, t1, small[:, j:j + 1], t0, ALU.mult, ALU.add, [bf("t0"), bf("t1"), bf("small")], [bf("t0")])

        def postM(h):
            i = rr("sq", 2)
            act(sq[i][:], t0, AF.Square, [bf("t0")], [bf(f"sq{i}")])
            b2 = psb(2, 4)
            mm(PS[b2][:, 0:BLK], ones_bf[:], sq[i][:], True, True, [bf(f"sq{i}"), bf("consts")], [bf(f"ps{b2}")])
            st["b2"] = b2

        def postB(h):
            b2 = st["b2"]
            r, rb = rstd_from(PS[b2][:, 0:BLK], BLK, 1.0 / 128, [bf(f"ps{b2}")])
            stt(cat_blk[:, h, :], t0, small[:, 2 + j:3 + j], r, ALU.mult, ALU.mult, [bf("t0"), rb, bf("small")], [bf(f"cat{h}")])

        loops(0, 0)
        loops(0, 1)
        for h in range(6):
            postA(h)
            if h < 5:
                loops(h + 1, 0)
            postM(h)
            if h < 5:
                loops(h + 1, 1)
            postB(h)
            if h == 2 and hook is not None:
                hook()

    SPF = SPARE.bitcast(F32)
    rho = TMP[:, 1200:1224]
    cs4 = TMP[:, 1224:1248]
    sn4 = TMP[:, 1248:1272]
    Sc = small[:, 16:64].rearrange("p (a r) -> p a r", r=2)

    def cmul(dr, di, ar, ai, br, bi, t1, t2, R, W):
        tt(t1, ar, br, ALU.mult, R, [bf("s5t")])
        tt(t2, ai, bi, ALU.mult, R, [bf("s5t")])
        tt(dr, t1, t2, ALU.subtract, [bf("s5t")], W)
        tt(t1, ar, bi, ALU.mult, R, [bf("s5t")])
        tt(t2, ai, br, ALU.mult, R, [bf("s5t")])
        tt(di, t1, t2, ALU.add, [bf("s5t")], W)

    def double_angle(c, s, t1, t2, n):
        for _ in range(n):
            tt(t1, c, c, ALU.mult, [bf("s5p")], [bf("s5t")])
            tt(t2, s, s, ALU.mult, [bf("s5p")], [bf("s5t")])
            stt(s, c, 2.0, s, ALU.mult, ALU.mult, [bf("s5p")], [bf("s5p")])
            tt(c, t1, t2, ALU.subtract, [bf("s5t")], [bf("s5p")])

    def s5_prep(l):
        i = l
        pb = [bf("s5p")]
        tb_ = [bf("s5t")]
        dtF = TMP[:, 1300:1306]
        dt64 = TMP[:, 1306:1312]
        P.dma(lambda e: e.dma_start(out=dtF, in_=s5dtF_d[i]), "s5ld", writes=pb)
        act(dtF, dtF, AF.Exp, pb, pb)
        ts(dt64, dtF, 1.0 / 64, None, ALU.mult, ALU.bypass, pb, pb)
        MIXF = MIX[:, 6144:24576].bitcast(F32)
        sl = [MIXF[:, 768 * q:768 * (q + 1)].rearrange("p (t n) -> p t n", t=6) for q in range(12)]
        A, I, mag, c, s, t1, t2, nr, cr, ci, Wr, Wi = sl
        rden, Br, Bi, Wr2, Wi2 = mag, A, I, mag, nr
        dtb = dtF.unsqueeze(2).to_broadcast([128, 6, 128])
        dt64b = dt64.unsqueeze(2).to_broadcast([128, 6, 128])
        P.dma(lambda e: e.dma_start(out=A, in_=s5F_d[i][:, :, 0, :]), "s5ld", writes=pb)
        P.dma(lambda e: e.dma_start(out=I, in_=s5F_d[i][:, :, 1, :]), "s5ld", writes=pb)
        tt(t1, A, dtb, ALU.mult, pb, tb_)
        act(mag, t1, AF.Exp, tb_, pb)
        tt(t1, I, dt64b, ALU.mult, pb, tb_)
        act(s, t1, AF.Sin, tb_, pb)
        act(c, t1, AF.Sin, tb_, pb, bias=epsT[:, 1:2])
        double_angle(c, s, t1, t2, 6)
        tt(c, mag, c, ALU.mult, pb, pb)
        tt(s, mag, s, ALU.mult, pb, pb)
        tt(t1, A, A, ALU.mult, pb, tb_)
        tt(t2, I, I, ALU.mult, pb, tb_)
        tt(t1, t1, t2, ALU.add, tb_, tb_)
        V(lambda e: e.reciprocal(out=rden, in_=t1), tb_, pb)
        ts(nr, c, -1.0, None, ALU.add, ALU.bypass, pb, pb)
        tt(t1, nr, A, ALU.mult, pb, tb_)
        tt(t2, s, I, ALU.mult, pb, tb_)
        tt(t1, t1, t2, ALU.add, tb_, tb_)
        tt(cr, t1, rden, ALU.mult, tb_ + pb, pb)
        tt(t1, s, A, ALU.mult, pb, tb_)
        tt(t2, nr, I, ALU.mult, pb, tb_)
        tt(t1, t1, t2, ALU.subtract, tb_, tb_)
        tt(ci, t1, rden, ALU.mult, tb_ + pb, pb)
        P.dma(lambda e: e.dma_start(out=Br, in_=s5F_d[i][:, :, 2, :]), "s5ld", reads=tb_, writes=pb)
        P.dma(lambda e: e.dma_start(out=Bi, in_=s5F_d[i][:, :, 3, :]), "s5ld", reads=tb_, writes=pb)
        cmul(Wr, Wi, cr, ci, Br, Bi, t1, t2, pb, pb)
        for k in range(4):
            V(lambda e, k=k, Wr=Wr: e.tensor_copy(out=WB[:, :, k, 0, :], in_=Wr), pb, [bf("WB")])
            V(lambda e, k=k, Wi=Wi: e.tensor_copy(out=WB[:, :, k, 1, :], in_=Wi), pb, [bf("WB")])
            if k < 3:
                cmul(Wr2, Wi2, c, s, Wr, Wi, t1, t2, pb, pb)
                Wr, Wr2 = Wr2, Wr
                Wi, Wi2 = Wi2, Wi
        V(lambda e: e.memset(small[:, 12:13], 0.0), pb + tb_, [bf("CA"), bf("Rtab"), bf("wglu"), bf("small12")])
        inM = TMP[:, 1400:1472].rearrange("p (a n) -> p a n", a=3)
        P.dma(lambda e: e.dma_start(out=inM, in_=s5M_d[i]), "s5ld", writes=pb)
        Am, Im, dtm = inM[:, 0, :], inM[:, 1, :], inM[:, 2, :]
        mt = [TMP[:, 1480 + 24 * q:1504 + 24 * q] for q in range(16)]
        xm, cm, sm, m1, m2, magm = mt[0:6]
        pr = [None, mt[6], mt[8], mt[10], mt[12]]
        pi_ = [None, mt[7], mt[9], mt[11], mt[13]]
        act(dtm, dtm, AF.Exp, pb, pb)
        tt(xm, Am, dtm, ALU.mult, pb, pb)
        act(rho, xm, AF.Exp, pb, pb, scale=4.0)
        act(magm, xm, AF.Exp, pb, pb)
        tt(xm, Im, dtm, ALU.mult, pb, pb)
        act(sm, xm, AF.Sin, pb, pb, scale=1.0 / 64)
        act(cm, xm, AF.Sin, pb, pb, scale=1.0 / 64, bias=epsT[:, 1:2])
        double_angle(cm, sm, m1, m2, 6)
        tt(pr[1], magm, cm, ALU.mult, pb, pb)
        tt(pi_[1], magm, sm, ALU.mult, pb, pb)
        for e_ in range(2, 5):
            cmul(pr[e_], pi_[e_], pr[e_ - 1], pi_[e_ - 1], pr[1], pi_[1], m1, m2, pb, pb)
        double_angle(cm, sm, m1, m2, 2)
        V(lambda e: e.tensor_copy(out=cs4, in_=cm), pb, pb)
        V(lambda e: e.tensor_copy(out=sn4, in_=sm), pb, pb)
        Cin = SPF[:, 0:1536].rearrange("p (r a n) -> p r a n", r=2, a=24)
        P.dma(lambda e: e.dma_start(out=Cin, in_=s5C_d[i]), "s5ld", writes=pb + tb_ + [bf("memT"), bf("mem_n")])
        Cr, Ci = Cin[:, 0], Cin[:, 1]
        c1 = SPF[:, 1536:2304].rearrange("p (a n) -> p a n", a=24)
        c2 = SPF[:, 2304:3072].rearrange("p (a n) -> p a n", a=24)
        V(lambda e: e.tensor_copy(out=CA[:, :, 0, 0, :], in_=Cr), pb, [bf("CA")])
        ts(CA[:, :, 0, 1, :], Ci, -1.0, None, ALU.mult, ALU.bypass, pb, [bf("CA")])
        for e_ in range(1, 5):
            prb = pr[e_].unsqueeze(2).to_broadcast([128, 24, 32])
            pib = pi_[e_].unsqueeze(2).to_broadcast([128, 24, 32])
            tt(c1, Cr, prb, ALU.mult, pb, tb_)
            tt(c2, Ci, pib, ALU.mult, pb, tb_)
            tt(CA[:, :, e_, 0, :], c1, c2, ALU.subtract, tb_, [bf("CA")])
            tt(c1, Cr, pib, ALU.mult, pb, tb_)
            tt(c2, Ci, prb, ALU.mult, pb, tb_)
            tt(c1, c1, c2, ALU.add, tb_, tb_)
            ts(CA[:, :, e_, 1, :], c1, -1.0, None, ALU.mult, ALU.bypass, tb_, [bf("CA")])
        Rr, Ri = Rtab[:, 0], Rtab[:, 1]
        V(lambda e: e.memset(Rr[:, :, 0:1], 1.0), writes=[bf("Rtab")])
        V(lambda e: e.memset(Ri[:, :, 0:1], 0.0), writes=[bf("Rtab")])
        qr, qi = cm, sm
        n = 1
        while n < 64:
            qrb = qr.unsqueeze(2).to_broadcast([128, 24, n])
            qib = qi.unsqueeze(2).to_broadcast([128, 24, n])
            a1 = c1[:, :, 0:n]
            a2 = c2[:, :, 0:n]
            tt(a1, Rr[:, :, 0:n], qrb, ALU.mult, pb + [bf("Rtab")], tb_)
            tt(a2, Ri[:, :, 0:n], qib, ALU.mult, pb + [bf("Rtab")], tb_)
            tt(Rr[:, :, n:2 * n], a1, a2, ALU.subtract, tb_, [bf("Rtab")])
            tt(a1, Rr[:, :, 0:n], qib, ALU.mult, pb + [bf("Rtab")], tb_)
            tt(a2, Ri[:, :, 0:n], qrb, ALU.mult, pb + [bf("Rtab")], tb_)
            tt(Ri[:, :, n:2 * n], a1, a2, ALU.add, tb_, [bf("Rtab")])
            double_angle(qr, qi, m1, m2, 1)
            n *= 2
        V(lambda e: e.memset(small[:, 16:64], 0.0), writes=[bf("Sc")])
        V(lambda e: e.memset(TMP[:, 1992:2312], 0.0), writes=[bf("Z3")])

        gsrc = wglu_d[i].rearrange("(k p) m -> p k m", p=128)
        for m in range(6):
            wload(gsrc[:, :, m * 128:(m + 1) * 128], wglu_bf[:, :, m * 128:(m + 1) * 128], bf("wglu"))
        P.op("gpsimd", lambda e: e.memset(sgt[0][:], 0.0), writes=[bf("sgt0")])
        P.op("gpsimd", lambda e: e.memset(sgt[1][:], 0.0), writes=[bf("sgt1")])
        P.op("gpsimd", lambda e: e.memset(stg[0][:], 0.0), writes=[bf("stg0")])
        P.op("gpsimd", lambda e: e.memset(wgb[1][:], 0.0), writes=[bf("wg1")])

    def s5_block(l, blk):
        i = l
        L2 = [SPF[:, 1536 + 256 * q:1792 + 256 * q].rearrange("p (a c) -> p a c", a=4) for q in range(6)]
        t1, t2, Xr, Xi, Vr, Vi = L2
        Sf = TMP[:, 0:520].rearrange("p (r a c) -> p r a c", r=2, a=4)
        Spb = TMP[:, 520:776].bitcast(BF16).rearrange("p (a r c) -> p a r c", a=4, r=2)
        yv = TMP[:, 776:1032]
        sgS = TMP[:, 1032:1160].bitcast(BF16)
        ini = TMP[:, 1160:1176].rearrange("p (a q) -> p a q", a=4)
        hS = h_blk
        stg0b = stg[0][:].bitcast(BF16)
        wflat = [w_[:].rearrange("p k n -> p (k n)") for w_ in wgb]
        um_set = [[sgt[0][:, 0:256], sgt[0][:, 256:512], sgt[1][:, 0:256], sgt[1][:, 256:512]],
                  [stg0b[:, q * 256:(q + 1) * 256] for q in range(4)]]
        um_buf = [[bf("sgt0"), bf("sgt0"), bf("sgt1"), bf("sgt1")], [bf("stg0")] * 4]
        Dsb_set = [SPF[:, 1024:1536], wflat[0].bitcast(F32)]
        Dsb_buf = [bf("Dsb"), bf("wg0")]
        sh_set = [[SPARE[:, q * 256:(q + 1) * 256] for q in range(8)],
                  [wflat[2][:, q * 256:(q + 1) * 256] for q in range(4)] + [wflat[3][:, q * 256:(q + 1) * 256] for q in range(4)]]
        sh_buf = [[bf(f"sh{q}") for q in range(8)], [bf("wu0")] * 4 + [bf("wu1")] * 4]
        Z3_set = [TMP[:, 1992:2312].bitcast(BF16).rearrange("p (e r n) -> p e r n", e=5, r=2),
                  wflat[1][:, 0:640].rearrange("p (e r n) -> p e r n", e=5, r=2)]
        Z3_buf = [bf("Z3"), bf("wg1")]
        bDs = {}

        def stageA(tau):
            pb_ = tau % 2
            um, ub = um_set[pb_], um_buf[pb_]
            um4 = [u_.rearrange("p (c j) -> p c j", j=4) for u_ in um]
            zb = bf(f"z{tau}")
            for q in range(3):
                act(um[q][q * 32:(q + 1) * 32, :], z_blk[q * 32:(q + 1) * 32, tau, :], AF.Copy, [zb], [ub[q]])
            act(um[3][64:128, :], z_blk[64:128, tau, :], AF.Copy, [zb], [ub[3]])
            P.op("gpsimd", lambda e: e.memset(um[3][64:96, :], 0.0), writes=[ub[3]])
            P.op("gpsimd", lambda e: e.tensor_copy(out=Z3_set[pb_][:, :, :, 32:64], in_=CA[:, 4 * tau + 3, :, :, :]), reads=[bf("CA")], writes=[Z3_buf[pb_]])
            bD = psb(2, 4)
            for q in range(4):
                for ri in range(2):
                    slot = q * 2 + ri
                    for t in range(4):
                        mm(PS[bD][:, slot * 64:(slot + 1) * 64], WB[:, tau, 3 - t, ri, :],
                           um4[q][:, :, t], t == 0, t == 3, [bf("WB"), ub[q]], [bf(f"ps{bD}")])
            act(Dsb_set[pb_], PS[bD][:, 0:512], AF.Copy, [bf(f"ps{bD}")], [Dsb_buf[pb_]])
            for q in range(4):
                for ri in range(2):
                    slot = q * 2 + ri
                    b0 = slot % 2
                    half = (slot // 2) % 2
                    pv = PS[b0][:, half * 256:(half + 1) * 256]
                    pv4 = pv.rearrange("p (c j) -> p c j", j=4)
                    pbuf = bf(f"ps{b0}")
                    for k in range(4):
                        mm(pv4[:, :, k:4], WB[:, tau, k, ri, :], um4[q][:, :, 0:4 - k],
                           k == 0, k == 3, [bf("WB"), ub[q]], [pbuf])
                    act(sh_set[pb_][slot], pv, AF.Copy, [pbuf], [sh_buf[pb_][slot]])

        def stageB(tau):
            pb_ = tau % 2
            Dsb = Dsb_set[pb_].rearrange("p (q r c) -> p q r c", q=4, r=2)
            Dr, Di = Dsb[:, :, 0, :], Dsb[:, :, 1, :]
            Rr, Ri = Rtab[:, 0, 4 * tau:4 * tau + 4, :], Rtab[:, 1, 4 * tau:4 * tau + 4, :]
            rb_ = [bf("Rtab"), Dsb_buf[pb_]]
            tt(t1, Rr, Dr, ALU.mult, rb_, [bf("l2t")])
            tt(t2, Ri, Di, ALU.mult, rb_, [bf("l2t")])
            tt(Xr, t1, t2, ALU.add, [bf("l2t")], [bf("X")])
            tt(t1, Rr, Di, ALU.mult, rb_, [bf("l2t")])
            tt(t2, Ri, Dr, ALU.mult, rb_, [bf("l2t")])
            tt(Xi, t1, t2, ALU.subtract, [bf("l2t")], [bf("X")])
            scr, sci = Sc[:, 4 * tau:4 * tau + 4, 0], Sc[:, 4 * tau:4 * tau + 4, 1]
            c4, s4 = cs4[:, 4 * tau:4 * tau + 4], sn4[:, 4 * tau:4 * tau + 4]
            ir, ii, ta, tb2 = ini[:, 0, :], ini[:, 1, :], ini[:, 2, :], ini[:, 3, :]
            ib = [bf("ini")]
            tt(ta, c4, scr, ALU.mult, [bf("Sc"), bf("s5p")], ib)
            tt(tb2, s4, sci, ALU.mult, [bf("Sc"), bf("s5p")], ib)
            tt(ir, ta, tb2, ALU.subtract, ib, ib)
            tt(ta, s4, scr, ALU.mult, [bf("Sc"), bf("s5p")], ib)
            tt(tb2, c4, sci, ALU.mult, [bf("Sc"), bf("s5p")], ib)
            tt(ii, ta, tb2, ALU.add, ib, ib)
            for q in range(4):
                pr_ = 4 * tau + q
                rbc = rho[:, pr_:pr_ + 1].to_broadcast([128, 64])
                V(lambda e, q=q, rbc=rbc: e.tensor_tensor_scan(out=Vr[:, q, :], data0=rbc, data1=Xr[:, q, :], initial=ir[:, q:q + 1],
                                                                op0=ALU.mult, op1=ALU.add), [bf("X"), bf("ini"), bf("s5p")], [bf("Vs")])
                V(lambda e, q=q, rbc=rbc: e.tensor_tensor_scan(out=Vi[:, q, :], data0=rbc, data1=Xi[:, q, :], initial=ii[:, q:q + 1],
                                                                op0=ALU.mult, op1=ALU.add), [bf("X"), bf("ini"), bf("s5p")], [bf("Vs")])
            V(lambda e: e.tensor_copy(out=Sf[:, 0, :, 0], in_=scr), [bf("Sc")], [bf("Sf")])
            V(lambda e: e.tensor_copy(out=Sf[:, 1, :, 0], in_=sci), [bf("Sc")], [bf("Sf")])
            rv = [bf("Rtab"), bf("Vs")]
            tt(t1, Rr, Vr, ALU.mult, rv, [bf("l2t")])
            tt(t2, Ri, Vi, ALU.mult, rv, [bf("l2t")])
            tt(Sf[:, 0, :, 1:65], t1, t2, ALU.subtract, [bf("l2t")], [bf("Sf")])
            tt(t1, Rr, Vi, ALU.mult, rv, [bf("l2t")])
            tt(t2, Ri, Vr, ALU.mult, rv, [bf("l2t")])
            tt(Sf[:, 1, :, 1:65], t1, t2, ALU.add, [bf("l2t")], [bf("Sf")])
            act(Spb[:, :, 0, :], Sf[:, 0, :, 0:64], AF.Copy, [bf("Sf")], [bf("Spb")])
            act(Spb[:, :, 1, :], Sf[:, 1, :, 0:64], AF.Copy, [bf("Sf")], [bf("Spb")])
            V(lambda e: e.tensor_copy(out=scr, in_=Sf[:, 0, :, 64]), [bf("Sf")], [bf("Sc")])
            V(lambda e: e.tensor_copy(out=sci, in_=Sf[:, 1, :, 64]), [bf("Sf")], [bf("Sc")])

        def stageC(tau):
            pb_ = tau % 2
            sh, shb = sh_set[pb_], sh_buf[pb_]
            Z3v = Z3_set[pb_]
            by = 4 + (tau % 2)
            for q in (3, 2, 0, 1):
                pr_ = 4 * tau + q
                if q == 3:
                    out = PS[by][64:128, 0:256]
                    lw = lambda e_, r_: Z3v[:, e_, r_, :]
                    wb_ = [Z3_buf[pb_]]
                else:
                    out = PS[by][q * 32:(q + 1) * 32, 0:256]
                    lw = lambda e_, r_, pr_=pr_: CA[:, pr_, e_, r_, :]
                    wb_ = [bf("CA")]
                out4 = out.rearrange("p (c j) -> p c j", j=4)
                first = (q != 2)
                mm(out, lw(0, 0), sh[q * 2], first, False, wb_ + [shb[q * 2]], [bf(f"ps{by}")])
                mm(out, lw(0, 1), sh[q * 2 + 1], False, False, wb_ + [shb[q * 2 + 1]], [bf(f"ps{by}")])
                for j in range(4):
                    mm(out4[:, :, j], lw(j + 1, 0), Spb[:, q, 0, :], False, False, wb_ + [bf("Spb")], [bf(f"ps{by}")])
                    mm(out4[:, :, j], lw(j + 1, 1), Spb[:, q, 1, :], False, j == 3, wb_ + [bf("Spb")], [bf(f"ps{by}")])

        def stageC2(tau):
            by = 4 + (tau % 2)
            stt(yv, z_blk[:, tau, :], cs("dF", i * 6 + tau), PS[by][:, 0:256], ALU.mult, ALU.add, [bf(f"z{tau}"), bf(f"ps{by}"), bf("cst")], [bf("yv")])
            act(hS[:, tau, :], yv, AF.Gelu_apprx_tanh, [bf("yv")], [bf("h_blk")])

        stageA(0)
        stageA(1)
        stageB(0)
        stageC(0)
        for tau in range(1, 6):
            if tau < 5:
                stageA(tau + 1)
            stageB(tau)
            stageC2(tau - 1)
            stageC(tau)
        stageC2(5)
        for m in range(6):
            bg = 6 + (m % 2)
            for k in range(6):
                mm(PS[bg][:, 0:256], wglu_bf[:, k, m * 128:(m + 1) * 128], hS[:, k, :], k == 0, k == 5,
                   [bf("wglu"), bf("h_blk")], [bf(f"ps{bg}")])
            act(sgS, PS[bg][:, 0:256], AF.Sigmoid, [bf(f"ps{bg}")], [bf("sgS")])
            tt(cat_blk[:, m, :], hS[:, m, :], sgS, ALU.mult, [bf("h_blk"), bf("sgS")], [bf(f"cat{m}")])

    import os
    STG = int(os.environ.get("KSTAGE", "99"))
    for l in range(n_layers):
        if STG >= 1:
            layer_start(l)
        if l < 2 and STG >= 2:
            s5_prep(l)
        P.barrier()
        if l >= 2:
            for p_ in range(2):
                zs_ = z_sets[p_]
                P.op("gpsimd", lambda e, zs_=zs_: e.memset(zs_[64:128, 0:6, :], 0.0), writes=[bf(f"z{m_}" + ("" if p_ == 0 else "_1")) for m_ in range(6)])
            P.op("gpsimd", lambda e: e.memset(TMP[0:64, 1280:2048], 0.0), writes=[bf(f"z{m_}") for m_ in range(6)])
            P.op("gpsimd", lambda e: e.memset(SPARE[0:64, 4096:5632], 0.0), writes=[bf(f"z{m_}_1") for m_ in range(6)])
            PAR["p"] = 0
            PAR["mbank"] = None
            win_block(l, 0)
            memattn_block(l, 0)
            for blk in range(NBLK):
                def hook(blk=blk):
                    if blk + 1 < NBLK:
                        PAR["p"] = (blk + 1) % 2
                        PAR["mbank"] = (2, 3)
                        win_block(l, blk + 1)
                        memattn_block(l, blk + 1)
                        PAR["p"] = blk % 2
                PAR["p"] = blk % 2
                diff_block(l, blk, hook)
                wout_block(l, blk)
            PAR["p"] = 0
            PAR["mbank"] = None
        else:
            for blk in range(NBLK):
                win_block(l, blk)
                memattn_block(l, blk)
                s5_block(l, blk)
                wout_block(l, blk)
        if STG >= 9:
            ffn(l)
        if l == 1 and n_layers > 2:
            kv_shared()
    P.barrier()
    ysrc = yT_d.rearrange("(k p) n -> p k n", p=128)
    for k in range(8):
        P.dma(lambda e, k=k: e.dma_start(out=ysrc[:, k, :], in_=xT[:, k, :]), "yst", reads=[bf(f"x{k}_{b}") for b in range(8)])
    P.rec["sync"].append(([(P.sems["yst"], P.dma_cnt["yst"])], None, None, 0))
    return P.finish()


_NC_CACHE = {}


def _prep_inputs(inp, n_cores=8):
    f32 = np.float32
    g = lambda k: np.asarray(inp[k], dtype=f32)
    cst = np.zeros((128, NCST), f32)

    def put(name, arr):
        o, w = CST_OFF[name]
        assert arr.shape == (128, w), (name, arr.shape)
        cst[:, o:o + w] = arr

    fm = lambda a: a.reshape(a.shape[0], 8, 128).transpose(2, 0, 1).reshape(128, -1)
    put("gmix", fm(g("norm_mix")))
    put("gffn", fm(g("norm_ffn")))
    put("gmem", fm(g("norm_mem")))
    put("gkv", fm(g("kv_norm")[None]))
    p64 = np.arange(128) % 64
    put("mqn", g("mem_q_norm")[:, p64].T)
    put("mkn", g("mem_k_norm")[:, p64].T)
    put("dqn", g("diff_q_norm")[:, p64].T)
    put("dkn", g("diff_k_norm")[p64][:, None])
    put("dsn", g("diff_sub_norm").T)
    lam = np.stack([g("diff_lambda_q1"), g("diff_lambda_k1"), g("diff_lambda_q2"), g("diff_lambda_k2")])
    put("lam", np.broadcast_to(lam.reshape(1, 512), (128, 512)))
    put("dF", g("ssm_d").reshape(2, 6, 128).transpose(2, 0, 1).reshape(128, 12))
    G, N, Pp = 48, 64, 16
    a_re, a_im, ldt = g("ssm_a_re"), g("ssm_a_im"), g("ssm_log_dt")
    b_re, b_im, c_re, c_im = g("ssm_b_re"), g("ssm_b_im"), g("ssm_c_re"), g("ssm_c_im")
    s5F = np.zeros((2, 128, 6, 4, 128), f32)
    s5dtF = np.zeros((2, 128, 6), f32)
    s5M = np.zeros((2, 128, 3, 24), f32)
    s5C = np.zeros((2, 128, 2, 24, 32), f32)
    for i in range(2):
        for gg in range(G):
            tau, gl = gg // 8, gg % 8
            rows = slice(gl * 16, gl * 16 + 16)
            half = gg % 2
            cols = slice(half * 64, half * 64 + 64)
            for hh in range(2):
                s5F[i, rows, tau, 0, hh * 64:(hh + 1) * 64] = a_re[i, gg][None, :]
                s5F[i, rows, tau, 1, hh * 64:(hh + 1) * 64] = a_im[i, gg][None, :]
            s5F[i, rows, tau, 2, cols] = b_re[i, gg].T
            s5F[i, rows, tau, 3, cols] = b_im[i, gg].T
            s5dtF[i, rows, tau] = ldt[i, gg]
            pr = gg // 2
            mrows = slice(half * 64, half * 64 + 64)
            s5M[i, mrows, 0, pr] = a_re[i, gg]
            s5M[i, mrows, 1, pr] = a_im[i, gg]
            s5M[i, mrows, 2, pr] = ldt[i, gg]
            s5C[i, mrows, 0, pr, half * 16:(half + 1) * 16] = c_re[i, gg].T
            s5C[i, mrows, 1, pr, half * 16:(half + 1) * 16] = c_im[i, gg].T
    shared = {"cst": cst, "s5F": s5F, "s5dtF": s5dtF, "s5M": s5M, "s5C": s5C}
    for k in ["w_in", "w_out", "mem_wk", "mem_wv", "ffn_w_gate", "ffn_w_up", "ffn_w_down", "ssm_w_glu", "diff_wk", "diff_wv"]:
        shared[k] = np.ascontiguousarray(g(k))
    x, mem = g("x"), g("mem")
    maps = []
    for b in range(n_cores):
        d = dict(shared)
        d["xT"] = np.ascontiguousarray(x[b].T)
        d["memT"] = np.ascontiguousarray(mem[b].T)
        maps.append(d)
    return maps


def kernel(**inputs):
    n_layers = int(inputs.pop("_n_layers", NL))
    maps = _prep_inputs(inputs)
    nc = build(n_layers)
    res = run_bass_kernel_spmd(nc, maps, core_ids=list(range(8)))
    out = np.stack([np.ascontiguousarray(r["yT"].T) for r in res.results], axis=0)
    return out.astype(np.float32)
```

```python
import contextlib
import numpy as np
import concourse.bass as bass
import concourse.mybir as mybir
from concourse.bass_utils import run_bass_kernel_spmd

F32 = mybir.dt.float32
BF16 = mybir.dt.bfloat16
AF = mybir.ActivationFunctionType
ALU = mybir.AluOpType
AX = mybir.AxisListType

SEM_ROT = 20000


class Buf:
    __slots__ = ("name", "w", "r")

    def __init__(self, name=""):
        self.name = name
        self.w = None
        self.r = []


class Prog:
    ENGS = ("tensor", "vector", "scalar", "gpsimd", "sync")

    def __init__(self):
        self.nc = bass.Bass("TRN2", target_bir_lowering=False)
        self.stack = contextlib.ExitStack()
        self.rec = {e: [] for e in self.ENGS}
        self.sems = {}
        self.cur = {}
        self.epoch = {e: 0 for e in self.ENGS}
        self.known = {e: {} for e in self.ENGS}
        self.dma_cnt = {}
        for e in self.ENGS:
            self._new_epoch(e)

    def sem(self, key):
        if key not in self.sems:
            self.sems[key] = self.stack.enter_context(self.nc.semaphore(str(key)))
        return self.sems[key]

    def _new_epoch(self, e):
        self.epoch[e] += 1
        key = (e, self.epoch[e])
        self.sem(key)
        self.cur[e] = [key, 0]

    def sbuf(self, name, shape, dt):
        return self.stack.enter_context(self.nc.sbuf_tensor(name, list(shape), dt))

    def psum(self, name, shape, dt=F32):
        return self.stack.enter_context(self.nc.psum_tensor(name, list(shape), dt))

    def dram(self, name, shape, dt, kind):
        return self.nc.dram_tensor(name, list(shape), dt, kind=kind).ap()

    def _deps(self, eng, reads, writes):
        toks = []
        for b in reads:
            if b.w is not None:
                toks.append(b.w)
        for b in writes:
            if b.w is not None:
                toks.append(b.w)
            toks.extend(b.r)
        need = {}
        for (k, v) in toks:
            if eng == "tensor" and k[0] == "tensor":
                continue
            if self.known[eng].get(k, 0) >= v:
                continue
            if need.get(k, 0) < v:
                need[k] = v
        for k, v in need.items():
            self.known[eng][k] = v
        return [(self.sems[k], v) for k, v in need.items()]

    def op(self, eng, fn, reads=(), writes=()):
        waits = self._deps(eng, reads, writes)
        cur = self.cur[eng]
        if cur[1] >= SEM_ROT:
            self._new_epoch(eng)
            cur = self.cur[eng]
        cur[1] += 1
        tok = (cur[0], cur[1])
        self.rec[eng].append((waits, fn, self.sems[cur[0]], 1))
        for b in writes:
            b.w = tok
            b.r = []
        for b in reads:
            if b not in writes:
                b.r.append(tok)
        return tok

    def dma(self, fn, semkey, reads=(), writes=(), eng="sync"):
        waits = self._deps(eng, reads, writes)
        self.sem(semkey)
        self.dma_cnt[semkey] = self.dma_cnt.get(semkey, 0) + 16
        tok = (semkey, self.dma_cnt[semkey])
        self.rec[eng].append((waits, fn, self.sems[semkey], 16))
        for b in writes:
            b.w = tok
            b.r = []
        for b in reads:
            if b not in writes:
                b.r.append(tok)
        return tok

    def wait_all(self, eng, bufs):
        waits = self._deps(eng, bufs, ())
        self.rec[eng].append((waits, None, None, 0))


    def barrier(self):
        toks = []
        for e in self.ENGS:
            for ep in range(1, self.epoch[e] + 1):
                k = (e, ep)
                v = self.cur[e][1] if ep == self.epoch[e] else None
                if v is None:
                    continue
                if v > 0:
                    toks.append((k, v))
        for k, v in self.dma_cnt.items():
            toks.append((k, v))
        for e in self.ENGS:
            waits = []
            for (k, v) in toks:
                if k[0] == e and isinstance(k, tuple) and k[0] in self.ENGS and e != "sync":
                    pass
                if self.known[e].get(k, 0) >= v:
                    continue
                self.known[e][k] = v
                waits.append((self.sems[k], v))
            if waits:
                self.rec[e].append((waits, None, None, 0))

    def finish(self):
        nc = self.nc
        rec = self.rec

        def replay(e, name):
            for (waits, fn, sem, inc) in rec[name]:
                for (s, v) in waits:
                    e.wait_ge(s, v)
                if fn is not None:
                    ins = fn(e)
                    ins.then_inc(sem, inc)

        with nc.Block() as block:
            @block.tensor
            def _(e):
                replay(e, "tensor")

            @block.vector
            def _(e):
                replay(e, "vector")

            @block.scalar
            def _(e):
                replay(e, "scalar")

            @block.gpsimd
            def _(e):
                replay(e, "gpsimd")

            @block.sync
            def _(e):
                replay(e, "sync")
        self.stack.close()
        return nc

import math

EPS = 1e-6
NL = 4
BLK = 256
NBLK = 8
DFF = 2816
NF = 22
FC = 4


def _cst_layout():
    off = {}
    n = 0
    for name, w in [("gmix", 32), ("gffn", 32), ("gmem", 32), ("gkv", 8), ("mqn", 4), ("mkn", 4),
                    ("dqn", 2), ("dkn", 1), ("dsn", 2), ("lam", 512), ("dF", 12)]:
        off[name] = (n, w)
        n += w
    return off, n


CST_OFF, NCST = _cst_layout()


def build(n_layers=NL):
    P = Prog()
    nc = P.nc
    dI = lambda name, shape: P.dram(name, shape, F32, "ExternalInput")
    xT_d = dI("xT", [1024, 2048])
    memT_d = dI("memT", [1024, 256])
    cst_d = dI("cst", [128, NCST])
    w_in_d = dI("w_in", [4, 1024, 1024])
    w_out_d = dI("w_out", [4, 1024, 1024])
    wk_d = dI("mem_wk", [4, 1024, 256])
    wv_d = dI("mem_wv", [4, 1024, 256])
    wg_d = dI("ffn_w_gate", [4, 1024, DFF])
    wu_d = dI("ffn_w_up", [4, 1024, DFF])
    wd_d = dI("ffn_w_down", [4, DFF, 1024])
    wglu_d = dI("ssm_w_glu", [2, 768, 768])
    dwk_d = dI("diff_wk", [1024, 768])
    dwv_d = dI("diff_wv", [1024, 768])
    s5F_d = dI("s5F", [2, 128, 6, 4, 128])
    s5dtF_d = dI("s5dtF", [2, 128, 6])
    s5M_d = dI("s5M", [2, 128, 3, 24])
    s5C_d = dI("s5C", [2, 128, 2, 24, 32])
    yT_d = P.dram("yT", [1024, 2048], F32, "ExternalOutput")

    xT = P.sbuf("xT_sb", [128, 8, 2048], F32)
    MIX = P.sbuf("MIX", [128, 24576], BF16)
    ARENA = P.sbuf("ARENA", [128, 28672], BF16)
    stg = [P.sbuf(f"stg{i}", [128, 1024], F32) for i in range(2)]
    wgb = [P.sbuf(f"wgb{i}", [128, 8, 128], BF16) for i in range(4)]
    cst = P.sbuf("cst_sb", [128, NCST], F32)
    ones_bf = P.sbuf("ones_bf", [128, 128], BF16)
    blk64 = P.sbuf("blk64", [128, 128], BF16)
    epsT = P.sbuf("epsT", [128, 2], F32)
    KmT = P.sbuf("KmT", [128, 2, 256], BF16)
    Vm = P.sbuf("Vm", [128, 2, 256], BF16)
    sq = [P.sbuf(f"sq{i}", [128, 256], BF16) for i in range(2)]
    rstd = [P.sbuf(f"rstd{i}", [128, 256], F32) for i in range(2)]
    sgt = [P.sbuf(f"sgt{i}", [128, 512], BF16) for i in range(2)]
    eM = [P.sbuf(f"eM{i}", [128, 256], BF16) for i in range(2)]
    rrm = P.sbuf("rrm", [128, 256], F32)
    TMP = P.sbuf("TMP", [128, 2320], F32)
    small = P.sbuf("small", [128, 64], F32)
    PS = [P.psum(f"ps{i}", [128, 512]) for i in range(8)]

    B = {}

    def bf(name):
        if name not in B:
            B[name] = Buf(name)
        return B[name]

    def cs(name, i=0, n=1):
        o, w = CST_OFF[name]
        return cst[:, o + i:o + i + n]

    w_in_bf = ARENA[:, 0:8192].rearrange("p (k m) -> p k m", k=8)
    w_out_bf = ARENA[:, 8192:16384].rearrange("p (k m) -> p k m", k=8)
    h_blk = ARENA[:, 16384:18432].rearrange("p (k n) -> p k n", k=8)
    z_blk = ARENA[:, 18432:20480].rearrange("p (k n) -> p k n", k=8)
    cat_blk = ARENA[:, 20480:22528].rearrange("p (k n) -> p k n", k=8)
    SPARE = ARENA[:, 22528:28672]
    hT = ARENA[:, 0:16384].rearrange("p (k n) -> p k n", k=8)
    aT = ARENA[:, 16384:24576].rearrange("p (f n) -> p f n", f=FC)
    wdb = ARENA[:, 24576:28672].rearrange("p (f n) -> p f n", f=FC)
    KdT = MIX[:, 0:12288].rearrange("p (t n) -> p t n", t=6)
    Vd = MIX[:, 12288:24576].rearrange("p (t n) -> p t n", t=16)
    WB = MIX[:, 0:6144].rearrange("p (t k r n) -> p t k r n", t=6, k=4, r=2)
    CA = MIX[:, 6144:13824].rearrange("p (a e r n) -> p a e r n", a=24, e=5, r=2)
    Rtab = MIX[:, 13824:19968].bitcast(F32).rearrange("p (r a c) -> p r a c", r=2, a=24)
    wglu_bf = MIX[:, 19968:24576].rearrange("p (k m) -> p k m", k=6)
    memT_sb = SPARE[:, 0:4096].bitcast(F32).rearrange("p (k n) -> p k n", k=8)
    mem_n = SPARE[:, 4096:6144].rearrange("p (k n) -> p k n", k=8)
    wk_bf = ARENA[:, 16384:18432].rearrange("p (k n) -> p k n", k=8)
    wv_bf = ARENA[:, 18432:20480].rearrange("p (k n) -> p k n", k=8)

    PAR = {"p": 0, "mbank": None}
    h_sets = [h_blk, SPARE[:, 0:2048].rearrange("p (k n) -> p k n", k=8)]
    z_sets = [z_blk, SPARE[:, 2048:4096].rearrange("p (k n) -> p k n", k=8)]
    catm_sets = [cat_blk[:, 6:8, :], SPARE[:, 5632:6144].rearrange("p (k n) -> p k n", k=2)]

    def HB():
        return h_sets[PAR["p"]]

    def ZB():
        return z_sets[PAR["p"]]

    def CATM():
        return catm_sets[PAR["p"]]

    def sfx():
        return "" if PAR["p"] == 0 else "_1"

    def V(fn, reads=(), writes=(), eng="vector"):
        return P.op(eng, fn, reads=reads, writes=writes)

    def tt(out, a, b, op, reads, writes, eng="vector"):
        return P.op(eng, lambda e: e.tensor_tensor(out=out, in0=a, in1=b, op=op), reads=reads, writes=writes)

    def stt(out, a, s, b, op0, op1, reads, writes):
        return P.op("vector", lambda e: e.scalar_tensor_tensor(out=out, in0=a, scalar=s, in1=b, op0=op0, op1=op1),
                    reads=reads, writes=writes)

    def ts(out, a, s1, s2, op0, op1, reads, writes, eng="vector"):
        return P.op(eng, lambda e: e.tensor_scalar(out=out, in0=a, scalar1=s1, scalar2=s2, op0=op0, op1=op1),
                    reads=reads, writes=writes)

    def act(out, a, func, reads, writes, scale=1.0, bias=None):
        if bias is None:
            return P.op("scalar", lambda e: e.activation(out=out, in_=a, func=func, scale=scale), reads=reads, writes=writes)
        return P.op("scalar", lambda e: e.activation(out=out, in_=a, func=func, scale=scale, bias=bias), reads=reads, writes=writes)

    def mm(out, lhsT, rhs, start, stop, reads, writes):
        return P.op("tensor", lambda e: e.matmul(out, lhsT=lhsT, rhs=rhs, start=start, stop=stop), reads=reads, writes=writes)

    stg_i = [0]

    cast_rot = [0]

    def wload(src, dst, dstbuf, shape3=None, engs=("gpsimd",)):
        i = stg_i[0] % 2
        stg_i[0] += 1
        n = 1
        for d in src.shape[1:]:
            n *= d
        sv = stg[i][:, 0:n]
        if len(src.shape) == 3:
            sv = sv.rearrange("p (a b) -> p a b", a=src.shape[1])
        P.dma(lambda e: e.dma_start(out=sv, in_=src), ("stg", i), writes=[bf(f"stg{i}")])
        ce = engs[cast_rot[0] % len(engs)]
        cast_rot[0] += 1
        if ce == "scalar":
            act(dst, sv, AF.Copy, [bf(f"stg{i}")], [dstbuf])
        else:
            P.op(ce, lambda e: e.tensor_copy(out=dst, in_=sv), reads=[bf(f"stg{i}")], writes=[dstbuf])

    psn = {}

    def psb(lo, hi):
        k = (lo, hi)
        psn[k] = psn.get(k, -1) + 1
        return lo + psn[k] % (hi - lo)

    rot = {}

    def rr(name, n):
        rot[name] = rot.get(name, -1) + 1
        return rot[name] % n

    def rstd_from(ps_ap, ncols, scale, reads):
        i = rr("rstd", 2)
        r = rstd[i][:, 0:ncols]
        act(r, ps_ap, AF.Ln, reads, [bf(f"rstd{i}")], scale=scale, bias=epsT[:, 0:1])
        act(r, r, AF.Exp, [bf(f"rstd{i}")], [bf(f"rstd{i}")], scale=-0.5)
        return r, bf(f"rstd{i}")

    def group_norm_evac(ps_ap, psbuf, ncols, gain, dst, dstbufs, full, split=None, defer=False):
        i = rr("sq", 2)
        s = sq[i][:, 0:ncols]
        act(s, ps_ap, AF.Square, [psbuf], [bf(f"sq{i}")])
        if defer:
            return lambda: _gne2(i, s, ps_ap, psbuf, ncols, gain, dst, dstbufs, full, split)
        _gne2(i, s, ps_ap, psbuf, ncols, gain, dst, dstbufs, full, split)

    def _gne2(i, s, ps_ap, psbuf, ncols, gain, dst, dstbufs, full, split):
        b2 = psb(2, 4)
        mm(PS[b2][:, 0:ncols], ones_bf[:] if full else blk64[:], s, True, True, [bf(f"sq{i}"), bf("consts")], [bf(f"ps{b2}")])
        r, rb = rstd_from(PS[b2][:, 0:ncols], ncols, 1.0 / (128 if full else 64), [bf(f"ps{b2}")])
        if split is None:
            stt(dst, ps_ap, gain, r, ALU.mult, ALU.mult, [psbuf, rb, bf("cst")], dstbufs)
        else:
            lo_, hi_ = split
            stt(lo_[0:64, :], ps_ap[0:64, :], gain[0:64, :], r[0:64, :], ALU.mult, ALU.mult, [psbuf, rb, bf("cst")], dstbufs)
            stt(hi_[64:128, :], ps_ap[64:128, :], gain[64:128, :], r[64:128, :], ALU.mult, ALU.mult, [psbuf, rb, bf("cst")], dstbufs)

    zhi_sets = [[TMP[:, 1280 + 128 * m_:1408 + 128 * m_].bitcast(BF16) for m_ in range(6)],
                [SPARE[:, 4096 + 256 * m_:4352 + 256 * m_] for m_ in range(6)]]

    def ZHI(m_):
        return zhi_sets[PAR["p"]][m_]

    def xnorm(blk, gname, gl, dst, dstbuf):
        c0 = blk * BLK
        b2 = psb(2, 4)
        for k in range(8):
            i = rr("sq", 2)
            act(sq[i][:], xT[:, k, c0:c0 + BLK], AF.Square, [bf(f"x{k}_{blk}")], [bf(f"sq{i}")])
            mm(PS[b2][:, 0:BLK], ones_bf[:], sq[i][:], k == 0, k == 7, [bf(f"sq{i}"), bf("consts")], [bf(f"ps{b2}")])
        r, rb = rstd_from(PS[b2][:, 0:BLK], BLK, 1.0 / 1024, [bf(f"ps{b2}")])
        for k in range(8):
            stt(dst[:, k, :], xT[:, k, c0:c0 + BLK], cs(gname, gl * 8 + k), r, ALU.mult, ALU.mult,
                [bf(f"x{k}_{blk}"), rb, bf("cst")], [dstbuf])

    P.dma(lambda e: e.dma_start(out=cst[:], in_=cst_d[:, :]), "cstld", writes=[bf("cst")])
    V(lambda e: e.memset(ones_bf[:], 1.0), writes=[bf("consts")])
    V(lambda e: e.memset(blk64[:], 0.0), writes=[bf("consts")])
    V(lambda e: e.memset(blk64[0:64, 0:64], 1.0), writes=[bf("consts")])
    V(lambda e: e.memset(blk64[64:128, 64:128], 1.0), writes=[bf("consts")])
    V(lambda e: e.memset(epsT[:, 0:1], EPS), writes=[bf("consts")])
    V(lambda e: e.memset(epsT[:, 1:2], math.pi / 2), writes=[bf("consts")])
    xsrc = xT_d.rearrange("(k p) n -> p k n", p=128)
    for k in range(8):
        for hh in range(2):
            P.dma(lambda e, k=k, hh=hh: e.dma_start(out=xT[:, k, hh * 1024:(hh + 1) * 1024], in_=xsrc[:, k, hh * 1024:(hh + 1) * 1024]),
                  "xld", writes=[bf(f"x{k}_{b}") for b in range(hh * 4, hh * 4 + 4)])

    def lam_prep():
        lamv = cs("lam", 0, 512).rearrange("p (a j d) -> p a j d", a=4, j=2)
        tmp = TMP[:, 0:64]
        for j in range(2):
            layer = 2 + j
            linit = 0.8 - 0.6 * math.exp(-0.3 * layer)
            for a in range(2):
                tt(tmp, lamv[:, 2 * a, j, :], lamv[:, 2 * a + 1, j, :], ALU.mult, [bf("cst")], [bf("lamtmp")])
                V(lambda e, a=a, j=j: e.reduce_sum(out=small[:, 8 + a:9 + a], in_=tmp, axis=AX.X), [bf("lamtmp")], [bf("small")])
                act(small[:, 8 + a:9 + a], small[:, 8 + a:9 + a], AF.Exp, [bf("small")], [bf("small")])
            tt(small[:, 10:11], small[:, 9:10], small[:, 8:9], ALU.subtract, [bf("small")], [bf("small")])
            ts(small[:, j:j + 1], small[:, 10:11], -linit, None, ALU.add, ALU.bypass, [bf("small")], [bf("small")])
            ts(small[:, 2 + j:3 + j], cs("dsn", j), 1.0 - linit, None, ALU.mult, ALU.bypass, [bf("cst")], [bf("small")])

    lam_prep()

    def layer_start(l):
        wi = w_in_d[l].rearrange("(k p) m -> p k m", p=128)
        wo = w_out_d[l].rearrange("(k p) m -> p k m", p=128)
        for m in range(8):
            wload(wi[:, :, m * 128:(m + 1) * 128], w_in_bf[:, :, m * 128:(m + 1) * 128], bf("w_in"), engs=("scalar",))
        for m in range(8):
            wload(wo[:, :, m * 128:(m + 1) * 128], w_out_bf[:, :, m * 128:(m + 1) * 128], bf("w_out"), engs=("scalar",))
        msrc = memT_d.rearrange("(k p) n -> p k n", p=128)
        P.dma(lambda e: e.dma_start(out=memT_sb, in_=msrc), "memld", writes=[bf("memT")])
        wks = wk_d[l].rearrange("(k p) m -> p k m", p=128)
        wvs = wv_d[l].rearrange("(k p) m -> p k m", p=128)
        for j in range(2):
            wload(wks[:, :, j * 128:(j + 1) * 128], wk_bf[:, :, j * 128:(j + 1) * 128], bf("wk"), engs=("scalar",))
            wload(wvs[:, :, j * 128:(j + 1) * 128], wv_bf[:, :, j * 128:(j + 1) * 128], bf("wv"), engs=("scalar",))
        b2 = psb(2, 4)
        for k in range(8):
            i = rr("sq", 2)
            act(sq[i][:], memT_sb[:, k, :], AF.Square, [bf("memT")], [bf(f"sq{i}")])
            mm(PS[b2][:, 0:256], ones_bf[:], sq[i][:], k == 0, k == 7, [bf(f"sq{i}"), bf("consts")], [bf(f"ps{b2}")])
        r, rb = rstd_from(PS[b2][:, 0:256], 256, 1.0 / 1024, [bf(f"ps{b2}")])
        for k in range(8):
            stt(mem_n[:, k, :], memT_sb[:, k, :], cs("gmem", l * 8 + k), r, ALU.mult, ALU.mult,
                [bf("memT"), rb, bf("cst")], [bf("mem_n")])
        for j in range(2):
            b0 = psb(0, 2)
            for k in range(8):
                mm(PS[b0][:, 0:256], wk_bf[:, k, j * 128:(j + 1) * 128], mem_n[:, k, :], k == 0, k == 7,
                   [bf("wk"), bf("mem_n")], [bf(f"ps{b0}")])
            group_norm_evac(PS[b0][:, 0:256], bf(f"ps{b0}"), 256, cs("mkn", l), KmT[:, j, :], [bf("KmT")], False)
        for i2 in range(2):
            b0 = psb(0, 2)
            for k in range(8):
                mm(PS[b0][:, 0:256], mem_n[:, k, i2 * 128:(i2 + 1) * 128], wv_bf[:, k, :], k == 0, k == 7,
                   [bf("wv"), bf("mem_n")], [bf(f"ps{b0}")])
            act(Vm[:, i2, :], PS[b0][:, 0:256], AF.Copy, [bf(f"ps{b0}")], [bf("Vm")])

    def win_block(l, blk):
        hb_, zb_ = HB(), ZB()
        hbuf = bf("h_blk" + sfx())
        xnorm(blk, "gmix", l, hb_, hbuf)
        for m in range(8):
            b0 = psb(0, 2)
            for k in range(8):
                mm(PS[b0][:, 0:BLK], w_in_bf[:, k, m * 128:(m + 1) * 128], hb_[:, k, :], k == 0, k == 7,
                   [bf("w_in"), hbuf], [bf(f"ps{b0}")])
            zbuf = bf(f"z{m}" + sfx())
            if m >= 6:
                group_norm_evac(PS[b0][:, 0:BLK], bf(f"ps{b0}"), BLK, cs("mqn", l), zb_[:, m, :], [zbuf], False)
            elif l >= 2:
                group_norm_evac(PS[b0][:, 0:BLK], bf(f"ps{b0}"), BLK, cs("dqn", l - 2), None, [zbuf], False,
                                split=(zb_[:, m, :], ZHI(m)))
            else:
                act(zb_[:, m, :], PS[b0][:, 0:BLK], AF.Copy, [bf(f"ps{b0}")], [zbuf])

    def memattn_block(l, blk):
        zb_, cm_ = ZB(), CATM()
        for h in range(4):
            j = h // 2
            po = (h % 2) * 64
            if PAR["mbank"] is None:
                nb, dbk = 4 + (h % 2), 6 + (h % 2)
            else:
                nb, dbk = PAR["mbank"]
            zbuf = bf(f"z{6 + j}" + sfx())
            eis = []
            for i2 in range(2):
                b0 = psb(0, 2)
                mm(PS[b0][:, 0:BLK], KmT[po:po + 64, j, i2 * 128:(i2 + 1) * 128], zb_[po:po + 64, 6 + j, :], True, True,
                   [bf("KmT"), zbuf], [bf(f"ps{b0}")])
                ei = rr("eM", 2)
                act(eM[ei][:], PS[b0][:, 0:BLK], AF.Exp, [bf(f"ps{b0}")], [bf(f"eM{ei}")], scale=0.125)
                eis.append(ei)
            for i2 in range(2):
                ei = eis[i2]
                mm(PS[nb][po:po + 64, 0:BLK], Vm[:, i2, h * 64:(h + 1) * 64], eM[ei][:], i2 == 0, i2 == 1,
                   [bf("Vm"), bf(f"eM{ei}")], [bf(f"ps{nb}")])
                mm(PS[dbk][po:po + 64, 0:BLK], ones_bf[:, 0:64], eM[ei][:], i2 == 0, i2 == 1,
                   [bf("consts"), bf(f"eM{ei}")], [bf(f"ps{dbk}")])
            V(lambda e, po=po, dbk=dbk: e.reciprocal(out=rrm[po:po + 64, :], in_=PS[dbk][po:po + 64, 0:BLK]), [bf(f"ps{dbk}")], [bf(f"rrm{po}")])
            tt(cm_[po:po + 64, j, :], PS[nb][po:po + 64, 0:BLK], rrm[po:po + 64, :], ALU.mult,
               [bf(f"ps{nb}"), bf(f"rrm{po}")], [bf(f"cat{6 + j}" + sfx())])

    def wout_block(l, blk):
        c0 = blk * BLK
        cm_ = CATM()
        for m in range(8):
            b0 = psb(0, 2)
            for k in range(8):
                rhs = cat_blk[:, k, :] if k < 6 else cm_[:, k - 6, :]
                cbuf = bf(f"cat{k}") if k < 6 else bf(f"cat{k}" + sfx())
                mm(PS[b0][:, 0:BLK], w_out_bf[:, k, m * 128:(m + 1) * 128], rhs, k == 0, k == 7,
                   [bf("w_out"), cbuf], [bf(f"ps{b0}")])
            tt(xT[:, m, c0:c0 + BLK], xT[:, m, c0:c0 + BLK], PS[b0][:, 0:BLK], ALU.add,
               [bf(f"ps{b0}"), bf(f"x{m}_{blk}")], [bf(f"x{m}_{blk}")])

    def ffn(l):
        P.barrier()
        gsrc = wg_d[l].rearrange("(k p) f -> p k f", p=128)
        usrc = wu_d[l].rearrange("(k p) f -> p k f", p=128)
        dsrc = wd_d[l].rearrange("(f p) m -> p f m", p=128)
        f = 0
        while f < NF:
            nfc = min(FC, NF - f)
            for fi in range(nfc):
                ff = f + fi
                gi = rr("wgb", 2)
                wload(gsrc[:, :, ff * 128:(ff + 1) * 128], wgb[gi][:], bf(f"wg{gi}"))
                wload(usrc[:, :, ff * 128:(ff + 1) * 128], wgb[2 + gi][:], bf(f"wu{gi}"))
                wload(dsrc[:, ff, :], wdb[:, fi, :], bf(f"wd{fi}"))
                for tb in range(4):
                    if ff == 0:
                        for blk in (2 * tb, 2 * tb + 1):
                            xnorm(blk, "gffn", l, hT[:, :, blk * BLK:(blk + 1) * BLK], bf(f"hT{blk}"))
                    bg = psb(0, 2)
                    bu = psb(2, 4)
                    for k in range(8):
                        mm(PS[bg][:], wgb[gi][:, k, :], hT[:, k, tb * 512:(tb + 1) * 512], k == 0, k == 7,
                           [bf(f"wg{gi}"), bf(f"hT{2 * tb}"), bf(f"hT{2 * tb + 1}")], [bf(f"ps{bg}")])
                    for k in range(8):
                        mm(PS[bu][:], wgb[2 + gi][:, k, :], hT[:, k, tb * 512:(tb + 1) * 512], k == 0, k == 7,
                           [bf(f"wu{gi}"), bf(f"hT{2 * tb}"), bf(f"hT{2 * tb + 1}")], [bf(f"ps{bu}")])
                    si = rr("sgt", 2)
                    act(sgt[si][:], PS[bg][:], AF.Silu, [bf(f"ps{bg}")], [bf(f"sgt{si}")])
                    tt(aT[:, fi, tb * 512:(tb + 1) * 512], sgt[si][:], PS[bu][:], ALU.mult,
                       [bf(f"sgt{si}"), bf(f"ps{bu}")], [bf(f"aT{fi}_{tb}")])
            for tb in range(4):
                for m in range(8):
                    bd = psb(4, 8)
                    for fi in range(nfc):
                        mm(PS[bd][:], wdb[:, fi, m * 128:(m + 1) * 128], aT[:, fi, tb * 512:(tb + 1) * 512], fi == 0, fi == nfc - 1,
                           [bf(f"wd{fi}"), bf(f"aT{fi}_{tb}")], [bf(f"ps{bd}")])
                    tt(xT[:, m, tb * 512:(tb + 1) * 512], xT[:, m, tb * 512:(tb + 1) * 512], PS[bd][:], ALU.add,
                       [bf(f"ps{bd}"), bf(f"x{m}_{2 * tb}"), bf(f"x{m}_{2 * tb + 1}")], [bf(f"x{m}_{2 * tb}"), bf(f"x{m}_{2 * tb + 1}")])
            f += nfc
        P.barrier()

    def kv_shared():
        for blk in range(NBLK):
            xnorm(blk, "gkv", 0, hT[:, :, blk * BLK:(blk + 1) * BLK], bf(f"hT{blk // 2}"))
        ksrc = dwk_d.rearrange("(k p) f -> p k f", p=128)
        vsrc = dwv_d.rearrange("(k p) f -> p k f", p=128)
        for t in range(6):
            gi = rr("wgb", 2)
            wload(ksrc[:, :, t * 128:(t + 1) * 128], wgb[gi][:], bf(f"wg{gi}"), engs=("scalar", "gpsimd"))
            wload(vsrc[:, :, t * 128:(t + 1) * 128], wgb[2 + gi][:], bf(f"wu{gi}"), engs=("scalar", "gpsimd"))
            for blk in range(NBLK):
                b0 = psb(0, 2)
                for k in range(8):
                    mm(PS[b0][:, 0:BLK], wgb[gi][:, k, :], hT[:, k, blk * BLK:(blk + 1) * BLK], k == 0, k == 7,
                       [bf(f"wg{gi}"), bf(f"hT{blk // 2}")], [bf(f"ps{b0}")])
                group_norm_evac(PS[b0][:, 0:BLK], bf(f"ps{b0}"), BLK, cs("dkn", 0), KdT[:, t, blk * BLK:(blk + 1) * BLK],
                                [bf(f"KdT{blk}")], False)
                for tt_ in (2 * blk, 2 * blk + 1):
                    b4 = psb(4, 8)
                    for k in range(8):
                        mm(PS[b4][:, 0:128], hT[:, k, tt_ * 128:(tt_ + 1) * 128], wgb[2 + gi][:, k, :], k == 0, k == 7,
                           [bf(f"wu{gi}"), bf(f"hT{tt_ // 4}")], [bf(f"ps{b4}")])
                    V(lambda e, tt_=tt_, t=t, b4=b4: e.tensor_copy(out=Vd[:, tt_, t * 128:(t + 1) * 128], in_=PS[b4][:, 0:128]),
                      [bf(f"ps{b4}")], [bf(f"Vd{tt_}")])
        P.barrier()

    def diff_block(l, blk, hook=None):
        j = l - 2
        zb_ = ZB()
        zhi_ = [ZHI(m_) for m_ in range(6)]
        sf_ = sfx()
        TF = TMP
        r0 = TF[:, 0:256]
        r1 = TF[:, 256:512]
        t0 = TF[:, 512:768]
        t1 = TF[:, 768:1024]
        eT = [TF[:, 1024:1152].bitcast(BF16), TF[:, 1152:1280].bitcast(BF16)]
        nkt = 2 * blk + 2

        def loops(h, c):
            hp = h % 2
            idx = c * 6 + h
            zt = idx // 2
            po = (idx % 2) * 64
            ob, db = 4 + hp, 6 + hp
            pendq = []
            Es = [sgt[0], sgt[1], TF[:, 1024:1280].bitcast(BF16)]
            Eb = [bf("sgt0"), bf("sgt1"), bf("e3")]

            def pv(unit, ei):
                for ik, kt in enumerate(unit):
                    n0 = max(0, kt - 2 * blk) * 128
                    N = BLK - n0
                    rhs = Es[ei][:, ik * 256:ik * 256 + N]
                    mm(PS[ob][:, c * 256 + n0:c * 256 + BLK], Vd[:, kt, h * 128:(h + 1) * 128], rhs, kt == 0, kt == nkt - 1,
                       [bf(f"Vd{kt}"), Eb[ei]], [bf(f"ps{ob}")])
                    mm(PS[db][:, c * 256 + n0:c * 256 + BLK], ones_bf[:], rhs, kt == 0, kt == nkt - 1,
                       [bf("consts"), Eb[ei]], [bf(f"ps{db}")])

            units = [(2 * p_, 2 * p_ + 1) for p_ in range(blk)] + [(2 * blk, 2 * blk + 1)]
            zq = zb_[:, zt, :] if po == 0 else zhi_[zt]
            for unit in units:
                diag = unit[0] >= 2 * blk
                b0 = psb(0, 2)
                for ik, kt in enumerate(unit):
                    n0 = max(0, kt - 2 * blk) * 128
                    N = BLK - n0
                    mm(PS[b0][:, ik * 256:ik * 256 + N], KdT[:, zt, kt * 128:(kt + 1) * 128], zq[:, n0:BLK], True, True,
                       [bf(f"KdT{kt // 2}"), bf(f"z{zt}" + sf_)], [bf(f"ps{b0}")])
                W = 384 if diag else 512
                ei = rr("eT3", 3)
                act(Es[ei][:, 0:W], PS[b0][:, 0:W], AF.Exp, [bf(f"ps{b0}")], [Eb[ei]], scale=0.125)
                if diag:
                    P.op("gpsimd", lambda e, ei=ei: e.memset(Es[ei][64:128, 0:64], 0.0), writes=[Eb[ei]])
                    P.op("gpsimd", lambda e, ei=ei: e.memset(Es[ei][64:128, 256:320], 0.0), writes=[Eb[ei]])
                pendq.append((unit, ei))
                if len(pendq) > 2:
                    pv(*pendq.pop(0))
            while pendq:
                pv(*pendq.pop(0))

        st = {}

        def postA(h):
            hp = h % 2
            ob, db = 4 + hp, 6 + hp
            V(lambda e: e.reciprocal(out=r0, in_=PS[db][:, 0:256]), [bf(f"ps{db}")], [bf("r0")])
            V(lambda e: e.reciprocal(out=r1, in_=PS[db][:, 256:512]), [bf(f"ps{db}")], [bf("r1")])
            tt(t0, PS[ob][:, 0:256], r0, ALU.mult, [bf(f"ps{ob}"), bf("r0")], [bf("t0")])
            tt(t1, PS[ob][:, 256:512], r1, ALU.mult, [bf(f"ps{ob}"), bf("r1")], [bf("t1")])
            stt(t0, t1, small[:, j:j + 1], t0, ALU.mult, ALU.add, [bf("t0"), bf("t1"), bf("small")], [bf("t0")])

        def postM(h):
            i = rr("sq", 2)
            act(sq[i][:], t0, AF.Square, [bf("t0")], [bf(f"sq{i}")])
            b2 = psb(2, 4)
            mm(PS[b2][:, 0:BLK], ones_bf[:], sq[i][:], True, True, [bf(f"sq{i}"), bf("consts")], [bf(f"ps{b2}")])
            st["b2"] = b2

        def postB(h):
            b2 = st["b2"]
            r, rb = rstd_from(PS[b2][:, 0:BLK], BLK, 1.0 / 128, [bf(f"ps{b2}")])
            stt(cat_blk[:, h, :], t0, small[:, 2 + j:3 + j], r, ALU.mult, ALU.mult, [bf("t0"), rb, bf("small")], [bf(f"cat{h}")])

        loops(0, 0)
        loops(0, 1)
        for h in range(6):
            postA(h)
            if h < 5:
                loops(h + 1, 0)
            postM(h)
            if h < 5:
                loops(h + 1, 1)
            postB(h)
            if h == 2 and hook is not None:
                hook()

    SPF = SPARE.bitcast(F32)
    rho = TMP[:, 1200:1224]
    cs4 = TMP[:, 1224:1248]
    sn4 = TMP[:, 1248:1272]
    Sc = small[:, 16:64].rearrange("p (a r) -> p a r", r=2)

    def cmul(dr, di, ar, ai, br, bi, t1, t2, R, W):
        tt(t1, ar, br, ALU.mult, R, [bf("s5t")])
        tt(t2, ai, bi, ALU.mult, R, [bf("s5t")])
        tt(dr, t1, t2, ALU.subtract, [bf("s5t")], W)
        tt(t1, ar, bi, ALU.mult, R, [bf("s5t")])
        tt(t2, ai, br, ALU.mult, R, [bf("s5t")])
        tt(di, t1, t2, ALU.add, [bf("s5t")], W)

    def double_angle(c, s, t1, t2, n):
        for _ in range(n):
            tt(t1, c, c, ALU.mult, [bf("s5p")], [bf("s5t")])
            tt(t2, s, s, ALU.mult, [bf("s5p")], [bf("s5t")])
            stt(s, c, 2.0, s, ALU.mult, ALU.mult, [bf("s5p")], [bf("s5p")])
            tt(c, t1, t2, ALU.subtract, [bf("s5t")], [bf("s5p")])

    def s5_prep_F(l):
        i = l
        pb = [bf("s5p")]
        tb_ = [bf("s5t")]
        dtF = TMP[:, 1300:1306]
        dt64 = TMP[:, 1306:1312]
        P.dma(lambda e: e.dma_start(out=dtF, in_=s5dtF_d[i]), "s5ld", writes=pb)
        act(dtF, dtF, AF.Exp, pb, pb)
        ts(dt64, dtF, 1.0 / 64, None, ALU.mult, ALU.bypass, pb, pb)
        MIXF = MIX[:, 6144:24576].bitcast(F32)
        sl = [MIXF[:, 768 * q:768 * (q + 1)].rearrange("p (t n) -> p t n", t=6) for q in range(12)]
        A, I, mag, c, s, t1, t2, nr, cr, ci, Wr, Wi = sl
        rden, Br, Bi, Wr2, Wi2 = mag, A, I, mag, nr
        dtb = dtF.unsqueeze(2).to_broadcast([128, 6, 128])
        dt64b = dt64.unsqueeze(2).to_broadcast([128, 6, 128])
        P.dma(lambda e: e.dma_start(out=A, in_=s5F_d[i][:, :, 0, :]), "s5ld", writes=pb)
        P.dma(lambda e: e.dma_start(out=I, in_=s5F_d[i][:, :, 1, :]), "s5ld", writes=pb)
        tt(t1, A, dtb, ALU.mult, pb, tb_)
        act(mag, t1, AF.Exp, tb_, pb)
        tt(t1, I, dt64b, ALU.mult, pb, tb_)
        act(s, t1, AF.Sin, tb_, pb)
        act(c, t1, AF.Sin, tb_, pb, bias=epsT[:, 1:2])
        double_angle(c, s, t1, t2, 6)
        tt(c, mag, c, ALU.mult, pb, pb)
        tt(s, mag, s, ALU.mult, pb, pb)
        tt(t1, A, A, ALU.mult, pb, tb_)
        tt(t2, I, I, ALU.mult, pb, tb_)
        tt(t1, t1, t2, ALU.add, tb_, tb_)
        V(lambda e: e.reciprocal(out=rden, in_=t1), tb_, pb)
        ts(nr, c, -1.0, None, ALU.add, ALU.bypass, pb, pb)
        tt(t1, nr, A, ALU.mult, pb, tb_)
        tt(t2, s, I, ALU.mult, pb, tb_)
        tt(t1, t1, t2, ALU.add, tb_, tb_)
        tt(cr, t1, rden, ALU.mult, tb_ + pb, pb)
        tt(t1, s, A, ALU.mult, pb, tb_)
        tt(t2, nr, I, ALU.mult, pb, tb_)
        tt(t1, t1, t2, ALU.subtract, tb_, tb_)
        tt(ci, t1, rden, ALU.mult, tb_ + pb, pb)
        P.dma(lambda e: e.dma_start(out=Br, in_=s5F_d[i][:, :, 2, :]), "s5ld", reads=tb_, writes=pb)
        P.dma(lambda e: e.dma_start(out=Bi, in_=s5F_d[i][:, :, 3, :]), "s5ld", reads=tb_, writes=pb)
        cmul(Wr, Wi, cr, ci, Br, Bi, t1, t2, pb, pb)
        for k in range(4):
            V(lambda e, k=k, Wr=Wr: e.tensor_copy(out=WB[:, :, k, 0, :], in_=Wr), pb, [bf("WB")])
            V(lambda e, k=k, Wi=Wi: e.tensor_copy(out=WB[:, :, k, 1, :], in_=Wi), pb, [bf("WB")])
            if k < 3:
                cmul(Wr2, Wi2, c, s, Wr, Wi, t1, t2, pb, pb)
                Wr, Wr2 = Wr2, Wr
                Wi, Wi2 = Wi2, Wi
        V(lambda e: e.memset(small[:, 12:13], 0.0), pb + tb_, [bf("CA"), bf("Rtab"), bf("wglu"), bf("small12")])

    def s5_prep_M(l):
        i = l
        pb = [bf("s5p")]
        tb_ = [bf("s5t")]
        m1 = None
        inM = TMP[:, 1400:1472].rearrange("p (a n) -> p a n", a=3)
        P.dma(lambda e: e.dma_start(out=inM, in_=s5M_d[i]), "s5ld", writes=pb)
        Am, Im, dtm = inM[:, 0, :], inM[:, 1, :], inM[:, 2, :]
        mt = [TMP[:, 1480 + 24 * q:1504 + 24 * q] for q in range(16)]
        xm, cm, sm, m1, m2, magm = mt[0:6]
        pr = [None, mt[6], mt[8], mt[10], mt[12]]
        pi_ = [None, mt[7], mt[9], mt[11], mt[13]]
        act(dtm, dtm, AF.Exp, pb, pb)
        tt(xm, Am, dtm, ALU.mult, pb, pb)
        act(rho, xm, AF.Exp, pb, pb, scale=4.0)
        act(magm, xm, AF.Exp, pb, pb)
        tt(xm, Im, dtm, ALU.mult, pb, pb)
        act(sm, xm, AF.Sin, pb, pb, scale=1.0 / 64)
        act(cm, xm, AF.Sin, pb, pb, scale=1.0 / 64, bias=epsT[:, 1:2])
        double_angle(cm, sm, m1, m2, 6)
        tt(pr[1], magm, cm, ALU.mult, pb, pb)
        tt(pi_[1], magm, sm, ALU.mult, pb, pb)
        for e_ in range(2, 5):
            cmul(pr[e_], pi_[e_], pr[e_ - 1], pi_[e_ - 1], pr[1], pi_[1], m1, m2, pb, pb)
        double_angle(cm, sm, m1, m2, 2)
        V(lambda e: e.tensor_copy(out=cs4, in_=cm), pb, pb)
        V(lambda e: e.tensor_copy(out=sn4, in_=sm), pb, pb)
        Cin = SPF[:, 0:1536].rearrange("p (r a n) -> p r a n", r=2, a=24)
        P.dma(lambda e: e.dma_start(out=Cin, in_=s5C_d[i]), "s5ld", writes=pb + tb_ + [bf("memT"), bf("mem_n")])
        Cr, Ci = Cin[:, 0], Cin[:, 1]
        c1 = SPF[:, 1536:2304].rearrange("p (a n) -> p a n", a=24)
        c2 = SPF[:, 2304:3072].rearrange("p (a n) -> p a n", a=24)
        V(lambda e: e.tensor_copy(out=CA[:, :, 0, 0, :], in_=Cr), pb, [bf("CA")])
        ts(CA[:, :, 0, 1, :], Ci, -1.0, None, ALU.mult, ALU.bypass, pb, [bf("CA")])
        for e_ in range(1, 5):
            prb = pr[e_].unsqueeze(2).to_broadcast([128, 24, 32])
            pib = pi_[e_].unsqueeze(2).to_broadcast([128, 24, 32])
            tt(c1, Cr, prb, ALU.mult, pb, tb_)
            tt(c2, Ci, pib, ALU.mult, pb, tb_)
            tt(CA[:, :, e_, 0, :], c1, c2, ALU.subtract, tb_, [bf("CA")])
            tt(c1, Cr, pib, ALU.mult, pb, tb_)
            tt(c2, Ci, prb, ALU.mult, pb, tb_)
            tt(c1, c1, c2, ALU.add, tb_, tb_)
            ts(CA[:, :, e_, 1, :], c1, -1.0, None, ALU.mult, ALU.bypass, tb_, [bf("CA")])
        Rr, Ri = Rtab[:, 0], Rtab[:, 1]
        V(lambda e: e.memset(Rr[:, :, 0:1], 1.0), writes=[bf("Rtab")])
        V(lambda e: e.memset(Ri[:, :, 0:1], 0.0), writes=[bf("Rtab")])
        qr, qi = cm, sm
        n = 1
        while n < 64:
            qrb = qr.unsqueeze(2).to_broadcast([128, 24, n])
            qib = qi.unsqueeze(2).to_broadcast([128, 24, n])
            a1 = c1[:, :, 0:n]
            a2 = c2[:, :, 0:n]
            tt(a1, Rr[:, :, 0:n], qrb, ALU.mult, pb + [bf("Rtab")], tb_)
            tt(a2, Ri[:, :, 0:n], qib, ALU.mult, pb + [bf("Rtab")], tb_)
            tt(Rr[:, :, n:2 * n], a1, a2, ALU.subtract, tb_, [bf("Rtab")])
            tt(a1, Rr[:, :, 0:n], qib, ALU.mult, pb + [bf("Rtab")], tb_)
            tt(a2, Ri[:, :, 0:n], qrb, ALU.mult, pb + [bf("Rtab")], tb_)
            tt(Ri[:, :, n:2 * n], a1, a2, ALU.add, tb_, [bf("Rtab")])
            double_angle(qr, qi, m1, m2, 1)
            n *= 2
        V(lambda e: e.memset(small[:, 16:64], 0.0), writes=[bf("Sc")])
        V(lambda e: e.memset(TMP[:, 1992:2312], 0.0), writes=[bf("Z3")])

        gsrc = wglu_d[i].rearrange("(k p) m -> p k m", p=128)
        for m in range(6):
            wload(gsrc[:, :, m * 128:(m + 1) * 128], wglu_bf[:, :, m * 128:(m + 1) * 128], bf("wglu"))
        P.op("gpsimd", lambda e: e.memset(sgt[0][:], 0.0), writes=[bf("sgt0")])
        P.op("gpsimd", lambda e: e.memset(sgt[1][:], 0.0), writes=[bf("sgt1")])
        P.op("gpsimd", lambda e: e.memset(stg[0][:], 0.0), writes=[bf("stg0")])
        P.op("gpsimd", lambda e: e.memset(wgb[1][:], 0.0), writes=[bf("wg1")])

    def s5_block(l, blk):
        i = l
        L2 = [SPF[:, 1536 + 256 * q:1792 + 256 * q].rearrange("p (a c) -> p a c", a=4) for q in range(6)]
        t1, t2, Xr, Xi, Vr, Vi = L2
        Sf = TMP[:, 0:520].rearrange("p (r a c) -> p r a c", r=2, a=4)
        Spb = TMP[:, 520:776].bitcast(BF16).rearrange("p (a r c) -> p a r c", a=4, r=2)
        yv = TMP[:, 776:1032]
        sgS = TMP[:, 1032:1160].bitcast(BF16)
        ini = TMP[:, 1160:1176].rearrange("p (a q) -> p a q", a=4)
        hS = h_blk
        stg0b = stg[0][:].bitcast(BF16)
        wflat = [w_[:].rearrange("p k n -> p (k n)") for w_ in wgb]
        um_set = [[sgt[0][:, 0:256], sgt[0][:, 256:512], sgt[1][:, 0:256], sgt[1][:, 256:512]],
                  [stg0b[:, q * 256:(q + 1) * 256] for q in range(4)]]
        um_buf = [[bf("sgt0"), bf("sgt0"), bf("sgt1"), bf("sgt1")], [bf("stg0")] * 4]
        Dsb_set = [SPF[:, 1024:1536], wflat[0].bitcast(F32)]
        Dsb_buf = [bf("Dsb"), bf("wg0")]
        sh_set = [[SPARE[:, q * 256:(q + 1) * 256] for q in range(8)],
                  [wflat[2][:, q * 256:(q + 1) * 256] for q in range(4)] + [wflat[3][:, q * 256:(q + 1) * 256] for q in range(4)]]
        sh_buf = [[bf(f"sh{q}") for q in range(8)], [bf("wu0")] * 4 + [bf("wu1")] * 4]
        Z3_set = [TMP[:, 1992:2312].bitcast(BF16).rearrange("p (e r n) -> p e r n", e=5, r=2),
                  wflat[1][:, 0:640].rearrange("p (e r n) -> p e r n", e=5, r=2)]
        Z3_buf = [bf("Z3"), bf("wg1")]
        bDs = {}

        def stageA(tau):
            pb_ = tau % 2
            um, ub = um_set[pb_], um_buf[pb_]
            um4 = [u_.rearrange("p (c j) -> p c j", j=4) for u_ in um]
            zb = bf(f"z{tau}")
            for q in range(3):
                act(um[q][q * 32:(q + 1) * 32, :], z_blk[q * 32:(q + 1) * 32, tau, :], AF.Copy, [zb], [ub[q]])
            act(um[3][64:128, :], z_blk[64:128, tau, :], AF.Copy, [zb], [ub[3]])
            P.op("gpsimd", lambda e: e.memset(um[3][64:96, :], 0.0), writes=[ub[3]])
            P.op("gpsimd", lambda e: e.tensor_copy(out=Z3_set[pb_][:, :, :, 32:64], in_=CA[:, 4 * tau + 3, :, :, :]), reads=[bf("CA")], writes=[Z3_buf[pb_]])
            bD = psb(2, 4)
            for q in range(4):
                for ri in range(2):
                    slot = q * 2 + ri
                    for t in range(4):
                        mm(PS[bD][:, slot * 64:(slot + 1) * 64], WB[:, tau, 3 - t, ri, :],
                           um4[q][:, :, t], t == 0, t == 3, [bf("WB"), ub[q]], [bf(f"ps{bD}")])
            act(Dsb_set[pb_], PS[bD][:, 0:512], AF.Copy, [bf(f"ps{bD}")], [Dsb_buf[pb_]])
            for q in range(4):
                for ri in range(2):
                    slot = q * 2 + ri
                    b0 = slot % 2
                    half = (slot // 2) % 2
                    pv = PS[b0][:, half * 256:(half + 1) * 256]
                    pv4 = pv.rearrange("p (c j) -> p c j", j=4)
                    pbuf = bf(f"ps{b0}")
                    for k in range(4):
                        mm(pv4[:, :, k:4], WB[:, tau, k, ri, :], um4[q][:, :, 0:4 - k],
                           k == 0, k == 3, [bf("WB"), ub[q]], [pbuf])
                    act(sh_set[pb_][slot], pv, AF.Copy, [pbuf], [sh_buf[pb_][slot]])

        def stageB(tau):
            pb_ = tau % 2
            Dsb = Dsb_set[pb_].rearrange("p (q r c) -> p q r c", q=4, r=2)
            Dr, Di = Dsb[:, :, 0, :], Dsb[:, :, 1, :]
            Rr, Ri = Rtab[:, 0, 4 * tau:4 * tau + 4, :], Rtab[:, 1, 4 * tau:4 * tau + 4, :]
            rb_ = [bf("Rtab"), Dsb_buf[pb_]]
            tt(t1, Rr, Dr, ALU.mult, rb_, [bf("l2t")])
            tt(t2, Ri, Di, ALU.mult, rb_, [bf("l2t")])
            tt(Xr, t1, t2, ALU.add, [bf("l2t")], [bf("X")])
            tt(t1, Rr, Di, ALU.mult, rb_, [bf("l2t")])
            tt(t2, Ri, Dr, ALU.mult, rb_, [bf("l2t")])
            tt(Xi, t1, t2, ALU.subtract, [bf("l2t")], [bf("X")])
            scr, sci = Sc[:, 4 * tau:4 * tau + 4, 0], Sc[:, 4 * tau:4 * tau + 4, 1]
            c4, s4 = cs4[:, 4 * tau:4 * tau + 4], sn4[:, 4 * tau:4 * tau + 4]
            ir, ii, ta, tb2 = ini[:, 0, :], ini[:, 1, :], ini[:, 2, :], ini[:, 3, :]
            ib = [bf("ini")]
            tt(ta, c4, scr, ALU.mult, [bf("Sc"), bf("s5p")], ib)
            tt(tb2, s4, sci, ALU.mult, [bf("Sc"), bf("s5p")], ib)
            tt(ir, ta, tb2, ALU.subtract, ib, ib)
            tt(ta, s4, scr, ALU.mult, [bf("Sc"), bf("s5p")], ib)
            tt(tb2, c4, sci, ALU.mult, [bf("Sc"), bf("s5p")], ib)
            tt(ii, ta, tb2, ALU.add, ib, ib)
            for q in range(4):
                pr_ = 4 * tau + q
                rbc = rho[:, pr_:pr_ + 1].to_broadcast([128, 64])
                V(lambda e, q=q, rbc=rbc: e.tensor_tensor_scan(out=Vr[:, q, :], data0=rbc, data1=Xr[:, q, :], initial=ir[:, q:q + 1],
                                                                op0=ALU.mult, op1=ALU.add), [bf("X"), bf("ini"), bf("s5p")], [bf("Vs")])
                V(lambda e, q=q, rbc=rbc: e.tensor_tensor_scan(out=Vi[:, q, :], data0=rbc, data1=Xi[:, q, :], initial=ii[:, q:q + 1],
                                                                op0=ALU.mult, op1=ALU.add), [bf("X"), bf("ini"), bf("s5p")], [bf("Vs")])
            V(lambda e: e.tensor_copy(out=Sf[:, 0, :, 0], in_=scr), [bf("Sc")], [bf("Sf")])
            V(lambda e: e.tensor_copy(out=Sf[:, 1, :, 0], in_=sci), [bf("Sc")], [bf("Sf")])
            rv = [bf("Rtab"), bf("Vs")]
            tt(t1, Rr, Vr, ALU.mult, rv, [bf("l2t")])
            tt(t2, Ri, Vi, ALU.mult, rv, [bf("l2t")])
            tt(Sf[:, 0, :, 1:65], t1, t2, ALU.subtract, [bf("l2t")], [bf("Sf")])
            tt(t1, Rr, Vi, ALU.mult, rv, [bf("l2t")])
            tt(t2, Ri, Vr, ALU.mult, rv, [bf("l2t")])
            tt(Sf[:, 1, :, 1:65], t1, t2, ALU.add, [bf("l2t")], [bf("Sf")])
            act(Spb[:, :, 0, :], Sf[:, 0, :, 0:64], AF.Copy, [bf("Sf")], [bf("Spb")])
            act(Spb[:, :, 1, :], Sf[:, 1, :, 0:64], AF.Copy, [bf("Sf")], [bf("Spb")])
            V(lambda e: e.tensor_copy(out=scr, in_=Sf[:, 0, :, 64]), [bf("Sf")], [bf("Sc")])
            V(lambda e: e.tensor_copy(out=sci, in_=Sf[:, 1, :, 64]), [bf("Sf")], [bf("Sc")])

        def stageC(tau):
            pb_ = tau % 2
            sh, shb = sh_set[pb_], sh_buf[pb_]
            Z3v = Z3_set[pb_]
            by = 4 + (tau % 2)
            for q in (3, 2, 0, 1):
                pr_ = 4 * tau + q
                if q == 3:
                    out = PS[by][64:128, 0:256]
                    lw = lambda e_, r_: Z3v[:, e_, r_, :]
                    wb_ = [Z3_buf[pb_]]
                else:
                    out = PS[by][q * 32:(q + 1) * 32, 0:256]
                    lw = lambda e_, r_, pr_=pr_: CA[:, pr_, e_, r_, :]
                    wb_ = [bf("CA")]
                out4 = out.rearrange("p (c j) -> p c j", j=4)
                first = (q != 2)
                mm(out, lw(0, 0), sh[q * 2], first, False, wb_ + [shb[q * 2]], [bf(f"ps{by}")])
                mm(out, lw(0, 1), sh[q * 2 + 1], False, False, wb_ + [shb[q * 2 + 1]], [bf(f"ps{by}")])
                for j in range(4):
                    mm(out4[:, :, j], lw(j + 1, 0), Spb[:, q, 0, :], False, False, wb_ + [bf("Spb")], [bf(f"ps{by}")])
                    mm(out4[:, :, j], lw(j + 1, 1), Spb[:, q, 1, :], False, j == 3, wb_ + [bf("Spb")], [bf(f"ps{by}")])

        def stageC2(tau):
            by = 4 + (tau % 2)
            stt(yv, z_blk[:, tau, :], cs("dF", i * 6 + tau), PS[by][:, 0:256], ALU.mult, ALU.add, [bf(f"z{tau}"), bf(f"ps{by}"), bf("cst")], [bf("yv")])
            act(hS[:, tau, :], yv, AF.Gelu_apprx_tanh, [bf("yv")], [bf("h_blk")])

        stageA(0)
        stageA(1)
        stageB(0)
        stageC(0)
        for tau in range(1, 6):
            if tau < 5:
                stageA(tau + 1)
            stageB(tau)
            stageC2(tau - 1)
            stageC(tau)
        stageC2(5)
        for m in range(6):
            bg = 6 + (m % 2)
            for k in range(6):
                mm(PS[bg][:, 0:256], wglu_bf[:, k, m * 128:(m + 1) * 128], hS[:, k, :], k == 0, k == 5,
                   [bf("wglu"), bf("h_blk")], [bf(f"ps{bg}")])
            act(sgS, PS[bg][:, 0:256], AF.Sigmoid, [bf(f"ps{bg}")], [bf("sgS")])
            tt(cat_blk[:, m, :], hS[:, m, :], sgS, ALU.mult, [bf("h_blk"), bf("sgS")], [bf(f"cat{m}")])

    import os
    STG = int(os.environ.get("KSTAGE", "99"))
    for l in range(n_layers):
        if l < 2:
            s5_prep_F(l)
        layer_start(l)
        if l < 2:
            s5_prep_M(l)
        P.barrier()
        if l >= 2:
            for p_ in range(2):
                zs_ = z_sets[p_]
                P.op("gpsimd", lambda e, zs_=zs_: e.memset(zs_[64:128, 0:6, :], 0.0), writes=[bf(f"z{m_}" + ("" if p_ == 0 else "_1")) for m_ in range(6)])
            P.op("gpsimd", lambda e: e.memset(TMP[0:64, 1280:2048], 0.0), writes=[bf(f"z{m_}") for m_ in range(6)])
            P.op("gpsimd", lambda e: e.memset(SPARE[0:64, 4096:5632], 0.0), writes=[bf(f"z{m_}_1") for m_ in range(6)])
            PAR["p"] = 0
            PAR["mbank"] = None
            win_block(l, 0)
            memattn_block(l, 0)
            for blk in range(NBLK):
                def hook(blk=blk):
                    if blk + 1 < NBLK:
                        PAR["p"] = (blk + 1) % 2
                        PAR["mbank"] = (2, 3)
                        win_block(l, blk + 1)
                        memattn_block(l, blk + 1)
                        PAR["p"] = blk % 2
                PAR["p"] = blk % 2
                diff_block(l, blk, hook)
                wout_block(l, blk)
            PAR["p"] = 0
            PAR["mbank"] = None
        else:
            for blk in range(NBLK):
                win_block(l, blk)
                memattn_block(l, blk)
                s5_block(l, blk)
                wout_block(l, blk)
        if STG >= 9:
            ffn(l)
        if l == 1 and n_layers > 2:
            kv_shared()
    P.barrier()
    ysrc = yT_d.rearrange("(k p) n -> p k n", p=128)
    for k in range(8):
        P.dma(lambda e, k=k: e.dma_start(out=ysrc[:, k, :], in_=xT[:, k, :]), "yst", reads=[bf(f"x{k}_{b}") for b in range(8)])
    P.rec["sync"].append(([(P.sems["yst"], P.dma_cnt["yst"])], None, None, 0))
    return P.finish()


_NC_CACHE = {}


def _prep_inputs(inp, n_cores=8):
    f32 = np.float32
    g = lambda k: np.asarray(inp[k], dtype=f32)
    cst = np.zeros((128, NCST), f32)

    def put(name, arr):
        o, w = CST_OFF[name]
        assert arr.shape == (128, w), (name, arr.shape)
        cst[:, o:o + w] = arr

    fm = lambda a: a.reshape(a.shape[0], 8, 128).transpose(2, 0, 1).reshape(128, -1)
    put("gmix", fm(g("norm_mix")))
    put("gffn", fm(g("norm_ffn")))
    put("gmem", fm(g("norm_mem")))
    put("gkv", fm(g("kv_norm")[None]))
    p64 = np.arange(128) % 64
    put("mqn", g("mem_q_norm")[:, p64].T)
    put("mkn", g("mem_k_norm")[:, p64].T)
    put("dqn", g("diff_q_norm")[:, p64].T)
    put("dkn", g("diff_k_norm")[p64][:, None])
    put("dsn", g("diff_sub_norm").T)
    lam = np.stack([g("diff_lambda_q1"), g("diff_lambda_k1"), g("diff_lambda_q2"), g("diff_lambda_k2")])
    put("lam", np.broadcast_to(lam.reshape(1, 512), (128, 512)))
    put("dF", g("ssm_d").reshape(2, 6, 128).transpose(2, 0, 1).reshape(128, 12))
    G, N, Pp = 48, 64, 16
    a_re, a_im, ldt = g("ssm_a_re"), g("ssm_a_im"), g("ssm_log_dt")
    b_re, b_im, c_re, c_im = g("ssm_b_re"), g("ssm_b_im"), g("ssm_c_re"), g("ssm_c_im")
    s5F = np.zeros((2, 128, 6, 4, 128), f32)
    s5dtF = np.zeros((2, 128, 6), f32)
    s5M = np.zeros((2, 128, 3, 24), f32)
    s5C = np.zeros((2, 128, 2, 24, 32), f32)
    for i in range(2):
        for gg in range(G):
            tau, gl = gg // 8, gg % 8
            rows = slice(gl * 16, gl * 16 + 16)
            half = gg % 2
            cols = slice(half * 64, half * 64 + 64)
            for hh in range(2):
                s5F[i, rows, tau, 0, hh * 64:(hh + 1) * 64] = a_re[i, gg][None, :]
                s5F[i, rows, tau, 1, hh * 64:(hh + 1) * 64] = a_im[i, gg][None, :]
            s5F[i, rows, tau, 2, cols] = b_re[i, gg].T
            s5F[i, rows, tau, 3, cols] = b_im[i, gg].T
            s5dtF[i, rows, tau] = ldt[i, gg]
            pr = gg // 2
            mrows = slice(half * 64, half * 64 + 64)
            s5M[i, mrows, 0, pr] = a_re[i, gg]
            s5M[i, mrows, 1, pr] = a_im[i, gg]
            s5M[i, mrows, 2, pr] = ldt[i, gg]
            s5C[i, mrows, 0, pr, half * 16:(half + 1) * 16] = c_re[i, gg].T
            s5C[i, mrows, 1, pr, half * 16:(half + 1) * 16] = c_im[i, gg].T
    shared = {"cst": cst, "s5F": s5F, "s5dtF": s5dtF, "s5M": s5M, "s5C": s5C}
    for k in ["w_in", "w_out", "mem_wk", "mem_wv", "ffn_w_gate", "ffn_w_up", "ffn_w_down", "ssm_w_glu", "diff_wk", "diff_wv"]:
        shared[k] = np.ascontiguousarray(g(k))
    x, mem = g("x"), g("mem")
    maps = []
    for b in range(n_cores):
        d = dict(shared)
        d["xT"] = np.ascontiguousarray(x[b].T)
        d["memT"] = np.ascontiguousarray(mem[b].T)
        maps.append(d)
    return maps


def kernel(**inputs):
    n_layers = int(inputs.pop("_n_layers", NL))
    maps = _prep_inputs(inputs)
    nc = build(n_layers)
    res = run_bass_kernel_spmd(nc, maps, core_ids=list(range(8)))
    out = np.stack([np.ascontiguousarray(r["yT"].T) for r in res.results], axis=0)
    return out.astype(np.float32)
```

```python
import contextlib
import numpy as np
import concourse.bass as bass
import concourse.mybir as mybir
from concourse.bass_utils import run_bass_kernel_spmd

F32 = mybir.dt.float32
BF16 = mybir.dt.bfloat16
AF = mybir.ActivationFunctionType
ALU = mybir.AluOpType
AX = mybir.AxisListType

SEM_ROT = 20000


class Buf:
    __slots__ = ("name", "w", "r")

    def __init__(self, name=""):
        self.name = name
        self.w = None
        self.r = []


class Prog:
    ENGS = ("tensor", "vector", "scalar", "gpsimd", "sync")

    def __init__(self):
        self.nc = bass.Bass("TRN2", target_bir_lowering=False)
        self.stack = contextlib.ExitStack()
        self.rec = {e: [] for e in self.ENGS}
        self.sems = {}
        self.cur = {}
        self.epoch = {e: 0 for e in self.ENGS}
        self.known = {e: {} for e in self.ENGS}
        self.dma_cnt = {}
        self.relax = False
        for e in self.ENGS:
            self._new_epoch(e)

    def sem(self, key):
        if key not in self.sems:
            self.sems[key] = self.stack.enter_context(self.nc.semaphore(str(key)))
        return self.sems[key]

    def _new_epoch(self, e):
        self.epoch[e] += 1
        key = (e, self.epoch[e])
        self.sem(key)
        self.cur[e] = [key, 0]

    def sbuf(self, name, shape, dt):
        return self.stack.enter_context(self.nc.sbuf_tensor(name, list(shape), dt))

    def psum(self, name, shape, dt=F32):
        return self.stack.enter_context(self.nc.psum_tensor(name, list(shape), dt))

    def dram(self, name, shape, dt, kind):
        return self.nc.dram_tensor(name, list(shape), dt, kind=kind).ap()

    def _deps(self, eng, reads, writes):
        toks = []
        for b in reads:
            if b.w is not None:
                toks.append(b.w)
        for b in writes:
            if b.w is not None:
                toks.append(b.w)
            toks.extend(b.r)
        need = {}
        for (k, v) in toks:
            if eng == "tensor" and k[0] == "tensor":
                continue
            if self.known[eng].get(k, 0) >= v:
                continue
            if need.get(k, 0) < v:
                need[k] = v
        for k, v in need.items():
            self.known[eng][k] = v
        return [(self.sems[k], v) for k, v in need.items()]

    def op(self, eng, fn, reads=(), writes=()):
        waits = self._deps(eng, reads, writes)
        cur = self.cur[eng]
        if cur[1] >= SEM_ROT:
            self._new_epoch(eng)
            cur = self.cur[eng]
        cur[1] += 1
        tok = (cur[0], cur[1])
        self.rec[eng].append((waits, fn, self.sems[cur[0]], 1))
        for b in writes:
            b.w = tok
            b.r = []
        for b in reads:
            if b not in writes:
                b.r.append(tok)
        return tok

    def dma(self, fn, semkey, reads=(), writes=(), eng="sync"):
        waits = self._deps(eng, reads, writes)
        self.sem(semkey)
        self.dma_cnt[semkey] = self.dma_cnt.get(semkey, 0) + 16
        tok = (semkey, self.dma_cnt[semkey])
        self.rec[eng].append((waits, fn, self.sems[semkey], 16))
        for b in writes:
            b.w = tok
            b.r = []
        for b in reads:
            if b not in writes:
                b.r.append(tok)
        return tok

    def wait_all(self, eng, bufs):
        waits = self._deps(eng, bufs, ())
        self.rec[eng].append((waits, None, None, 0))


    def barrier(self):
        toks = []
        for e in self.ENGS:
            for ep in range(1, self.epoch[e] + 1):
                k = (e, ep)
                v = self.cur[e][1] if ep == self.epoch[e] else None
                if v is None:
                    continue
                if v > 0:
                    toks.append((k, v))
        for k, v in self.dma_cnt.items():
            toks.append((k, v))
        for e in self.ENGS:
            waits = []
            for (k, v) in toks:
                if k[0] == e and isinstance(k, tuple) and k[0] in self.ENGS and e != "sync":
                    pass
                if self.known[e].get(k, 0) >= v:
                    continue
                self.known[e][k] = v
                waits.append((self.sems[k], v))
            if waits:
                self.rec[e].append((waits, None, None, 0))

    def finish(self):
        nc = self.nc
        rec = self.rec

        def replay(e, name):
            for (waits, fn, sem, inc) in rec[name]:
                for (s, v) in waits:
                    e.wait_ge(s, v)
                if fn is not None:
                    ins = fn(e)
                    ins.then_inc(sem, inc)

        with nc.Block() as block:
            @block.tensor
            def _(e):
                replay(e, "tensor")

            @block.vector
            def _(e):
                replay(e, "vector")

            @block.scalar
            def _(e):
                replay(e, "scalar")

            @block.gpsimd
            def _(e):
                replay(e, "gpsimd")

            @block.sync
            def _(e):
                replay(e, "sync")
        self.stack.close()
        return nc

import math

EPS = 1e-6
NL = 4
BLK = 256
NBLK = 8
DFF = 2816
NF = 22
FC = 4


def _cst_layout():
    off = {}
    n = 0
    for name, w in [("gmix", 32), ("gffn", 32), ("gmem", 32), ("gkv", 8), ("mqn", 4), ("mkn", 4),
                    ("dqn", 2), ("dkn", 1), ("dsn", 2), ("lam", 512), ("dF", 12)]:
        off[name] = (n, w)
        n += w
    return off, n


CST_OFF, NCST = _cst_layout()


def build(n_layers=NL):
    P = Prog()
    nc = P.nc
    dI = lambda name, shape: P.dram(name, shape, F32, "ExternalInput")
    xT_d = dI("xT", [1024, 2048])
    memT_d = dI("memT", [1024, 256])
    cst_d = dI("cst", [128, NCST])
    w_in_d = dI("w_in", [4, 1024, 1024])
    w_out_d = dI("w_out", [4, 1024, 1024])
    wk_d = dI("mem_wk", [4, 1024, 256])
    wv_d = dI("mem_wv", [4, 1024, 256])
    wg_d = dI("ffn_w_gate", [4, 1024, DFF])
    wu_d = dI("ffn_w_up", [4, 1024, DFF])
    wd_d = dI("ffn_w_down", [4, DFF, 1024])
    wglu_d = dI("ssm_w_glu", [2, 768, 768])
    dwk_d = dI("diff_wk", [1024, 768])
    dwv_d = dI("diff_wv", [1024, 768])
    s5F_d = dI("s5F", [2, 128, 6, 4, 128])
    s5dtF_d = dI("s5dtF", [2, 128, 6])
    s5M_d = dI("s5M", [2, 128, 3, 24])
    s5C_d = dI("s5C", [2, 128, 2, 24, 32])
    yT_d = P.dram("yT", [1024, 2048], F32, "ExternalOutput")

    xT = P.sbuf("xT_sb", [128, 8, 2048], F32)
    MIX = P.sbuf("MIX", [128, 24576], BF16)
    ARENA = P.sbuf("ARENA", [128, 28672], BF16)
    stg = [P.sbuf(f"stg{i}", [128, 1024], F32) for i in range(2)]
    wgb = [P.sbuf(f"wgb{i}", [128, 8, 128], BF16) for i in range(4)]
    cst = P.sbuf("cst_sb", [128, NCST], F32)
    ones_bf = P.sbuf("ones_bf", [128, 128], BF16)
    blk64 = P.sbuf("blk64", [128, 128], BF16)
    epsT = P.sbuf("epsT", [128, 2], F32)
    KmT = P.sbuf("KmT", [128, 2, 256], BF16)
    Vm = P.sbuf("Vm", [128, 2, 256], BF16)
    sq = [P.sbuf(f"sq{i}", [128, 256], BF16) for i in range(2)]
    rstd = [P.sbuf(f"rstd{i}", [128, 256], F32) for i in range(2)]
    sgt = [P.sbuf(f"sgt{i}", [128, 512], BF16) for i in range(2)]
    eM = [P.sbuf(f"eM{i}", [128, 256], BF16) for i in range(2)]
    rrm = P.sbuf("rrm", [128, 256], F32)
    TMP = P.sbuf("TMP", [128, 2320], F32)
    small = P.sbuf("small", [128, 64], F32)
    PS = [P.psum(f"ps{i}", [128, 512]) for i in range(8)]

    B = {}

    def bf(name):
        if name not in B:
            B[name] = Buf(name)
        return B[name]

    def cs(name, i=0, n=1):
        o, w = CST_OFF[name]
        return cst[:, o + i:o + i + n]

    w_in_bf = ARENA[:, 0:8192].rearrange("p (k m) -> p k m", k=8)
    w_out_bf = ARENA[:, 8192:16384].rearrange("p (k m) -> p k m", k=8)
    h_blk = ARENA[:, 16384:18432].rearrange("p (k n) -> p k n", k=8)
    z_blk = ARENA[:, 18432:20480].rearrange("p (k n) -> p k n", k=8)
    cat_blk = ARENA[:, 20480:22528].rearrange("p (k n) -> p k n", k=8)
    SPARE = ARENA[:, 22528:28672]
    hT = ARENA[:, 0:16384].rearrange("p (k n) -> p k n", k=8)
    aT = ARENA[:, 16384:24576].rearrange("p (f n) -> p f n", f=FC)
    wdb = ARENA[:, 24576:28672].rearrange("p (f n) -> p f n", f=FC)
    KdT = MIX[:, 0:12288].rearrange("p (t n) -> p t n", t=6)
    Vd = MIX[:, 12288:24576].rearrange("p (t n) -> p t n", t=16)
    WB = MIX[:, 0:6144].rearrange("p (t k r n) -> p t k r n", t=6, k=4, r=2)
    CA = MIX[:, 6144:13824].rearrange("p (a e r n) -> p a e r n", a=24, e=5, r=2)
    Rtab = MIX[:, 13824:19968].bitcast(F32).rearrange("p (r a c) -> p r a c", r=2, a=24)
    wglu_bf = MIX[:, 19968:24576].rearrange("p (k m) -> p k m", k=6)
    memT_sb = SPARE[:, 0:4096].bitcast(F32).rearrange("p (k n) -> p k n", k=8)
    mem_n = SPARE[:, 4096:6144].rearrange("p (k n) -> p k n", k=8)
    wk_bf = ARENA[:, 16384:18432].rearrange("p (k n) -> p k n", k=8)
    wv_bf = ARENA[:, 18432:20480].rearrange("p (k n) -> p k n", k=8)

    PAR = {"p": 0, "mbank": None}
    h_sets = [h_blk, SPARE[:, 0:2048].rearrange("p (k n) -> p k n", k=8)]
    z_sets = [z_blk, SPARE[:, 2048:4096].rearrange("p (k n) -> p k n", k=8)]
    catm_sets = [cat_blk[:, 6:8, :], SPARE[:, 5632:6144].rearrange("p (k n) -> p k n", k=2)]

    def HB():
        return h_sets[PAR["p"]]

    def ZB():
        return z_sets[PAR["p"]]

    def CATM():
        return catm_sets[PAR["p"]]

    def sfx():
        return "" if PAR["p"] == 0 else "_1"

    def V(fn, reads=(), writes=(), eng="vector"):
        return P.op(eng, fn, reads=reads, writes=writes)

    def tt(out, a, b, op, reads, writes, eng="vector"):
        return P.op(eng, lambda e: e.tensor_tensor(out=out, in0=a, in1=b, op=op), reads=reads, writes=writes)

    def stt(out, a, s, b, op0, op1, reads, writes):
        return P.op("vector", lambda e: e.scalar_tensor_tensor(out=out, in0=a, scalar=s, in1=b, op0=op0, op1=op1),
                    reads=reads, writes=writes)

    def ts(out, a, s1, s2, op0, op1, reads, writes, eng="vector"):
        return P.op(eng, lambda e: e.tensor_scalar(out=out, in0=a, scalar1=s1, scalar2=s2, op0=op0, op1=op1),
                    reads=reads, writes=writes)

    def act(out, a, func, reads, writes, scale=1.0, bias=None):
        if bias is None:
            return P.op("scalar", lambda e: e.activation(out=out, in_=a, func=func, scale=scale), reads=reads, writes=writes)
        return P.op("scalar", lambda e: e.activation(out=out, in_=a, func=func, scale=scale, bias=bias), reads=reads, writes=writes)

    def mm(out, lhsT, rhs, start, stop, reads, writes):
        return P.op("tensor", lambda e: e.matmul(out, lhsT=lhsT, rhs=rhs, start=start, stop=stop), reads=reads, writes=writes)

    stg_i = [0]

    cast_rot = [0]

    def wload(src, dst, dstbuf, shape3=None, engs=("gpsimd",)):
        i = stg_i[0] % 2
        stg_i[0] += 1
        n = 1
        for d in src.shape[1:]:
            n *= d
        sv = stg[i][:, 0:n]
        if len(src.shape) == 3:
            sv = sv.rearrange("p (a b) -> p a b", a=src.shape[1])
        P.dma(lambda e: e.dma_start(out=sv, in_=src), ("stg", i), writes=[bf(f"stg{i}")])
        ce = engs[cast_rot[0] % len(engs)]
        cast_rot[0] += 1
        if ce == "scalar":
            act(dst, sv, AF.Copy, [bf(f"stg{i}")], [dstbuf])
        else:
            P.op(ce, lambda e: e.tensor_copy(out=dst, in_=sv), reads=[bf(f"stg{i}")], writes=[dstbuf])

    psn = {}

    def psb(lo, hi):
        k = (lo, hi)
        psn[k] = psn.get(k, -1) + 1
        return lo + psn[k] % (hi - lo)

    rot = {}

    def rr(name, n):
        rot[name] = rot.get(name, -1) + 1
        return rot[name] % n

    def rstd_from(ps_ap, ncols, scale, reads):
        i = rr("rstd", 2)
        r = rstd[i][:, 0:ncols]
        act(r, ps_ap, AF.Ln, reads, [bf(f"rstd{i}")], scale=scale, bias=epsT[:, 0:1])
        act(r, r, AF.Exp, [bf(f"rstd{i}")], [bf(f"rstd{i}")], scale=-0.5)
        return r, bf(f"rstd{i}")

    def group_norm_evac(ps_ap, psbuf, ncols, gain, dst, dstbufs, full, split=None, defer=False):
        i = rr("sq", 2)
        s = sq[i][:, 0:ncols]
        act(s, ps_ap, AF.Square, [psbuf], [bf(f"sq{i}")])
        if defer:
            return lambda: _gne2(i, s, ps_ap, psbuf, ncols, gain, dst, dstbufs, full, split)
        _gne2(i, s, ps_ap, psbuf, ncols, gain, dst, dstbufs, full, split)

    def _gne2(i, s, ps_ap, psbuf, ncols, gain, dst, dstbufs, full, split):
        b2 = psb(2, 4)
        mm(PS[b2][:, 0:ncols], ones_bf[:] if full else blk64[:], s, True, True, [bf(f"sq{i}"), bf("consts")], [bf(f"ps{b2}")])
        r, rb = rstd_from(PS[b2][:, 0:ncols], ncols, 1.0 / (128 if full else 64), [bf(f"ps{b2}")])
        if split is None:
            stt(dst, ps_ap, gain, r, ALU.mult, ALU.mult, [psbuf, rb, bf("cst")], dstbufs)
        else:
            lo_, hi_ = split
            stt(lo_[0:64, :], ps_ap[0:64, :], gain[0:64, :], r[0:64, :], ALU.mult, ALU.mult, [psbuf, rb, bf("cst")], dstbufs)
            stt(hi_[64:128, :], ps_ap[64:128, :], gain[64:128, :], r[64:128, :], ALU.mult, ALU.mult, [psbuf, rb, bf("cst")], dstbufs)

    zhi_sets = [[TMP[:, 1280 + 128 * m_:1408 + 128 * m_].bitcast(BF16) for m_ in range(6)],
                [SPARE[:, 4096 + 256 * m_:4352 + 256 * m_] for m_ in range(6)]]

    def ZHI(m_):
        return zhi_sets[PAR["p"]][m_]

    def xnorm(blk, gname, gl, dst, dstbuf):
        c0 = blk * BLK
        b2 = psb(2, 4)
        for k in range(8):
            i = rr("sq", 2)
            act(sq[i][:], xT[:, k, c0:c0 + BLK], AF.Square, [bf(f"x{k}_{blk}")], [bf(f"sq{i}")])
            mm(PS[b2][:, 0:BLK], ones_bf[:], sq[i][:], k == 0, k == 7, [bf(f"sq{i}"), bf("consts")], [bf(f"ps{b2}")])
        r, rb = rstd_from(PS[b2][:, 0:BLK], BLK, 1.0 / 1024, [bf(f"ps{b2}")])
        for k in range(8):
            stt(dst[:, k, :], xT[:, k, c0:c0 + BLK], cs(gname, gl * 8 + k), r, ALU.mult, ALU.mult,
                [bf(f"x{k}_{blk}"), rb, bf("cst")], [dstbuf])

    P.dma(lambda e: e.dma_start(out=cst[:], in_=cst_d[:, :]), "cstld", writes=[bf("cst")])
    V(lambda e: e.memset(ones_bf[:], 1.0), writes=[bf("consts")])
    V(lambda e: e.memset(blk64[:], 0.0), writes=[bf("consts")])
    V(lambda e: e.memset(blk64[0:64, 0:64], 1.0), writes=[bf("consts")])
    V(lambda e: e.memset(blk64[64:128, 64:128], 1.0), writes=[bf("consts")])
    V(lambda e: e.memset(epsT[:, 0:1], EPS), writes=[bf("consts")])
    V(lambda e: e.memset(epsT[:, 1:2], math.pi / 2), writes=[bf("consts")])
    xsrc = xT_d.rearrange("(k p) n -> p k n", p=128)
    for k in range(8):
        for hh in range(2):
            P.dma(lambda e, k=k, hh=hh: e.dma_start(out=xT[:, k, hh * 1024:(hh + 1) * 1024], in_=xsrc[:, k, hh * 1024:(hh + 1) * 1024]),
                  "xld", writes=[bf(f"x{k}_{b}") for b in range(hh * 4, hh * 4 + 4)])

    def lam_prep():
        lamv = cs("lam", 0, 512).rearrange("p (a j d) -> p a j d", a=4, j=2)
        tmp = TMP[:, 0:64]
        for j in range(2):
            layer = 2 + j
            linit = 0.8 - 0.6 * math.exp(-0.3 * layer)
            for a in range(2):
                tt(tmp, lamv[:, 2 * a, j, :], lamv[:, 2 * a + 1, j, :], ALU.mult, [bf("cst")], [bf("lamtmp")])
                V(lambda e, a=a, j=j: e.reduce_sum(out=small[:, 8 + a:9 + a], in_=tmp, axis=AX.X), [bf("lamtmp")], [bf("small")])
                act(small[:, 8 + a:9 + a], small[:, 8 + a:9 + a], AF.Exp, [bf("small")], [bf("small")])
            tt(small[:, 10:11], small[:, 9:10], small[:, 8:9], ALU.subtract, [bf("small")], [bf("small")])
            ts(small[:, j:j + 1], small[:, 10:11], -linit, None, ALU.add, ALU.bypass, [bf("small")], [bf("small")])
            ts(small[:, 2 + j:3 + j], cs("dsn", j), 1.0 - linit, None, ALU.mult, ALU.bypass, [bf("cst")], [bf("small")])

    lam_prep()

    def layer_start(l):
        wi = w_in_d[l].rearrange("(k p) m -> p k m", p=128)
        wo = w_out_d[l].rearrange("(k p) m -> p k m", p=128)
        for m in range(8):
            wload(wi[:, :, m * 128:(m + 1) * 128], w_in_bf[:, :, m * 128:(m + 1) * 128], bf("w_in"), engs=("scalar", "vector", "gpsimd"))
        for m in range(8):
            wload(wo[:, :, m * 128:(m + 1) * 128], w_out_bf[:, :, m * 128:(m + 1) * 128], bf("w_out"), engs=("scalar", "vector", "gpsimd"))
        msrc = memT_d.rearrange("(k p) n -> p k n", p=128)
        P.dma(lambda e: e.dma_start(out=memT_sb, in_=msrc), "memld", writes=[bf("memT")])
        wks = wk_d[l].rearrange("(k p) m -> p k m", p=128)
        wvs = wv_d[l].rearrange("(k p) m -> p k m", p=128)
        for j in range(2):
            wload(wks[:, :, j * 128:(j + 1) * 128], wk_bf[:, :, j * 128:(j + 1) * 128], bf("wk"), engs=("scalar", "vector"))
            wload(wvs[:, :, j * 128:(j + 1) * 128], wv_bf[:, :, j * 128:(j + 1) * 128], bf("wv"), engs=("scalar", "vector"))
        b2 = psb(2, 4)
        for k in range(8):
            i = rr("sq", 2)
            act(sq[i][:], memT_sb[:, k, :], AF.Square, [bf("memT")], [bf(f"sq{i}")])
            mm(PS[b2][:, 0:256], ones_bf[:], sq[i][:], k == 0, k == 7, [bf(f"sq{i}"), bf("consts")], [bf(f"ps{b2}")])
        r, rb = rstd_from(PS[b2][:, 0:256], 256, 1.0 / 1024, [bf(f"ps{b2}")])
        for k in range(8):
            stt(mem_n[:, k, :], memT_sb[:, k, :], cs("gmem", l * 8 + k), r, ALU.mult, ALU.mult,
                [bf("memT"), rb, bf("cst")], [bf("mem_n")])
        for j in range(2):
            b0 = psb(0, 2)
            for k in range(8):
                mm(PS[b0][:, 0:256], wk_bf[:, k, j * 128:(j + 1) * 128], mem_n[:, k, :], k == 0, k == 7,
                   [bf("wk"), bf("mem_n")], [bf(f"ps{b0}")])
            group_norm_evac(PS[b0][:, 0:256], bf(f"ps{b0}"), 256, cs("mkn", l), KmT[:, j, :], [bf("KmT")], False)
        for i2 in range(2):
            b0 = psb(0, 2)
            for k in range(8):
                mm(PS[b0][:, 0:256], mem_n[:, k, i2 * 128:(i2 + 1) * 128], wv_bf[:, k, :], k == 0, k == 7,
                   [bf("wv"), bf("mem_n")], [bf(f"ps{b0}")])
            act(Vm[:, i2, :], PS[b0][:, 0:256], AF.Copy, [bf(f"ps{b0}")], [bf("Vm")])

    def win_block(l, blk):
        hb_, zb_ = HB(), ZB()
        hbuf = bf("h_blk" + sfx())
        xnorm(blk, "gmix", l, hb_, hbuf)
        for m in range(8):
            b0 = psb(0, 2)
            for k in range(8):
                mm(PS[b0][:, 0:BLK], w_in_bf[:, k, m * 128:(m + 1) * 128], hb_[:, k, :], k == 0, k == 7,
                   [bf("w_in"), hbuf], [bf(f"ps{b0}")])
            zbuf = bf(f"z{m}" + sfx())
            if m >= 6:
                group_norm_evac(PS[b0][:, 0:BLK], bf(f"ps{b0}"), BLK, cs("mqn", l), zb_[:, m, :], [zbuf], False)
            elif l >= 2:
                group_norm_evac(PS[b0][:, 0:BLK], bf(f"ps{b0}"), BLK, cs("dqn", l - 2), None, [zbuf], False,
                                split=(zb_[:, m, :], ZHI(m)))
            else:
                act(zb_[:, m, :], PS[b0][:, 0:BLK], AF.Copy, [bf(f"ps{b0}")], [zbuf])

    def memattn_block(l, blk):
        zb_, cm_ = ZB(), CATM()
        for h in range(4):
            j = h // 2
            po = (h % 2) * 64
            if PAR["mbank"] is None:
                nb, dbk = 4 + (h % 2), 6 + (h % 2)
            else:
                nb, dbk = PAR["mbank"]
            zbuf = bf(f"z{6 + j}" + sfx())
            eis = []
            for i2 in range(2):
                b0 = psb(0, 2)
                mm(PS[b0][:, 0:BLK], KmT[po:po + 64, j, i2 * 128:(i2 + 1) * 128], zb_[po:po + 64, 6 + j, :], True, True,
                   [bf("KmT"), zbuf], [bf(f"ps{b0}")])
                ei = rr("eM", 2)
                act(eM[ei][:], PS[b0][:, 0:BLK], AF.Exp, [bf(f"ps{b0}")], [bf(f"eM{ei}")], scale=0.125)
                eis.append(ei)
            for i2 in range(2):
                ei = eis[i2]
                mm(PS[nb][po:po + 64, 0:BLK], Vm[:, i2, h * 64:(h + 1) * 64], eM[ei][:], i2 == 0, i2 == 1,
                   [bf("Vm"), bf(f"eM{ei}")], [bf(f"ps{nb}")])
                mm(PS[dbk][po:po + 64, 0:BLK], ones_bf[:, 0:64], eM[ei][:], i2 == 0, i2 == 1,
                   [bf("consts"), bf(f"eM{ei}")], [bf(f"ps{dbk}")])
            V(lambda e, po=po, dbk=dbk: e.reciprocal(out=rrm[po:po + 64, :], in_=PS[dbk][po:po + 64, 0:BLK]), [bf(f"ps{dbk}")], [bf(f"rrm{po}")])
            tt(cm_[po:po + 64, j, :], PS[nb][po:po + 64, 0:BLK], rrm[po:po + 64, :], ALU.mult,
               [bf(f"ps{nb}"), bf(f"rrm{po}")], [bf(f"cat{6 + j}" + sfx())])

    def wout_block(l, blk):
        c0 = blk * BLK
        cm_ = CATM()
        for m in range(8):
            b0 = psb(0, 2)
            for k in range(8):
                rhs = cat_blk[:, k, :] if k < 6 else cm_[:, k - 6, :]
                cbuf = bf(f"cat{k}") if k < 6 else bf(f"cat{k}" + sfx())
                mm(PS[b0][:, 0:BLK], w_out_bf[:, k, m * 128:(m + 1) * 128], rhs, k == 0, k == 7,
                   [bf("w_out"), cbuf], [bf(f"ps{b0}")])
            tt(xT[:, m, c0:c0 + BLK], xT[:, m, c0:c0 + BLK], PS[b0][:, 0:BLK], ALU.add,
               [bf(f"ps{b0}"), bf(f"x{m}_{blk}")], [bf(f"x{m}_{blk}")])

    def ffn(l):
        P.barrier()
        gsrc = wg_d[l].rearrange("(k p) f -> p k f", p=128)
        usrc = wu_d[l].rearrange("(k p) f -> p k f", p=128)
        dsrc = wd_d[l].rearrange("(f p) m -> p f m", p=128)
        f = 0
        while f < NF:
            nfc = min(FC, NF - f)
            for fi in range(nfc):
                ff = f + fi
                gi = rr("wgb", 2)
                wload(gsrc[:, :, ff * 128:(ff + 1) * 128], wgb[gi][:], bf(f"wg{gi}"))
                wload(usrc[:, :, ff * 128:(ff + 1) * 128], wgb[2 + gi][:], bf(f"wu{gi}"))
                wload(dsrc[:, ff, :], wdb[:, fi, :], bf(f"wd{fi}"))
                for tb in range(4):
                    if ff == 0:
                        for blk in (2 * tb, 2 * tb + 1):
                            xnorm(blk, "gffn", l, hT[:, :, blk * BLK:(blk + 1) * BLK], bf(f"hT{blk}"))
                    bg = psb(0, 2)
                    bu = psb(2, 4)
                    for k in range(8):
                        mm(PS[bg][:], wgb[gi][:, k, :], hT[:, k, tb * 512:(tb + 1) * 512], k == 0, k == 7,
                           [bf(f"wg{gi}"), bf(f"hT{2 * tb}"), bf(f"hT{2 * tb + 1}")], [bf(f"ps{bg}")])
                    for k in range(8):
                        mm(PS[bu][:], wgb[2 + gi][:, k, :], hT[:, k, tb * 512:(tb + 1) * 512], k == 0, k == 7,
                           [bf(f"wu{gi}"), bf(f"hT{2 * tb}"), bf(f"hT{2 * tb + 1}")], [bf(f"ps{bu}")])
                    si = rr("sgt", 2)
                    act(sgt[si][:], PS[bg][:], AF.Silu, [bf(f"ps{bg}")], [bf(f"sgt{si}")])
                    tt(aT[:, fi, tb * 512:(tb + 1) * 512], sgt[si][:], PS[bu][:], ALU.mult,
                       [bf(f"sgt{si}"), bf(f"ps{bu}")], [bf(f"aT{fi}_{tb}")])
            for tb in range(4):
                for m in range(8):
                    bd = psb(4, 8)
                    for fi in range(nfc):
                        mm(PS[bd][:], wdb[:, fi, m * 128:(m + 1) * 128], aT[:, fi, tb * 512:(tb + 1) * 512], fi == 0, fi == nfc - 1,
                           [bf(f"wd{fi}"), bf(f"aT{fi}_{tb}")], [bf(f"ps{bd}")])
                    tt(xT[:, m, tb * 512:(tb + 1) * 512], xT[:, m, tb * 512:(tb + 1) * 512], PS[bd][:], ALU.add,
                       [bf(f"ps{bd}"), bf(f"x{m}_{2 * tb}"), bf(f"x{m}_{2 * tb + 1}")], [bf(f"x{m}_{2 * tb}"), bf(f"x{m}_{2 * tb + 1}")])
            f += nfc
        P.barrier()

    def kv_shared():
        for blk in range(NBLK):
            xnorm(blk, "gkv", 0, hT[:, :, blk * BLK:(blk + 1) * BLK], bf(f"hT{blk // 2}"))
        ksrc = dwk_d.rearrange("(k p) f -> p k f", p=128)
        vsrc = dwv_d.rearrange("(k p) f -> p k f", p=128)
        for t in range(6):
            gi = rr("wgb", 2)
            wload(ksrc[:, :, t * 128:(t + 1) * 128], wgb[gi][:], bf(f"wg{gi}"), engs=("scalar", "gpsimd"))
            wload(vsrc[:, :, t * 128:(t + 1) * 128], wgb[2 + gi][:], bf(f"wu{gi}"), engs=("scalar", "gpsimd"))
            for blk in range(NBLK):
                b0 = psb(0, 2)
                for k in range(8):
                    mm(PS[b0][:, 0:BLK], wgb[gi][:, k, :], hT[:, k, blk * BLK:(blk + 1) * BLK], k == 0, k == 7,
                       [bf(f"wg{gi}"), bf(f"hT{blk // 2}")], [bf(f"ps{b0}")])
                group_norm_evac(PS[b0][:, 0:BLK], bf(f"ps{b0}"), BLK, cs("dkn", 0), KdT[:, t, blk * BLK:(blk + 1) * BLK],
                                [bf(f"KdT{blk}")], False)
                for tt_ in (2 * blk, 2 * blk + 1):
                    b4 = psb(4, 8)
                    for k in range(8):
                        mm(PS[b4][:, 0:128], hT[:, k, tt_ * 128:(tt_ + 1) * 128], wgb[2 + gi][:, k, :], k == 0, k == 7,
                           [bf(f"wu{gi}"), bf(f"hT{tt_ // 4}")], [bf(f"ps{b4}")])
                    V(lambda e, tt_=tt_, t=t, b4=b4: e.tensor_copy(out=Vd[:, tt_, t * 128:(t + 1) * 128], in_=PS[b4][:, 0:128]),
                      [bf(f"ps{b4}")], [bf(f"Vd{tt_}")])
        P.barrier()

    def diff_block(l, blk, hook=None):
        j = l - 2
        zb_ = ZB()
        zhi_ = [ZHI(m_) for m_ in range(6)]
        sf_ = sfx()
        TF = TMP
        r0 = TF[:, 0:256]
        r1 = TF[:, 256:512]
        t0 = TF[:, 512:768]
        t1 = TF[:, 768:1024]
        eT = [TF[:, 1024:1152].bitcast(BF16), TF[:, 1152:1280].bitcast(BF16)]
        nkt = 2 * blk + 2

        def loops(h, c):
            hp = h % 2
            idx = c * 6 + h
            zt = idx // 2
            po = (idx % 2) * 64
            ob, db = 4 + hp, 6 + hp
            pendq = []
            Es = [sgt[0], sgt[1], TF[:, 1024:1280].bitcast(BF16)]
            Eb = [bf("sgt0"), bf("sgt1"), bf("e3")]

            def pv(unit, ei):
                for ik, kt in enumerate(unit):
                    n0 = max(0, kt - 2 * blk) * 128
                    N = BLK - n0
                    rhs = Es[ei][:, ik * 256:ik * 256 + N]
                    mm(PS[ob][:, c * 256 + n0:c * 256 + BLK], Vd[:, kt, h * 128:(h + 1) * 128], rhs, kt == 0, kt == nkt - 1,
                       [bf(f"Vd{kt}"), Eb[ei]], [bf(f"ps{ob}")])
                    mm(PS[db][:, c * 256 + n0:c * 256 + BLK], ones_bf[:], rhs, kt == 0, kt == nkt - 1,
                       [bf("consts"), Eb[ei]], [bf(f"ps{db}")])

            units = [(2 * p_, 2 * p_ + 1) for p_ in range(blk)] + [(2 * blk, 2 * blk + 1)]
            zq = zb_[:, zt, :] if po == 0 else zhi_[zt]
            for unit in units:
                diag = unit[0] >= 2 * blk
                b0 = psb(0, 2)
                for ik, kt in enumerate(unit):
                    n0 = max(0, kt - 2 * blk) * 128
                    N = BLK - n0
                    mm(PS[b0][:, ik * 256:ik * 256 + N], KdT[:, zt, kt * 128:(kt + 1) * 128], zq[:, n0:BLK], True, True,
                       [bf(f"KdT{kt // 2}"), bf(f"z{zt}" + sf_)], [bf(f"ps{b0}")])
                W = 384 if diag else 512
                ei = rr("eT3", 3)
                act(Es[ei][:, 0:W], PS[b0][:, 0:W], AF.Exp, [bf(f"ps{b0}")], [Eb[ei]], scale=0.125)
                if diag:
                    P.op("gpsimd", lambda e, ei=ei: e.memset(Es[ei][64:128, 0:64], 0.0), writes=[Eb[ei]])
                    P.op("gpsimd", lambda e, ei=ei: e.memset(Es[ei][64:128, 256:320], 0.0), writes=[Eb[ei]])
                pendq.append((unit, ei))
                if len(pendq) > 2:
                    pv(*pendq.pop(0))
            while pendq:
                pv(*pendq.pop(0))

        st = {}

        def postA(h):
            hp = h % 2
            ob, db = 4 + hp, 6 + hp
            V(lambda e: e.reciprocal(out=r0, in_=PS[db][:, 0:256]), [bf(f"ps{db}")], [bf("r0")])
            V(lambda e: e.reciprocal(out=r1, in_=PS[db][:, 256:512]), [bf(f"ps{db}")], [bf("r1")])
            tt(t0, PS[ob][:, 0:256], r0, ALU.mult, [bf(f"ps{ob}"), bf("r0")], [bf("t0")])
            tt(t1, PS[ob][:, 256:512], r1, ALU.mult, [bf(f"ps{ob}"), bf("r1")], [bf("t1")])
            stt(t0, t1, small[:, j:j + 1], t0, ALU.mult, ALU.add, [bf("t0"), bf("t1"), bf("small")], [bf("t0")])

        def postM(h):
            i = rr("sq", 2)
            act(sq[i][:], t0, AF.Square, [bf("t0")], [bf(f"sq{i}")])
            b2 = psb(2, 4)
            mm(PS[b2][:, 0:BLK], ones_bf[:], sq[i][:], True, True, [bf(f"sq{i}"), bf("consts")], [bf(f"ps{b2}")])
            st["b2"] = b2

        def postB(h):
            b2 = st["b2"]
            r, rb = rstd_from(PS[b2][:, 0:BLK], BLK, 1.0 / 128, [bf(f"ps{b2}")])
            stt(cat_blk[:, h, :], t0, small[:, 2 + j:3 + j], r, ALU.mult, ALU.mult, [bf("t0"), rb, bf("small")], [bf(f"cat{h}")])

        loops(0, 0)
        loops(0, 1)
        for h in range(6):
            postA(h)
            if h < 5:
                loops(h + 1, 0)
            postM(h)
            if h < 5:
                loops(h + 1, 1)
            postB(h)
            if h == 2 and hook is not None:
                hook()

    SPF = SPARE.bitcast(F32)
    rho = TMP[:, 1200:1224]
    cs4 = TMP[:, 1224:1248]
    sn4 = TMP[:, 1248:1272]
    Sc = small[:, 16:64].rearrange("p (a r) -> p a r", r=2)

    def cmul(dr, di, ar, ai, br, bi, t1, t2, R, W):
        tt(t1, ar, br, ALU.mult, R, [bf("s5t")])
        tt(t2, ai, bi, ALU.mult, R, [bf("s5t")])
        tt(dr, t1, t2, ALU.subtract, [bf("s5t")], W)
        tt(t1, ar, bi, ALU.mult, R, [bf("s5t")])
        tt(t2, ai, br, ALU.mult, R, [bf("s5t")])
        tt(di, t1, t2, ALU.add, [bf("s5t")], W)

    def double_angle(c, s, t1, t2, n):
        for _ in range(n):
            tt(t1, c, c, ALU.mult, [bf("s5p")], [bf("s5t")])
            tt(t2, s, s, ALU.mult, [bf("s5p")], [bf("s5t")])
            stt(s, c, 2.0, s, ALU.mult, ALU.mult, [bf("s5p")], [bf("s5p")])
            tt(c, t1, t2, ALU.subtract, [bf("s5t")], [bf("s5p")])

    def s5_prep(l):
        i = l
        pb = [bf("s5p")]
        tb_ = [bf("s5t")]
        dtF = TMP[:, 1300:1306]
        dt64 = TMP[:, 1306:1312]
        P.dma(lambda e: e.dma_start(out=dtF, in_=s5dtF_d[i]), "s5ld", writes=pb)
        act(dtF, dtF, AF.Exp, pb, pb)
        ts(dt64, dtF, 1.0 / 64, None, ALU.mult, ALU.bypass, pb, pb)
        MIXF = MIX[:, 6144:24576].bitcast(F32)
        sl = [MIXF[:, 768 * q:768 * (q + 1)].rearrange("p (t n) -> p t n", t=6) for q in range(12)]
        A, I, mag, c, s, t1, t2, nr, cr, ci, Wr, Wi = sl
        rden, Br, Bi, Wr2, Wi2 = mag, A, I, mag, nr
        dtb = dtF.unsqueeze(2).to_broadcast([128, 6, 128])
        dt64b = dt64.unsqueeze(2).to_broadcast([128, 6, 128])
        P.dma(lambda e: e.dma_start(out=A, in_=s5F_d[i][:, :, 0, :]), "s5ld", writes=pb)
        P.dma(lambda e: e.dma_start(out=I, in_=s5F_d[i][:, :, 1, :]), "s5ld", writes=pb)
        tt(t1, A, dtb, ALU.mult, pb, tb_)
        act(mag, t1, AF.Exp, tb_, pb)
        tt(t1, I, dt64b, ALU.mult, pb, tb_)
        act(s, t1, AF.Sin, tb_, pb)
        act(c, t1, AF.Sin, tb_, pb, bias=epsT[:, 1:2])
        double_angle(c, s, t1, t2, 6)
        tt(c, mag, c, ALU.mult, pb, pb)
        tt(s, mag, s, ALU.mult, pb, pb)
        tt(t1, A, A, ALU.mult, pb, tb_)
        tt(t2, I, I, ALU.mult, pb, tb_)
        tt(t1, t1, t2, ALU.add, tb_, tb_)
        V(lambda e: e.reciprocal(out=rden, in_=t1), tb_, pb)
        ts(nr, c, -1.0, None, ALU.add, ALU.bypass, pb, pb)
        tt(t1, nr, A, ALU.mult, pb, tb_)
        tt(t2, s, I, ALU.mult, pb, tb_)
        tt(t1, t1, t2, ALU.add, tb_, tb_)
        tt(cr, t1, rden, ALU.mult, tb_ + pb, pb)
        tt(t1, s, A, ALU.mult, pb, tb_)
        tt(t2, nr, I, ALU.mult, pb, tb_)
        tt(t1, t1, t2, ALU.subtract, tb_, tb_)
        tt(ci, t1, rden, ALU.mult, tb_ + pb, pb)
        P.dma(lambda e: e.dma_start(out=Br, in_=s5F_d[i][:, :, 2, :]), "s5ld", reads=tb_, writes=pb)
        P.dma(lambda e: e.dma_start(out=Bi, in_=s5F_d[i][:, :, 3, :]), "s5ld", reads=tb_, writes=pb)
        cmul(Wr, Wi, cr, ci, Br, Bi, t1, t2, pb, pb)
        for k in range(4):
            V(lambda e, k=k, Wr=Wr: e.tensor_copy(out=WB[:, :, k, 0, :], in_=Wr), pb, [bf("WB")])
            V(lambda e, k=k, Wi=Wi: e.tensor_copy(out=WB[:, :, k, 1, :], in_=Wi), pb, [bf("WB")])
            if k < 3:
                cmul(Wr2, Wi2, c, s, Wr, Wi, t1, t2, pb, pb)
                Wr, Wr2 = Wr2, Wr
                Wi, Wi2 = Wi2, Wi
        V(lambda e: e.memset(small[:, 12:13], 0.0), pb + tb_, [bf("CA"), bf("Rtab"), bf("wglu"), bf("small12")])
        inM = TMP[:, 1400:1472].rearrange("p (a n) -> p a n", a=3)
        P.dma(lambda e: e.dma_start(out=inM, in_=s5M_d[i]), "s5ld", writes=pb)
        Am, Im, dtm = inM[:, 0, :], inM[:, 1, :], inM[:, 2, :]
        mt = [TMP[:, 1480 + 24 * q:1504 + 24 * q] for q in range(16)]
        xm, cm, sm, m1, m2, magm = mt[0:6]
        pr = [None, mt[6], mt[8], mt[10], mt[12]]
        pi_ = [None, mt[7], mt[9], mt[11], mt[13]]
        act(dtm, dtm, AF.Exp, pb, pb)
        tt(xm, Am, dtm, ALU.mult, pb, pb)
        act(rho, xm, AF.Exp, pb, pb, scale=4.0)
        act(magm, xm, AF.Exp, pb, pb)
        tt(xm, Im, dtm, ALU.mult, pb, pb)
        act(sm, xm, AF.Sin, pb, pb, scale=1.0 / 64)
        act(cm, xm, AF.Sin, pb, pb, scale=1.0 / 64, bias=epsT[:, 1:2])
        double_angle(cm, sm, m1, m2, 6)
        tt(pr[1], magm, cm, ALU.mult, pb, pb)
        tt(pi_[1], magm, sm, ALU.mult, pb, pb)
        for e_ in range(2, 5):
            cmul(pr[e_], pi_[e_], pr[e_ - 1], pi_[e_ - 1], pr[1], pi_[1], m1, m2, pb, pb)
        double_angle(cm, sm, m1, m2, 2)
        V(lambda e: e.tensor_copy(out=cs4, in_=cm), pb, pb)
        V(lambda e: e.tensor_copy(out=sn4, in_=sm), pb, pb)
        Cin = SPF[:, 0:1536].rearrange("p (r a n) -> p r a n", r=2, a=24)
        P.dma(lambda e: e.dma_start(out=Cin, in_=s5C_d[i]), "s5ld", writes=pb + tb_ + [bf("memT"), bf("mem_n")])
        Cr, Ci = Cin[:, 0], Cin[:, 1]
        c1 = SPF[:, 1536:2304].rearrange("p (a n) -> p a n", a=24)
        c2 = SPF[:, 2304:3072].rearrange("p (a n) -> p a n", a=24)
        V(lambda e: e.tensor_copy(out=CA[:, :, 0, 0, :], in_=Cr), pb, [bf("CA")])
        ts(CA[:, :, 0, 1, :], Ci, -1.0, None, ALU.mult, ALU.bypass, pb, [bf("CA")])
        for e_ in range(1, 5):
            prb = pr[e_].unsqueeze(2).to_broadcast([128, 24, 32])
            pib = pi_[e_].unsqueeze(2).to_broadcast([128, 24, 32])
            tt(c1, Cr, prb, ALU.mult, pb, tb_)
            tt(c2, Ci, pib, ALU.mult, pb, tb_)
            tt(CA[:, :, e_, 0, :], c1, c2, ALU.subtract, tb_, [bf("CA")])
            tt(c1, Cr, pib, ALU.mult, pb, tb_)
            tt(c2, Ci, prb, ALU.mult, pb, tb_)
            tt(c1, c1, c2, ALU.add, tb_, tb_)
            ts(CA[:, :, e_, 1, :], c1, -1.0, None, ALU.mult, ALU.bypass, tb_, [bf("CA")])
        Rr, Ri = Rtab[:, 0], Rtab[:, 1]
        V(lambda e: e.memset(Rr[:, :, 0:1], 1.0), writes=[bf("Rtab")])
        V(lambda e: e.memset(Ri[:, :, 0:1], 0.0), writes=[bf("Rtab")])
        qr, qi = cm, sm
        n = 1
        while n < 64:
            qrb = qr.unsqueeze(2).to_broadcast([128, 24, n])
            qib = qi.unsqueeze(2).to_broadcast([128, 24, n])
            a1 = c1[:, :, 0:n]
            a2 = c2[:, :, 0:n]
            tt(a1, Rr[:, :, 0:n], qrb, ALU.mult, pb + [bf("Rtab")], tb_)
            tt(a2, Ri[:, :, 0:n], qib, ALU.mult, pb + [bf("Rtab")], tb_)
            tt(Rr[:, :, n:2 * n], a1, a2, ALU.subtract, tb_, [bf("Rtab")])
            tt(a1, Rr[:, :, 0:n], qib, ALU.mult, pb + [bf("Rtab")], tb_)
            tt(a2, Ri[:, :, 0:n], qrb, ALU.mult, pb + [bf("Rtab")], tb_)
            tt(Ri[:, :, n:2 * n], a1, a2, ALU.add, tb_, [bf("Rtab")])
            double_angle(qr, qi, m1, m2, 1)
            n *= 2
        V(lambda e: e.memset(small[:, 16:64], 0.0), writes=[bf("Sc")])
        V(lambda e: e.memset(TMP[:, 1992:2312], 0.0), writes=[bf("Z3")])

        gsrc = wglu_d[i].rearrange("(k p) m -> p k m", p=128)
        for m in range(6):
            wload(gsrc[:, :, m * 128:(m + 1) * 128], wglu_bf[:, :, m * 128:(m + 1) * 128], bf("wglu"))
        P.op("gpsimd", lambda e: e.memset(sgt[0][:], 0.0), writes=[bf("sgt0")])
        P.op("gpsimd", lambda e: e.memset(sgt[1][:], 0.0), writes=[bf("sgt1")])
        P.op("gpsimd", lambda e: e.memset(stg[0][:], 0.0), writes=[bf("stg0")])
        P.op("gpsimd", lambda e: e.memset(wgb[1][:], 0.0), writes=[bf("wg1")])

    def s5_block(l, blk):
        i = l
        L2 = [SPF[:, 1536 + 256 * q:1792 + 256 * q].rearrange("p (a c) -> p a c", a=4) for q in range(6)]
        t1, t2, Xr, Xi, Vr, Vi = L2
        Sf = TMP[:, 0:520].rearrange("p (r a c) -> p r a c", r=2, a=4)
        Spb = TMP[:, 520:776].bitcast(BF16).rearrange("p (a r c) -> p a r c", a=4, r=2)
        yv = TMP[:, 776:1032]
        sgS = TMP[:, 1032:1160].bitcast(BF16)
        ini = TMP[:, 1160:1176].rearrange("p (a q) -> p a q", a=4)
        hS = h_blk
        stg0b = stg[0][:].bitcast(BF16)
        wflat = [w_[:].rearrange("p k n -> p (k n)") for w_ in wgb]
        um_set = [[sgt[0][:, 0:256], sgt[0][:, 256:512], sgt[1][:, 0:256], sgt[1][:, 256:512]],
                  [stg0b[:, q * 256:(q + 1) * 256] for q in range(4)]]
        um_buf = [[bf("sgt0"), bf("sgt0"), bf("sgt1"), bf("sgt1")], [bf("stg0")] * 4]
        Dsb_set = [SPF[:, 1024:1536], wflat[0].bitcast(F32)]
        Dsb_buf = [bf("Dsb"), bf("wg0")]
        sh_set = [[SPARE[:, q * 256:(q + 1) * 256] for q in range(8)],
                  [wflat[2][:, q * 256:(q + 1) * 256] for q in range(4)] + [wflat[3][:, q * 256:(q + 1) * 256] for q in range(4)]]
        sh_buf = [[bf(f"sh{q}") for q in range(8)], [bf("wu0")] * 4 + [bf("wu1")] * 4]
        Z3_set = [TMP[:, 1992:2312].bitcast(BF16).rearrange("p (e r n) -> p e r n", e=5, r=2),
                  wflat[1][:, 0:640].rearrange("p (e r n) -> p e r n", e=5, r=2)]
        Z3_buf = [bf("Z3"), bf("wg1")]
        bDs = {}

        def stageA(tau):
            pb_ = tau % 2
            um, ub = um_set[pb_], um_buf[pb_]
            um4 = [u_.rearrange("p (c j) -> p c j", j=4) for u_ in um]
            zb = bf(f"z{tau}")
            for q in range(3):
                act(um[q][q * 32:(q + 1) * 32, :], z_blk[q * 32:(q + 1) * 32, tau, :], AF.Copy, [zb], [ub[q]])
            act(um[3][64:128, :], z_blk[64:128, tau, :], AF.Copy, [zb], [ub[3]])
            P.op("gpsimd", lambda e: e.memset(um[3][64:96, :], 0.0), writes=[ub[3]])
            P.op("gpsimd", lambda e: e.tensor_copy(out=Z3_set[pb_][:, :, :, 32:64], in_=CA[:, 4 * tau + 3, :, :, :]), reads=[bf("CA")], writes=[Z3_buf[pb_]])
            bD = psb(2, 4)
            for q in range(4):
                for ri in range(2):
                    slot = q * 2 + ri
                    for t in range(4):
                        mm(PS[bD][:, slot * 64:(slot + 1) * 64], WB[:, tau, 3 - t, ri, :],
                           um4[q][:, :, t], t == 0, t == 3, [bf("WB"), ub[q]], [bf(f"ps{bD}")])
            act(Dsb_set[pb_], PS[bD][:, 0:512], AF.Copy, [bf(f"ps{bD}")], [Dsb_buf[pb_]])
            for q in range(4):
                for ri in range(2):
                    slot = q * 2 + ri
                    b0 = slot % 2
                    half = (slot // 2) % 2
                    pv = PS[b0][:, half * 256:(half + 1) * 256]
                    pv4 = pv.rearrange("p (c j) -> p c j", j=4)
                    pbuf = bf(f"ps{b0}")
                    for k in range(4):
                        mm(pv4[:, :, k:4], WB[:, tau, k, ri, :], um4[q][:, :, 0:4 - k],
                           k == 0, k == 3, [bf("WB"), ub[q]], [pbuf])
                    act(sh_set[pb_][slot], pv, AF.Copy, [pbuf], [sh_buf[pb_][slot]])

        def stageB(tau):
            pb_ = tau % 2
            Dsb = Dsb_set[pb_].rearrange("p (q r c) -> p q r c", q=4, r=2)
            Dr, Di = Dsb[:, :, 0, :], Dsb[:, :, 1, :]
            Rr, Ri = Rtab[:, 0, 4 * tau:4 * tau + 4, :], Rtab[:, 1, 4 * tau:4 * tau + 4, :]
            db_ = Dsb_buf[pb_]
            Bt1, Bt2, BXr, BXi, BVr, BVi = bf("L2t1"), bf("L2t2"), bf("L2Xr"), bf("L2Xi"), bf("L2Vr"), bf("L2Vi")
            RT = bf("Rtab")
            tt(t1, Rr, Dr, ALU.mult, [RT, db_], [Bt1])
            tt(t2, Ri, Di, ALU.mult, [RT, db_], [Bt2])
            tt(Vr, Rr, Di, ALU.mult, [RT, db_], [BVr])
            tt(Vi, Ri, Dr, ALU.mult, [RT, db_], [BVi])
            scr, sci = Sc[:, 4 * tau:4 * tau + 4, 0], Sc[:, 4 * tau:4 * tau + 4, 1]
            c4, s4 = cs4[:, 4 * tau:4 * tau + 4], sn4[:, 4 * tau:4 * tau + 4]
            ir, ii, ta, tb2 = ini[:, 0, :], ini[:, 1, :], ini[:, 2, :], ini[:, 3, :]
            tc, td = TMP[:, 1176:1180], TMP[:, 1180:1184]
            sp_ = [bf("Sc"), bf("s5p")]
            tt(ta, c4, scr, ALU.mult, sp_, [bf("ita")])
            tt(tb2, s4, sci, ALU.mult, sp_, [bf("itb")])
            tt(tc, s4, scr, ALU.mult, sp_, [bf("itc")])
            tt(td, c4, sci, ALU.mult, sp_, [bf("itd")])
            tt(Xr, t1, t2, ALU.add, [Bt1, Bt2], [BXr])
            tt(Xi, Vr, Vi, ALU.subtract, [BVr, BVi], [BXi])
            tt(ir, ta, tb2, ALU.subtract, [bf("ita"), bf("itb")], [bf("iir")])
            tt(ii, tc, td, ALU.add, [bf("itc"), bf("itd")], [bf("iii")])
            for q in range(4):
                pr_ = 4 * tau + q
                rbc = rho[:, pr_:pr_ + 1].to_broadcast([128, 64])
                V(lambda e, q=q, rbc=rbc: e.tensor_tensor_scan(out=Vr[:, q, :], data0=rbc, data1=Xr[:, q, :], initial=ir[:, q:q + 1],
                                                                op0=ALU.mult, op1=ALU.add), [BXr, bf("iir"), bf("s5p")], [BVr])
            for q in range(4):
                pr_ = 4 * tau + q
                rbc = rho[:, pr_:pr_ + 1].to_broadcast([128, 64])
                V(lambda e, q=q, rbc=rbc: e.tensor_tensor_scan(out=Vi[:, q, :], data0=rbc, data1=Xi[:, q, :], initial=ii[:, q:q + 1],
                                                                op0=ALU.mult, op1=ALU.add), [BXi, bf("iii"), bf("s5p")], [BVi])
            V(lambda e: e.tensor_copy(out=Sf[:, 0, :, 0], in_=scr), [bf("Sc")], [bf("Sf0")])
            V(lambda e: e.tensor_copy(out=Sf[:, 1, :, 0], in_=sci), [bf("Sc")], [bf("Sf1")])
            tt(t1, Rr, Vr, ALU.mult, [RT, BVr], [Bt1])
            tt(t2, Ri, Vi, ALU.mult, [RT, BVi], [Bt2])
            tt(Xr, Rr, Vi, ALU.mult, [RT, BVi], [BXr])
            tt(Xi, Ri, Vr, ALU.mult, [RT, BVr], [BXi])
            tt(Sf[:, 0, :, 1:65], t1, t2, ALU.subtract, [Bt1, Bt2], [bf("Sf0")])
            tt(Sf[:, 1, :, 1:65], Xr, Xi, ALU.add, [BXr, BXi], [bf("Sf1")])
            act(Spb[:, :, 0, :], Sf[:, 0, :, 0:64], AF.Copy, [bf("Sf0")], [bf("Spb")])
            act(Spb[:, :, 1, :], Sf[:, 1, :, 0:64], AF.Copy, [bf("Sf1")], [bf("Spb")])
            V(lambda e: e.tensor_copy(out=scr, in_=Sf[:, 0, :, 64]), [bf("Sf0")], [bf("Sc")])
            V(lambda e: e.tensor_copy(out=sci, in_=Sf[:, 1, :, 64]), [bf("Sf1")], [bf("Sc")])

        def stageC(tau):
            pb_ = tau % 2
            sh, shb = sh_set[pb_], sh_buf[pb_]
            Z3v = Z3_set[pb_]
            by = 4 + (tau % 2)
            for q in (3, 2, 0, 1):
                pr_ = 4 * tau + q
                if q == 3:
                    out = PS[by][64:128, 0:256]
                    lw = lambda e_, r_: Z3v[:, e_, r_, :]
                    wb_ = [Z3_buf[pb_]]
                else:
                    out = PS[by][q * 32:(q + 1) * 32, 0:256]
                    lw = lambda e_, r_, pr_=pr_: CA[:, pr_, e_, r_, :]
                    wb_ = [bf("CA")]
                out4 = out.rearrange("p (c j) -> p c j", j=4)
                first = (q != 2)
                mm(out, lw(0, 0), sh[q * 2], first, False, wb_ + [shb[q * 2]], [bf(f"ps{by}")])
                mm(out, lw(0, 1), sh[q * 2 + 1], False, False, wb_ + [shb[q * 2 + 1]], [bf(f"ps{by}")])
                for j in range(4):
                    mm(out4[:, :, j], lw(j + 1, 0), Spb[:, q, 0, :], False, False, wb_ + [bf("Spb")], [bf(f"ps{by}")])
                    mm(out4[:, :, j], lw(j + 1, 1), Spb[:, q, 1, :], False, j == 3, wb_ + [bf("Spb")], [bf(f"ps{by}")])

        def stageC2(tau):
            by = 4 + (tau % 2)
            stt(yv, z_blk[:, tau, :], cs("dF", i * 6 + tau), PS[by][:, 0:256], ALU.mult, ALU.add, [bf(f"z{tau}"), bf(f"ps{by}"), bf("cst")], [bf("yv")])
            act(hS[:, tau, :], yv, AF.Gelu_apprx_tanh, [bf("yv")], [bf("h_blk")])

        stageA(0)
        stageA(1)
        stageB(0)
        stageC(0)
        for tau in range(1, 6):
            if tau < 5:
                stageA(tau + 1)
            stageB(tau)
            stageC2(tau - 1)
            stageC(tau)
        stageC2(5)
        for m in range(6):
            bg = 6 + (m % 2)
            for k in range(6):
                mm(PS[bg][:, 0:256], wglu_bf[:, k, m * 128:(m + 1) * 128], hS[:, k, :], k == 0, k == 5,
                   [bf("wglu"), bf("h_blk")], [bf(f"ps{bg}")])
            act(sgS, PS[bg][:, 0:256], AF.Sigmoid, [bf(f"ps{bg}")], [bf("sgS")])
            tt(cat_blk[:, m, :], hS[:, m, :], sgS, ALU.mult, [bf("h_blk"), bf("sgS")], [bf(f"cat{m}")])

    import os
    STG = int(os.environ.get("KSTAGE", "99"))
    for l in range(n_layers):
        if STG >= 1:
            layer_start(l)
        if l < 2 and STG >= 2:
            s5_prep(l)
        P.barrier()
        if l >= 2:
            for p_ in range(2):
                zs_ = z_sets[p_]
                P.op("gpsimd", lambda e, zs_=zs_: e.memset(zs_[64:128, 0:6, :], 0.0), writes=[bf(f"z{m_}" + ("" if p_ == 0 else "_1")) for m_ in range(6)])
            P.op("gpsimd", lambda e: e.memset(TMP[0:64, 1280:2048], 0.0), writes=[bf(f"z{m_}") for m_ in range(6)])
            P.op("gpsimd", lambda e: e.memset(SPARE[0:64, 4096:5632], 0.0), writes=[bf(f"z{m_}_1") for m_ in range(6)])
            PAR["p"] = 0
            PAR["mbank"] = None
            win_block(l, 0)
            memattn_block(l, 0)
            for blk in range(NBLK):
                def hook(blk=blk):
                    if blk + 1 < NBLK:
                        PAR["p"] = (blk + 1) % 2
                        PAR["mbank"] = (2, 3)
                        win_block(l, blk + 1)
                        memattn_block(l, blk + 1)
                        PAR["p"] = blk % 2
                PAR["p"] = blk % 2
                diff_block(l, blk, hook)
                wout_block(l, blk)
            PAR["p"] = 0
            PAR["mbank"] = None
        else:
            for blk in range(NBLK):
                win_block(l, blk)
                memattn_block(l, blk)
                s5_block(l, blk)
                wout_block(l, blk)
        if STG >= 9:
            ffn(l)
        if l == 1 and n_layers > 2:
            kv_shared()
    P.barrier()
    ysrc = yT_d.rearrange("(k p) n -> p k n", p=128)
    for k in range(8):
        P.dma(lambda e, k=k: e.dma_start(out=ysrc[:, k, :], in_=xT[:, k, :]), "yst", reads=[bf(f"x{k}_{b}") for b in range(8)])
    P.rec["sync"].append(([(P.sems["yst"], P.dma_cnt["yst"])], None, None, 0))
    return P.finish()


_NC_CACHE = {}


def _prep_inputs(inp, n_cores=8):
    f32 = np.float32
    g = lambda k: np.asarray(inp[k], dtype=f32)
    cst = np.zeros((128, NCST), f32)

    def put(name, arr):
        o, w = CST_OFF[name]
        assert arr.shape == (128, w), (name, arr.shape)
        cst[:, o:o + w] = arr

    fm = lambda a: a.reshape(a.shape[0], 8, 128).transpose(2, 0, 1).reshape(128, -1)
    put("gmix", fm(g("norm_mix")))
    put("gffn", fm(g("norm_ffn")))
    put("gmem", fm(g("norm_mem")))
    put("gkv", fm(g("kv_norm")[None]))
    p64 = np.arange(128) % 64
    put("mqn", g("mem_q_norm")[:, p64].T)
    put("mkn", g("mem_k_norm")[:, p64].T)
    put("dqn", g("diff_q_norm")[:, p64].T)
    put("dkn", g("diff_k_norm")[p64][:, None])
    put("dsn", g("diff_sub_norm").T)
    lam = np.stack([g("diff_lambda_q1"), g("diff_lambda_k1"), g("diff_lambda_q2"), g("diff_lambda_k2")])
    put("lam", np.broadcast_to(lam.reshape(1, 512), (128, 512)))
    put("dF", g("ssm_d").reshape(2, 6, 128).transpose(2, 0, 1).reshape(128, 12))
    G, N, Pp = 48, 64, 16
    a_re, a_im, ldt = g("ssm_a_re"), g("ssm_a_im"), g("ssm_log_dt")
    b_re, b_im, c_re, c_im = g("ssm_b_re"), g("ssm_b_im"), g("ssm_c_re"), g("ssm_c_im")
    s5F = np.zeros((2, 128, 6, 4, 128), f32)
    s5dtF = np.zeros((2, 128, 6), f32)
    s5M = np.zeros((2, 128, 3, 24), f32)
    s5C = np.zeros((2, 128, 2, 24, 32), f32)
    for i in range(2):
        for gg in range(G):
            tau, gl = gg // 8, gg % 8
            rows = slice(gl * 16, gl * 16 + 16)
            half = gg % 2
            cols = slice(half * 64, half * 64 + 64)
            for hh in range(2):
                s5F[i, rows, tau, 0, hh * 64:(hh + 1) * 64] = a_re[i, gg][None, :]
                s5F[i, rows, tau, 1, hh * 64:(hh + 1) * 64] = a_im[i, gg][None, :]
            s5F[i, rows, tau, 2, cols] = b_re[i, gg].T
            s5F[i, rows, tau, 3, cols] = b_im[i, gg].T
            s5dtF[i, rows, tau] = ldt[i, gg]
            pr = gg // 2
            mrows = slice(half * 64, half * 64 + 64)
            s5M[i, mrows, 0, pr] = a_re[i, gg]
            s5M[i, mrows, 1, pr] = a_im[i, gg]
            s5M[i, mrows, 2, pr] = ldt[i, gg]
            s5C[i, mrows, 0, pr, half * 16:(half + 1) * 16] = c_re[i, gg].T
            s5C[i, mrows, 1, pr, half * 16:(half + 1) * 16] = c_im[i, gg].T
    shared = {"cst": cst, "s5F": s5F, "s5dtF": s5dtF, "s5M": s5M, "s5C": s5C}
    for k in ["w_in", "w_out", "mem_wk", "mem_wv", "ffn_w_gate", "ffn_w_up", "ffn_w_down", "ssm_w_glu", "diff_wk", "diff_wv"]:
        shared[k] = np.ascontiguousarray(g(k))
    x, mem = g("x"), g("mem")
    maps = []
    for b in range(n_cores):
        d = dict(shared)
        d["xT"] = np.ascontiguousarray(x[b].T)
        d["memT"] = np.ascontiguousarray(mem[b].T)
        maps.append(d)
    return maps


def kernel(**inputs):
    n_layers = int(inputs.pop("_n_layers", NL))
    maps = _prep_inputs(inputs)
    nc = build(n_layers)
    res = run_bass_kernel_spmd(nc, maps, core_ids=list(range(8)))
    out = np.stack([np.ascontiguousarray(r["yT"].T) for r in res.results], axis=0)
    return out.astype(np.float32)
```

```python
import contextlib
import numpy as np
import concourse.bass as bass
import concourse.mybir as mybir
from concourse.bass_utils import run_bass_kernel_spmd

F32 = mybir.dt.float32
BF16 = mybir.dt.bfloat16
AF = mybir.ActivationFunctionType
ALU = mybir.AluOpType
AX = mybir.AxisListType

SEM_ROT = 20000


class Buf:
    __slots__ = ("name", "w", "r")

    def __init__(self, name=""):
        self.name = name
        self.w = None
        self.r = []


class Prog:
    ENGS = ("tensor", "vector", "scalar", "gpsimd", "sync")

    def __init__(self):
        self.nc = bass.Bass("TRN2", target_bir_lowering=False)
        self.stack = contextlib.ExitStack()
        self.rec = {e: [] for e in self.ENGS}
        self.sems = {}
        self.cur = {}
        self.epoch = {e: 0 for e in self.ENGS}
        self.known = {e: {} for e in self.ENGS}
        self.dma_cnt = {}
        self.relax = False
        for e in self.ENGS:
            self._new_epoch(e)

    def sem(self, key):
        if key not in self.sems:
            self.sems[key] = self.stack.enter_context(self.nc.semaphore(str(key)))
        return self.sems[key]

    def _new_epoch(self, e):
        self.epoch[e] += 1
        key = (e, self.epoch[e])
        self.sem(key)
        self.cur[e] = [key, 0]

    def sbuf(self, name, shape, dt):
        return self.stack.enter_context(self.nc.sbuf_tensor(name, list(shape), dt))

    def psum(self, name, shape, dt=F32):
        return self.stack.enter_context(self.nc.psum_tensor(name, list(shape), dt))

    def dram(self, name, shape, dt, kind):
        return self.nc.dram_tensor(name, list(shape), dt, kind=kind).ap()

    def _deps(self, eng, reads, writes):
        toks = []
        for b in reads:
            if b.w is not None:
                toks.append(b.w)
        for b in writes:
            if b.w is not None:
                toks.append(b.w)
            toks.extend(b.r)
        need = {}
        for (k, v) in toks:
            if eng == "tensor" and k[0] == "tensor":
                continue
            if self.known[eng].get(k, 0) >= v:
                continue
            if need.get(k, 0) < v:
                need[k] = v
        for k, v in need.items():
            self.known[eng][k] = v
        return [(self.sems[k], v) for k, v in need.items()]

    def op(self, eng, fn, reads=(), writes=()):
        waits = self._deps(eng, reads, writes)
        cur = self.cur[eng]
        if cur[1] >= SEM_ROT:
            self._new_epoch(eng)
            cur = self.cur[eng]
        cur[1] += 1
        tok = (cur[0], cur[1])
        self.rec[eng].append((waits, fn, self.sems[cur[0]], 1))
        for b in writes:
            b.w = tok
            b.r = []
        for b in reads:
            if b not in writes:
                b.r.append(tok)
        return tok

    def dma(self, fn, semkey, reads=(), writes=(), eng="sync"):
        waits = self._deps(eng, reads, writes)
        self.sem(semkey)
        self.dma_cnt[semkey] = self.dma_cnt.get(semkey, 0) + 16
        tok = (semkey, self.dma_cnt[semkey])
        self.rec[eng].append((waits, fn, self.sems[semkey], 16))
        for b in writes:
            b.w = tok
            b.r = []
        for b in reads:
            if b not in writes:
                b.r.append(tok)
        return tok

    def wait_all(self, eng, bufs):
        waits = self._deps(eng, bufs, ())
        self.rec[eng].append((waits, None, None, 0))


    def barrier(self):
        toks = []
        for e in self.ENGS:
            for ep in range(1, self.epoch[e] + 1):
                k = (e, ep)
                v = self.cur[e][1] if ep == self.epoch[e] else None
                if v is None:
                    continue
                if v > 0:
                    toks.append((k, v))
        for k, v in self.dma_cnt.items():
            toks.append((k, v))
        for e in self.ENGS:
            waits = []
            for (k, v) in toks:
                if k[0] == e and isinstance(k, tuple) and k[0] in self.ENGS and e != "sync":
                    pass
                if self.known[e].get(k, 0) >= v:
                    continue
                self.known[e][k] = v
                waits.append((self.sems[k], v))
            if waits:
                self.rec[e].append((waits, None, None, 0))

    def finish(self):
        nc = self.nc
        rec = self.rec

        def replay(e, name):
            for (waits, fn, sem, inc) in rec[name]:
                for (s, v) in waits:
                    e.wait_ge(s, v)
                if fn is not None:
                    ins = fn(e)
                    ins.then_inc(sem, inc)

        with nc.Block() as block:
            @block.tensor
            def _(e):
                replay(e, "tensor")

            @block.vector
            def _(e):
                replay(e, "vector")

            @block.scalar
            def _(e):
                replay(e, "scalar")

            @block.gpsimd
            def _(e):
                replay(e, "gpsimd")

            @block.sync
            def _(e):
                replay(e, "sync")
        self.stack.close()
        return nc

import math

EPS = 1e-6
NL = 4
BLK = 256
NBLK = 8
DFF = 2816
NF = 22
FC = 4


def _cst_layout():
    off = {}
    n = 0
    for name, w in [("gmix", 32), ("gffn", 32), ("gmem", 32), ("gkv", 8), ("mqn", 4), ("mkn", 4),
                    ("dqn", 2), ("dkn", 1), ("dsn", 2), ("lam", 512), ("dF", 12)]:
        off[name] = (n, w)
        n += w
    return off, n


CST_OFF, NCST = _cst_layout()


def build(n_layers=NL):
    P = Prog()
    nc = P.nc
    dI = lambda name, shape: P.dram(name, shape, F32, "ExternalInput")
    xT_d = dI("xT", [1024, 2048])
    memT_d = dI("memT", [1024, 256])
    cst_d = dI("cst", [128, NCST])
    w_in_d = dI("w_in", [4, 1024, 1024])
    w_out_d = dI("w_out", [4, 1024, 1024])
    wk_d = dI("mem_wk", [4, 1024, 256])
    wv_d = dI("mem_wv", [4, 1024, 256])
    wg_d = dI("ffn_w_gate", [4, 1024, DFF])
    wu_d = dI("ffn_w_up", [4, 1024, DFF])
    wd_d = dI("ffn_w_down", [4, DFF, 1024])
    wglu_d = dI("ssm_w_glu", [2, 768, 768])
    dwk_d = dI("diff_wk", [1024, 768])
    dwv_d = dI("diff_wv", [1024, 768])
    s5F_d = dI("s5F", [2, 128, 6, 4, 128])
    s5dtF_d = dI("s5dtF", [2, 128, 6])
    s5M_d = dI("s5M", [2, 128, 3, 24])
    s5C_d = dI("s5C", [2, 128, 2, 24, 32])
    yT_d = P.dram("yT", [1024, 2048], F32, "ExternalOutput")

    xT = P.sbuf("xT_sb", [128, 8, 2048], F32)
    MIX = P.sbuf("MIX", [128, 24576], BF16)
    ARENA = P.sbuf("ARENA", [128, 28672], BF16)
    stg = [P.sbuf(f"stg{i}", [128, 1024], F32) for i in range(2)]
    wgb = [P.sbuf(f"wgb{i}", [128, 8, 128], BF16) for i in range(4)]
    cst = P.sbuf("cst_sb", [128, NCST], F32)
    ones_bf = P.sbuf("ones_bf", [128, 128], BF16)
    blk64 = P.sbuf("blk64", [128, 128], BF16)
    epsT = P.sbuf("epsT", [128, 2], F32)
    KmT = P.sbuf("KmT", [128, 2, 256], BF16)
    Vm = P.sbuf("Vm", [128, 2, 256], BF16)
    sq = [P.sbuf(f"sq{i}", [128, 256], BF16) for i in range(2)]
    rstd = [P.sbuf(f"rstd{i}", [128, 256], F32) for i in range(2)]
    sgt = [P.sbuf(f"sgt{i}", [128, 512], BF16) for i in range(2)]
    eM = [P.sbuf(f"eM{i}", [128, 256], BF16) for i in range(2)]
    rrm = P.sbuf("rrm", [128, 256], F32)
    rstdN = P.sbuf("rstdN", [128, 256], F32)
    TMP = P.sbuf("TMP", [128, 2320], F32)
    small = P.sbuf("small", [128, 64], F32)
    PS = [P.psum(f"ps{i}", [128, 512]) for i in range(8)]

    B = {}

    def bf(name):
        if name not in B:
            B[name] = Buf(name)
        return B[name]

    def cs(name, i=0, n=1):
        o, w = CST_OFF[name]
        return cst[:, o + i:o + i + n]

    w_in_bf = ARENA[:, 0:8192].rearrange("p (k m) -> p k m", k=8)
    w_out_bf = ARENA[:, 8192:16384].rearrange("p (k m) -> p k m", k=8)
    h_blk = ARENA[:, 16384:18432].rearrange("p (k n) -> p k n", k=8)
    z_blk = ARENA[:, 18432:20480].rearrange("p (k n) -> p k n", k=8)
    cat_blk = ARENA[:, 20480:22528].rearrange("p (k n) -> p k n", k=8)
    SPARE = ARENA[:, 22528:28672]
    hT = ARENA[:, 0:16384].rearrange("p (k n) -> p k n", k=8)
    aT = ARENA[:, 16384:24576].rearrange("p (f n) -> p f n", f=FC)
    wdb = ARENA[:, 24576:28672].rearrange("p (f n) -> p f n", f=FC)
    KdT = MIX[:, 0:12288].rearrange("p (t n) -> p t n", t=6)
    Vd = MIX[:, 12288:24576].rearrange("p (t n) -> p t n", t=16)
    WB = MIX[:, 0:6144].rearrange("p (t k r n) -> p t k r n", t=6, k=4, r=2)
    CA = MIX[:, 6144:13824].rearrange("p (a e r n) -> p a e r n", a=24, e=5, r=2)
    Rtab = MIX[:, 13824:19968].bitcast(F32).rearrange("p (r a c) -> p r a c", r=2, a=24)
    wglu_bf = MIX[:, 19968:24576].rearrange("p (k m) -> p k m", k=6)
    memT_sb = SPARE[:, 0:4096].bitcast(F32).rearrange("p (k n) -> p k n", k=8)
    mem_n = SPARE[:, 4096:6144].rearrange("p (k n) -> p k n", k=8)
    wk_bf = ARENA[:, 16384:18432].rearrange("p (k n) -> p k n", k=8)
    wv_bf = ARENA[:, 18432:20480].rearrange("p (k n) -> p k n", k=8)

    PAR = {"p": 0, "mbank": None}
    h_sets = [h_blk, SPARE[:, 0:2048].rearrange("p (k n) -> p k n", k=8)]
    z_sets = [z_blk, SPARE[:, 2048:4096].rearrange("p (k n) -> p k n", k=8)]
    catm_sets = [cat_blk[:, 6:8, :], SPARE[:, 5632:6144].rearrange("p (k n) -> p k n", k=2)]

    def HB():
        return h_sets[PAR["p"]]

    def ZB():
        return z_sets[PAR["p"]]

    def CATM():
        return catm_sets[PAR["p"]]

    def sfx():
        return "" if PAR["p"] == 0 else "_1"

    def V(fn, reads=(), writes=(), eng="vector"):
        return P.op(eng, fn, reads=reads, writes=writes)

    def tt(out, a, b, op, reads, writes, eng="vector"):
        return P.op(eng, lambda e: e.tensor_tensor(out=out, in0=a, in1=b, op=op), reads=reads, writes=writes)

    def stt(out, a, s, b, op0, op1, reads, writes):
        return P.op("vector", lambda e: e.scalar_tensor_tensor(out=out, in0=a, scalar=s, in1=b, op0=op0, op1=op1),
                    reads=reads, writes=writes)

    def ts(out, a, s1, s2, op0, op1, reads, writes, eng="vector"):
        return P.op(eng, lambda e: e.tensor_scalar(out=out, in0=a, scalar1=s1, scalar2=s2, op0=op0, op1=op1),
                    reads=reads, writes=writes)

    def act(out, a, func, reads, writes, scale=1.0, bias=None):
        if bias is None:
            return P.op("scalar", lambda e: e.activation(out=out, in_=a, func=func, scale=scale), reads=reads, writes=writes)
        return P.op("scalar", lambda e: e.activation(out=out, in_=a, func=func, scale=scale, bias=bias), reads=reads, writes=writes)

    def mm(out, lhsT, rhs, start, stop, reads, writes):
        return P.op("tensor", lambda e: e.matmul(out, lhsT=lhsT, rhs=rhs, start=start, stop=stop), reads=reads, writes=writes)

    stg_i = [0]

    cast_rot = [0]

    def wload(src, dst, dstbuf, shape3=None, engs=("gpsimd",)):
        i = stg_i[0] % 2
        stg_i[0] += 1
        n = 1
        for d in src.shape[1:]:
            n *= d
        sv = stg[i][:, 0:n]
        if len(src.shape) == 3:
            sv = sv.rearrange("p (a b) -> p a b", a=src.shape[1])
        P.dma(lambda e: e.dma_start(out=sv, in_=src), ("stg", i), writes=[bf(f"stg{i}")])
        ce = engs[cast_rot[0] % len(engs)]
        cast_rot[0] += 1
        if ce == "scalar":
            act(dst, sv, AF.Copy, [bf(f"stg{i}")], [dstbuf])
        else:
            P.op(ce, lambda e: e.tensor_copy(out=dst, in_=sv), reads=[bf(f"stg{i}")], writes=[dstbuf])

    psn = {}

    def psb(lo, hi):
        k = (lo, hi)
        psn[k] = psn.get(k, -1) + 1
        return lo + psn[k] % (hi - lo)

    rot = {}

    def rr(name, n):
        rot[name] = rot.get(name, -1) + 1
        return rot[name] % n

    def rstd_from(ps_ap, ncols, scale, reads, dst=None):
        if dst is None:
            i = rr("rstd", 2)
            r = rstd[i][:, 0:ncols]
            rb = bf(f"rstd{i}")
        else:
            r, rb = dst
        act(r, ps_ap, AF.Ln, reads, [rb], scale=scale, bias=epsT[:, 0:1])
        act(r, r, AF.Exp, [rb], [rb], scale=-0.5)
        return r, rb

    def group_norm_evac(ps_ap, psbuf, ncols, gain, dst, dstbufs, full, split=None, defer=False):
        i = rr("sq", 2)
        s = sq[i][:, 0:ncols]
        act(s, ps_ap, AF.Square, [psbuf], [bf(f"sq{i}")])
        if defer:
            return lambda: _gne2(i, s, ps_ap, psbuf, ncols, gain, dst, dstbufs, full, split)
        _gne2(i, s, ps_ap, psbuf, ncols, gain, dst, dstbufs, full, split)

    def _gne2(i, s, ps_ap, psbuf, ncols, gain, dst, dstbufs, full, split):
        b2 = psb(2, 4)
        mm(PS[b2][:, 0:ncols], ones_bf[:] if full else blk64[:], s, True, True, [bf(f"sq{i}"), bf("consts")], [bf(f"ps{b2}")])
        r, rb = rstd_from(PS[b2][:, 0:ncols], ncols, 1.0 / (128 if full else 64), [bf(f"ps{b2}")])
        if split is None:
            stt(dst, ps_ap, gain, r, ALU.mult, ALU.mult, [psbuf, rb, bf("cst")], dstbufs)
        else:
            lo_, hi_ = split
            stt(lo_[0:64, :], ps_ap[0:64, :], gain[0:64, :], r[0:64, :], ALU.mult, ALU.mult, [psbuf, rb, bf("cst")], dstbufs)
            stt(hi_[64:128, :], ps_ap[64:128, :], gain[64:128, :], r[64:128, :], ALU.mult, ALU.mult, [psbuf, rb, bf("cst")], dstbufs)

    zhi_sets = [[TMP[:, 1280 + 128 * m_:1408 + 128 * m_].bitcast(BF16) for m_ in range(6)],
                [SPARE[:, 4096 + 256 * m_:4352 + 256 * m_] for m_ in range(6)]]

    def ZHI(m_):
        return zhi_sets[PAR["p"]][m_]

    def xnorm(blk, gname, gl, dst, dstbuf, pre=None, only_p1=False):
        c0 = blk * BLK
        if pre is not None:
            r, rb = pre
            for k in range(8):
                stt(dst[:, k, :], xT[:, k, c0:c0 + BLK], cs(gname, gl * 8 + k), r, ALU.mult, ALU.mult,
                    [bf(f"x{k}_{blk}"), rb, bf("cst")], [dstbuf])
            return None
        b2 = psb(2, 4)
        for k in range(8):
            i = rr("sq", 2)
            act(sq[i][:], xT[:, k, c0:c0 + BLK], AF.Square, [bf(f"x{k}_{blk}")], [bf(f"sq{i}")])
            mm(PS[b2][:, 0:BLK], ones_bf[:], sq[i][:], k == 0, k == 7, [bf(f"sq{i}"), bf("consts")], [bf(f"ps{b2}")])
        if only_p1:
            return rstd_from(PS[b2][:, 0:BLK], BLK, 1.0 / 1024, [bf(f"ps{b2}")], dst=(rstdN[:, 0:BLK], bf("rstdN")))
        r, rb = rstd_from(PS[b2][:, 0:BLK], BLK, 1.0 / 1024, [bf(f"ps{b2}")])
        for k in range(8):
            stt(dst[:, k, :], xT[:, k, c0:c0 + BLK], cs(gname, gl * 8 + k), r, ALU.mult, ALU.mult,
                [bf(f"x{k}_{blk}"), rb, bf("cst")], [dstbuf])

    P.dma(lambda e: e.dma_start(out=cst[:], in_=cst_d[:, :]), "cstld", writes=[bf("cst")])
    V(lambda e: e.memset(ones_bf[:], 1.0), writes=[bf("consts")])
    V(lambda e: e.memset(blk64[:], 0.0), writes=[bf("consts")])
    V(lambda e: e.memset(blk64[0:64, 0:64], 1.0), writes=[bf("consts")])
    V(lambda e: e.memset(blk64[64:128, 64:128], 1.0), writes=[bf("consts")])
    V(lambda e: e.memset(epsT[:, 0:1], EPS), writes=[bf("consts")])
    V(lambda e: e.memset(epsT[:, 1:2], math.pi / 2), writes=[bf("consts")])
    xsrc = xT_d.rearrange("(k p) n -> p k n", p=128)
    for k in range(8):
        for hh in range(2):
            P.dma(lambda e, k=k, hh=hh: e.dma_start(out=xT[:, k, hh * 1024:(hh + 1) * 1024], in_=xsrc[:, k, hh * 1024:(hh + 1) * 1024]),
                  "xld", writes=[bf(f"x{k}_{b}") for b in range(hh * 4, hh * 4 + 4)])

    def lam_prep():
        lamv = cs("lam", 0, 512).rearrange("p (a j d) -> p a j d", a=4, j=2)
        tmp = TMP[:, 0:64]
        for j in range(2):
            layer = 2 + j
            linit = 0.8 - 0.6 * math.exp(-0.3 * layer)
            for a in range(2):
                tt(tmp, lamv[:, 2 * a, j, :], lamv[:, 2 * a + 1, j, :], ALU.mult, [bf("cst")], [bf("lamtmp")])
                V(lambda e, a=a, j=j: e.reduce_sum(out=small[:, 8 + a:9 + a], in_=tmp, axis=AX.X), [bf("lamtmp")], [bf("small")])
                act(small[:, 8 + a:9 + a], small[:, 8 + a:9 + a], AF.Exp, [bf("small")], [bf("small")])
            tt(small[:, 10:11], small[:, 9:10], small[:, 8:9], ALU.subtract, [bf("small")], [bf("small")])
            ts(small[:, j:j + 1], small[:, 10:11], -linit, None, ALU.add, ALU.bypass, [bf("small")], [bf("small")])
            ts(small[:, 2 + j:3 + j], cs("dsn", j), 1.0 - linit, None, ALU.mult, ALU.bypass, [bf("cst")], [bf("small")])

    lam_prep()

    def layer_start(l):
        wi = w_in_d[l].rearrange("(k p) m -> p k m", p=128)
        wo = w_out_d[l].rearrange("(k p) m -> p k m", p=128)
        for m in range(8):
            wload(wi[:, :, m * 128:(m + 1) * 128], w_in_bf[:, :, m * 128:(m + 1) * 128], bf("w_in"), engs=("scalar", "vector", "gpsimd"))
        for m in range(8):
            wload(wo[:, :, m * 128:(m + 1) * 128], w_out_bf[:, :, m * 128:(m + 1) * 128], bf("w_out"), engs=("scalar", "vector", "gpsimd"))
        msrc = memT_d.rearrange("(k p) n -> p k n", p=128)
        P.dma(lambda e: e.dma_start(out=memT_sb, in_=msrc), "memld", writes=[bf("memT")])
        wks = wk_d[l].rearrange("(k p) m -> p k m", p=128)
        wvs = wv_d[l].rearrange("(k p) m -> p k m", p=128)
        for j in range(2):
            wload(wks[:, :, j * 128:(j + 1) * 128], wk_bf[:, :, j * 128:(j + 1) * 128], bf("wk"), engs=("scalar", "vector"))
            wload(wvs[:, :, j * 128:(j + 1) * 128], wv_bf[:, :, j * 128:(j + 1) * 128], bf("wv"), engs=("scalar", "vector"))
        b2 = psb(2, 4)
        for k in range(8):
            i = rr("sq", 2)
            act(sq[i][:], memT_sb[:, k, :], AF.Square, [bf("memT")], [bf(f"sq{i}")])
            mm(PS[b2][:, 0:256], ones_bf[:], sq[i][:], k == 0, k == 7, [bf(f"sq{i}"), bf("consts")], [bf(f"ps{b2}")])
        r, rb = rstd_from(PS[b2][:, 0:256], 256, 1.0 / 1024, [bf(f"ps{b2}")])
        for k in range(8):
            stt(mem_n[:, k, :], memT_sb[:, k, :], cs("gmem", l * 8 + k), r, ALU.mult, ALU.mult,
                [bf("memT"), rb, bf("cst")], [bf("mem_n")])
        for j in range(2):
            b0 = psb(0, 2)
            for k in range(8):
                mm(PS[b0][:, 0:256], wk_bf[:, k, j * 128:(j + 1) * 128], mem_n[:, k, :], k == 0, k == 7,
                   [bf("wk"), bf("mem_n")], [bf(f"ps{b0}")])
            group_norm_evac(PS[b0][:, 0:256], bf(f"ps{b0}"), 256, cs("mkn", l), KmT[:, j, :], [bf("KmT")], False)
        for i2 in range(2):
            b0 = psb(0, 2)
            for k in range(8):
                mm(PS[b0][:, 0:256], mem_n[:, k, i2 * 128:(i2 + 1) * 128], wv_bf[:, k, :], k == 0, k == 7,
                   [bf("wv"), bf("mem_n")], [bf(f"ps{b0}")])
            act(Vm[:, i2, :], PS[b0][:, 0:256], AF.Copy, [bf(f"ps{b0}")], [bf("Vm")])

    def win_block(l, blk, pre=None):
        hb_, zb_ = HB(), ZB()
        hbuf = bf("h_blk" + sfx())
        xnorm(blk, "gmix", l, hb_, hbuf, pre=pre)
        for m in range(8):
            b0 = psb(0, 2)
            for k in range(8):
                mm(PS[b0][:, 0:BLK], w_in_bf[:, k, m * 128:(m + 1) * 128], hb_[:, k, :], k == 0, k == 7,
                   [bf("w_in"), hbuf], [bf(f"ps{b0}")])
            zbuf = bf(f"z{m}" + sfx())
            if m >= 6:
                group_norm_evac(PS[b0][:, 0:BLK], bf(f"ps{b0}"), BLK, cs("mqn", l), zb_[:, m, :], [zbuf], False)
            elif l >= 2:
                group_norm_evac(PS[b0][:, 0:BLK], bf(f"ps{b0}"), BLK, cs("dqn", l - 2), None, [zbuf], False,
                                split=(zb_[:, m, :], ZHI(m)))
            else:
                act(zb_[:, m, :], PS[b0][:, 0:BLK], AF.Copy, [bf(f"ps{b0}")], [zbuf])

    def memattn_block(l, blk):
        zb_, cm_ = ZB(), CATM()
        for h in range(4):
            j = h // 2
            po = (h % 2) * 64
            if PAR["mbank"] is None:
                nb, dbk = 4 + (h % 2), 6 + (h % 2)
            else:
                nb, dbk = PAR["mbank"]
            zbuf = bf(f"z{6 + j}" + sfx())
            eis = []
            for i2 in range(2):
                b0 = psb(0, 2)
                mm(PS[b0][:, 0:BLK], KmT[po:po + 64, j, i2 * 128:(i2 + 1) * 128], zb_[po:po + 64, 6 + j, :], True, True,
                   [bf("KmT"), zbuf], [bf(f"ps{b0}")])
                ei = rr("eM", 2)
                act(eM[ei][:], PS[b0][:, 0:BLK], AF.Exp, [bf(f"ps{b0}")], [bf(f"eM{ei}")], scale=0.125)
                eis.append(ei)
            for i2 in range(2):
                ei = eis[i2]
                mm(PS[nb][po:po + 64, 0:BLK], Vm[:, i2, h * 64:(h + 1) * 64], eM[ei][:], i2 == 0, i2 == 1,
                   [bf("Vm"), bf(f"eM{ei}")], [bf(f"ps{nb}")])
                mm(PS[dbk][po:po + 64, 0:BLK], ones_bf[:, 0:64], eM[ei][:], i2 == 0, i2 == 1,
                   [bf("consts"), bf(f"eM{ei}")], [bf(f"ps{dbk}")])
            V(lambda e, po=po, dbk=dbk: e.reciprocal(out=rrm[po:po + 64, :], in_=PS[dbk][po:po + 64, 0:BLK]), [bf(f"ps{dbk}")], [bf(f"rrm{po}")])
            tt(cm_[po:po + 64, j, :], PS[nb][po:po + 64, 0:BLK], rrm[po:po + 64, :], ALU.mult,
               [bf(f"ps{nb}"), bf(f"rrm{po}")], [bf(f"cat{6 + j}" + sfx())])

    def wout_block(l, blk):
        c0 = blk * BLK
        cm_ = CATM()
        for m in range(8):
            b0 = psb(0, 2)
            for k in range(8):
                rhs = cat_blk[:, k, :] if k < 6 else cm_[:, k - 6, :]
                cbuf = bf(f"cat{k}") if k < 6 else bf(f"cat{k}" + sfx())
                mm(PS[b0][:, 0:BLK], w_out_bf[:, k, m * 128:(m + 1) * 128], rhs, k == 0, k == 7,
                   [bf("w_out"), cbuf], [bf(f"ps{b0}")])
            tt(xT[:, m, c0:c0 + BLK], xT[:, m, c0:c0 + BLK], PS[b0][:, 0:BLK], ALU.add,
               [bf(f"ps{b0}"), bf(f"x{m}_{blk}")], [bf(f"x{m}_{blk}")])

    def ffn(l):
        P.barrier()
        gsrc = wg_d[l].rearrange("(k p) f -> p k f", p=128)
        usrc = wu_d[l].rearrange("(k p) f -> p k f", p=128)
        dsrc = wd_d[l].rearrange("(f p) m -> p f m", p=128)
        f = 0
        while f < NF:
            nfc = min(FC, NF - f)
            for fi in range(nfc):
                ff = f + fi
                gi = rr("wgb", 2)
                wload(gsrc[:, :, ff * 128:(ff + 1) * 128], wgb[gi][:], bf(f"wg{gi}"))
                wload(usrc[:, :, ff * 128:(ff + 1) * 128], wgb[2 + gi][:], bf(f"wu{gi}"))
                wload(dsrc[:, ff, :], wdb[:, fi, :], bf(f"wd{fi}"))
                for tb in range(4):
                    if ff == 0:
                        for blk in (2 * tb, 2 * tb + 1):
                            xnorm(blk, "gffn", l, hT[:, :, blk * BLK:(blk + 1) * BLK], bf(f"hT{blk}"))
                    bg = psb(0, 2)
                    bu = psb(2, 4)
                    for k in range(8):
                        mm(PS[bg][:], wgb[gi][:, k, :], hT[:, k, tb * 512:(tb + 1) * 512], k == 0, k == 7,
                           [bf(f"wg{gi}"), bf(f"hT{2 * tb}"), bf(f"hT{2 * tb + 1}")], [bf(f"ps{bg}")])
                    for k in range(8):
                        mm(PS[bu][:], wgb[2 + gi][:, k, :], hT[:, k, tb * 512:(tb + 1) * 512], k == 0, k == 7,
                           [bf(f"wu{gi}"), bf(f"hT{2 * tb}"), bf(f"hT{2 * tb + 1}")], [bf(f"ps{bu}")])
                    si = rr("sgt", 2)
                    act(sgt[si][:], PS[bg][:], AF.Silu, [bf(f"ps{bg}")], [bf(f"sgt{si}")])
                    tt(aT[:, fi, tb * 512:(tb + 1) * 512], sgt[si][:], PS[bu][:], ALU.mult,
                       [bf(f"sgt{si}"), bf(f"ps{bu}")], [bf(f"aT{fi}_{tb}")])
            for tb in range(4):
                for m in range(8):
                    bd = psb(4, 8)
                    for fi in range(nfc):
                        mm(PS[bd][:], wdb[:, fi, m * 128:(m + 1) * 128], aT[:, fi, tb * 512:(tb + 1) * 512], fi == 0, fi == nfc - 1,
                           [bf(f"wd{fi}"), bf(f"aT{fi}_{tb}")], [bf(f"ps{bd}")])
                    tt(xT[:, m, tb * 512:(tb + 1) * 512], xT[:, m, tb * 512:(tb + 1) * 512], PS[bd][:], ALU.add,
                       [bf(f"ps{bd}"), bf(f"x{m}_{2 * tb}"), bf(f"x{m}_{2 * tb + 1}")], [bf(f"x{m}_{2 * tb}"), bf(f"x{m}_{2 * tb + 1}")])
            f += nfc
        P.barrier()

    def kv_shared():
        for blk in range(NBLK):
            xnorm(blk, "gkv", 0, hT[:, :, blk * BLK:(blk + 1) * BLK], bf(f"hT{blk // 2}"))
        ksrc = dwk_d.rearrange("(k p) f -> p k f", p=128)
        vsrc = dwv_d.rearrange("(k p) f -> p k f", p=128)
        for t in range(6):
            gi = rr("wgb", 2)
            wload(ksrc[:, :, t * 128:(t + 1) * 128], wgb[gi][:], bf(f"wg{gi}"), engs=("scalar", "gpsimd"))
            wload(vsrc[:, :, t * 128:(t + 1) * 128], wgb[2 + gi][:], bf(f"wu{gi}"), engs=("scalar", "gpsimd"))
            for blk in range(NBLK):
                b0 = psb(0, 2)
                for k in range(8):
                    mm(PS[b0][:, 0:BLK], wgb[gi][:, k, :], hT[:, k, blk * BLK:(blk + 1) * BLK], k == 0, k == 7,
                       [bf(f"wg{gi}"), bf(f"hT{blk // 2}")], [bf(f"ps{b0}")])
                group_norm_evac(PS[b0][:, 0:BLK], bf(f"ps{b0}"), BLK, cs("dkn", 0), KdT[:, t, blk * BLK:(blk + 1) * BLK],
                                [bf(f"KdT{blk}")], False)
                for tt_ in (2 * blk, 2 * blk + 1):
                    b4 = psb(4, 8)
                    for k in range(8):
                        mm(PS[b4][:, 0:128], hT[:, k, tt_ * 128:(tt_ + 1) * 128], wgb[2 + gi][:, k, :], k == 0, k == 7,
                           [bf(f"wu{gi}"), bf(f"hT{tt_ // 4}")], [bf(f"ps{b4}")])
                    V(lambda e, tt_=tt_, t=t, b4=b4: e.tensor_copy(out=Vd[:, tt_, t * 128:(t + 1) * 128], in_=PS[b4][:, 0:128]),
                      [bf(f"ps{b4}")], [bf(f"Vd{tt_}")])
        P.barrier()

    def diff_block(l, blk, hook=None):
        j = l - 2
        zb_ = ZB()
        zhi_ = [ZHI(m_) for m_ in range(6)]
        sf_ = sfx()
        TF = TMP
        r0 = TF[:, 0:256]
        r1 = TF[:, 256:512]
        t0 = TF[:, 512:768]
        t1 = TF[:, 768:1024]
        eT = [TF[:, 1024:1152].bitcast(BF16), TF[:, 1152:1280].bitcast(BF16)]
        nkt = 2 * blk + 2

        def loops(h, c):
            hp = h % 2
            idx = c * 6 + h
            zt = idx // 2
            po = (idx % 2) * 64
            ob, db = 4 + hp, 6 + hp
            pendq = []
            Es = [sgt[0], sgt[1], TF[:, 1024:1280].bitcast(BF16)]
            Eb = [bf("sgt0"), bf("sgt1"), bf("e3")]

            def pv(unit, ei):
                for ik, kt in enumerate(unit):
                    n0 = max(0, kt - 2 * blk) * 128
                    N = BLK - n0
                    rhs = Es[ei][:, ik * 256:ik * 256 + N]
                    mm(PS[ob][:, c * 256 + n0:c * 256 + BLK], Vd[:, kt, h * 128:(h + 1) * 128], rhs, kt == 0, kt == nkt - 1,
                       [bf(f"Vd{kt}"), Eb[ei]], [bf(f"ps{ob}")])
                    mm(PS[db][:, c * 256 + n0:c * 256 + BLK], ones_bf[:], rhs, kt == 0, kt == nkt - 1,
                       [bf("consts"), Eb[ei]], [bf(f"ps{db}")])

            units = [(2 * p_, 2 * p_ + 1) for p_ in range(blk)] + [(2 * blk, 2 * blk + 1)]
            zq = zb_[:, zt, :] if po == 0 else zhi_[zt]
            for unit in units:
                diag = unit[0] >= 2 * blk
                b0 = psb(0, 2)
                for ik, kt in enumerate(unit):
                    n0 = max(0, kt - 2 * blk) * 128
                    N = BLK - n0
                    mm(PS[b0][:, ik * 256:ik * 256 + N], KdT[:, zt, kt * 128:(kt + 1) * 128], zq[:, n0:BLK], True, True,
                       [bf(f"KdT{kt // 2}"), bf(f"z{zt}" + sf_)], [bf(f"ps{b0}")])
                W = 384 if diag else 512
                ei = rr("eT3", 3)
                act(Es[ei][:, 0:W], PS[b0][:, 0:W], AF.Exp, [bf(f"ps{b0}")], [Eb[ei]], scale=0.125)
                if diag:
                    P.op("gpsimd", lambda e, ei=ei: e.memset(Es[ei][64:128, 0:64], 0.0), writes=[Eb[ei]])
                    P.op("gpsimd", lambda e, ei=ei: e.memset(Es[ei][64:128, 256:320], 0.0), writes=[Eb[ei]])
                pendq.append((unit, ei))
                if len(pendq) > 2:
                    pv(*pendq.pop(0))
            while pendq:
                pv(*pendq.pop(0))

        st = {}

        def postA(h):
            hp = h % 2
            ob, db = 4 + hp, 6 + hp
            V(lambda e: e.reciprocal(out=r0, in_=PS[db][:, 0:256]), [bf(f"ps{db}")], [bf("r0")])
            V(lambda e: e.reciprocal(out=r1, in_=PS[db][:, 256:512]), [bf(f"ps{db}")], [bf("r1")])
            tt(t0, PS[ob][:, 0:256], r0, ALU.mult, [bf(f"ps{ob}"), bf("r0")], [bf("t0")])
            tt(t1, PS[ob][:, 256:512], r1, ALU.mult, [bf(f"ps{ob}"), bf("r1")], [bf("t1")])
            stt(t0, t1, small[:, j:j + 1], t0, ALU.mult, ALU.add, [bf("t0"), bf("t1"), bf("small")], [bf("t0")])

        def postM(h):
            i = rr("sq", 2)
            act(sq[i][:], t0, AF.Square, [bf("t0")], [bf(f"sq{i}")])
            b2 = psb(2, 4)
            mm(PS[b2][:, 0:BLK], ones_bf[:], sq[i][:], True, True, [bf(f"sq{i}"), bf("consts")], [bf(f"ps{b2}")])
            st["b2"] = b2

        def postB(h):
            b2 = st["b2"]
            r, rb = rstd_from(PS[b2][:, 0:BLK], BLK, 1.0 / 128, [bf(f"ps{b2}")])
            stt(cat_blk[:, h, :], t0, small[:, 2 + j:3 + j], r, ALU.mult, ALU.mult, [bf("t0"), rb, bf("small")], [bf(f"cat{h}")])

        loops(0, 0)
        loops(0, 1)
        for h in range(6):
            postA(h)
            if h < 5:
                loops(h + 1, 0)
            postM(h)
            if h < 5:
                loops(h + 1, 1)
            postB(h)
            if h == 2 and hook is not None:
                hook()

    SPF = SPARE.bitcast(F32)
    rho = TMP[:, 1200:1224]
    cs4 = TMP[:, 1224:1248]
    sn4 = TMP[:, 1248:1272]
    Sc = small[:, 16:64].rearrange("p (a r) -> p a r", r=2)

    def cmul(dr, di, ar, ai, br, bi, t1, t2, R, W):
        tt(t1, ar, br, ALU.mult, R, [bf("s5t")])
        tt(t2, ai, bi, ALU.mult, R, [bf("s5t")])
        tt(dr, t1, t2, ALU.subtract, [bf("s5t")], W)
        tt(t1, ar, bi, ALU.mult, R, [bf("s5t")])
        tt(t2, ai, br, ALU.mult, R, [bf("s5t")])
        tt(di, t1, t2, ALU.add, [bf("s5t")], W)

    def double_angle(c, s, t1, t2, n):
        for _ in range(n):
            tt(t1, c, c, ALU.mult, [bf("s5p")], [bf("s5t")])
            tt(t2, s, s, ALU.mult, [bf("s5p")], [bf("s5t")])
            stt(s, c, 2.0, s, ALU.mult, ALU.mult, [bf("s5p")], [bf("s5p")])
            tt(c, t1, t2, ALU.subtract, [bf("s5t")], [bf("s5p")])

    def s5_prep(l):
        i = l
        pb = [bf("s5p")]
        tb_ = [bf("s5t")]
        dtF = TMP[:, 1300:1306]
        dt64 = TMP[:, 1306:1312]
        P.dma(lambda e: e.dma_start(out=dtF, in_=s5dtF_d[i]), "s5ld", writes=pb)
        act(dtF, dtF, AF.Exp, pb, pb)
        ts(dt64, dtF, 1.0 / 64, None, ALU.mult, ALU.bypass, pb, pb)
        MIXF = MIX[:, 6144:24576].bitcast(F32)
        sl = [MIXF[:, 768 * q:768 * (q + 1)].rearrange("p (t n) -> p t n", t=6) for q in range(12)]
        A, I, mag, c, s, t1, t2, nr, cr, ci, Wr, Wi = sl
        rden, Br, Bi, Wr2, Wi2 = mag, A, I, mag, nr
        dtb = dtF.unsqueeze(2).to_broadcast([128, 6, 128])
        dt64b = dt64.unsqueeze(2).to_broadcast([128, 6, 128])
        P.dma(lambda e: e.dma_start(out=A, in_=s5F_d[i][:, :, 0, :]), "s5ld", writes=pb)
        P.dma(lambda e: e.dma_start(out=I, in_=s5F_d[i][:, :, 1, :]), "s5ld", writes=pb)
        tt(t1, A, dtb, ALU.mult, pb, tb_)
        act(mag, t1, AF.Exp, tb_, pb)
        tt(t1, I, dt64b, ALU.mult, pb, tb_)
        act(s, t1, AF.Sin, tb_, pb)
        act(c, t1, AF.Sin, tb_, pb, bias=epsT[:, 1:2])
        double_angle(c, s, t1, t2, 6)
        tt(c, mag, c, ALU.mult, pb, pb)
        tt(s, mag, s, ALU.mult, pb, pb)
        tt(t1, A, A, ALU.mult, pb, tb_)
        tt(t2, I, I, ALU.mult, pb, tb_)
        tt(t1, t1, t2, ALU.add, tb_, tb_)
        V(lambda e: e.reciprocal(out=rden, in_=t1), tb_, pb)
        ts(nr, c, -1.0, None, ALU.add, ALU.bypass, pb, pb)
        tt(t1, nr, A, ALU.mult, pb, tb_)
        tt(t2, s, I, ALU.mult, pb, tb_)
        tt(t1, t1, t2, ALU.add, tb_, tb_)
        tt(cr, t1, rden, ALU.mult, tb_ + pb, pb)
        tt(t1, s, A, ALU.mult, pb, tb_)
        tt(t2, nr, I, ALU.mult, pb, tb_)
        tt(t1, t1, t2, ALU.subtract, tb_, tb_)
        tt(ci, t1, rden, ALU.mult, tb_ + pb, pb)
        P.dma(lambda e: e.dma_start(out=Br, in_=s5F_d[i][:, :, 2, :]), "s5ld", reads=tb_, writes=pb)
        P.dma(lambda e: e.dma_start(out=Bi, in_=s5F_d[i][:, :, 3, :]), "s5ld", reads=tb_, writes=pb)
        cmul(Wr, Wi, cr, ci, Br, Bi, t1, t2, pb, pb)
        for k in range(4):
            V(lambda e, k=k, Wr=Wr: e.tensor_copy(out=WB[:, :, k, 0, :], in_=Wr), pb, [bf("WB")])
            V(lambda e, k=k, Wi=Wi: e.tensor_copy(out=WB[:, :, k, 1, :], in_=Wi), pb, [bf("WB")])
            if k < 3:
                cmul(Wr2, Wi2, c, s, Wr, Wi, t1, t2, pb, pb)
                Wr, Wr2 = Wr2, Wr
                Wi, Wi2 = Wi2, Wi
        V(lambda e: e.memset(small[:, 12:13], 0.0), pb + tb_, [bf("CA"), bf("Rtab"), bf("wglu"), bf("small12")])
        inM = TMP[:, 1400:1472].rearrange("p (a n) -> p a n", a=3)
        P.dma(lambda e: e.dma_start(out=inM, in_=s5M_d[i]), "s5ld", writes=pb)
        Am, Im, dtm = inM[:, 0, :], inM[:, 1, :], inM[:, 2, :]
        mt = [TMP[:, 1480 + 24 * q:1504 + 24 * q] for q in range(16)]
        xm, cm, sm, m1, m2, magm = mt[0:6]
        pr = [None, mt[6], mt[8], mt[10], mt[12]]
        pi_ = [None, mt[7], mt[9], mt[11], mt[13]]
        act(dtm, dtm, AF.Exp, pb, pb)
        tt(xm, Am, dtm, ALU.mult, pb, pb)
        act(rho, xm, AF.Exp, pb, pb, scale=4.0)
        act(magm, xm, AF.Exp, pb, pb)
        tt(xm, Im, dtm, ALU.mult, pb, pb)
        act(sm, xm, AF.Sin, pb, pb, scale=1.0 / 64)
        act(cm, xm, AF.Sin, pb, pb, scale=1.0 / 64, bias=epsT[:, 1:2])
        double_angle(cm, sm, m1, m2, 6)
        tt(pr[1], magm, cm, ALU.mult, pb, pb)
        tt(pi_[1], magm, sm, ALU.mult, pb, pb)
        for e_ in range(2, 5):
            cmul(pr[e_], pi_[e_], pr[e_ - 1], pi_[e_ - 1], pr[1], pi_[1], m1, m2, pb, pb)
        double_angle(cm, sm, m1, m2, 2)
        V(lambda e: e.tensor_copy(out=cs4, in_=cm), pb, pb)
        V(lambda e: e.tensor_copy(out=sn4, in_=sm), pb, pb)
        Cin = SPF[:, 0:1536].rearrange("p (r a n) -> p r a n", r=2, a=24)
        P.dma(lambda e: e.dma_start(out=Cin, in_=s5C_d[i]), "s5ld", writes=pb + tb_ + [bf("memT"), bf("mem_n")])
        Cr, Ci = Cin[:, 0], Cin[:, 1]
        c1 = SPF[:, 1536:2304].rearrange("p (a n) -> p a n", a=24)
        c2 = SPF[:, 2304:3072].rearrange("p (a n) -> p a n", a=24)
        V(lambda e: e.tensor_copy(out=CA[:, :, 0, 0, :], in_=Cr), pb, [bf("CA")])
        ts(CA[:, :, 0, 1, :], Ci, -1.0, None, ALU.mult, ALU.bypass, pb, [bf("CA")])
        for e_ in range(1, 5):
            prb = pr[e_].unsqueeze(2).to_broadcast([128, 24, 32])
            pib = pi_[e_].unsqueeze(2).to_broadcast([128, 24, 32])
            tt(c1, Cr, prb, ALU.mult, pb, tb_)
            tt(c2, Ci, pib, ALU.mult, pb, tb_)
            tt(CA[:, :, e_, 0, :], c1, c2, ALU.subtract, tb_, [bf("CA")])
            tt(c1, Cr, pib, ALU.mult, pb, tb_)
            tt(c2, Ci, prb, ALU.mult, pb, tb_)
            tt(c1, c1, c2, ALU.add, tb_, tb_)
            ts(CA[:, :, e_, 1, :], c1, -1.0, None, ALU.mult, ALU.bypass, tb_, [bf("CA")])
        Rr, Ri = Rtab[:, 0], Rtab[:, 1]
        V(lambda e: e.memset(Rr[:, :, 0:1], 1.0), writes=[bf("Rtab")])
        V(lambda e: e.memset(Ri[:, :, 0:1], 0.0), writes=[bf("Rtab")])
        qr, qi = cm, sm
        n = 1
        while n < 64:
            qrb = qr.unsqueeze(2).to_broadcast([128, 24, n])
            qib = qi.unsqueeze(2).to_broadcast([128, 24, n])
            a1 = c1[:, :, 0:n]
            a2 = c2[:, :, 0:n]
            tt(a1, Rr[:, :, 0:n], qrb, ALU.mult, pb + [bf("Rtab")], tb_)
            tt(a2, Ri[:, :, 0:n], qib, ALU.mult, pb + [bf("Rtab")], tb_)
            tt(Rr[:, :, n:2 * n], a1, a2, ALU.subtract, tb_, [bf("Rtab")])
            tt(a1, Rr[:, :, 0:n], qib, ALU.mult, pb + [bf("Rtab")], tb_)
            tt(a2, Ri[:, :, 0:n], qrb, ALU.mult, pb + [bf("Rtab")], tb_)
            tt(Ri[:, :, n:2 * n], a1, a2, ALU.add, tb_, [bf("Rtab")])
            double_angle(qr, qi, m1, m2, 1)
            n *= 2
        V(lambda e: e.memset(small[:, 16:64], 0.0), writes=[bf("Sc")])
        V(lambda e: e.memset(TMP[:, 1992:2312], 0.0), writes=[bf("Z3")])

        gsrc = wglu_d[i].rearrange("(k p) m -> p k m", p=128)
        for m in range(6):
            wload(gsrc[:, :, m * 128:(m + 1) * 128], wglu_bf[:, :, m * 128:(m + 1) * 128], bf("wglu"))
        P.op("gpsimd", lambda e: e.memset(sgt[0][:], 0.0), writes=[bf("sgt0")])
        P.op("gpsimd", lambda e: e.memset(sgt[1][:], 0.0), writes=[bf("sgt1")])
        P.op("gpsimd", lambda e: e.memset(stg[0][:], 0.0), writes=[bf("stg0")])
        P.op("gpsimd", lambda e: e.memset(wgb[1][:], 0.0), writes=[bf("wg1")])

    def s5_block(l, blk):
        i = l
        L2 = [SPF[:, 1536 + 256 * q:1792 + 256 * q].rearrange("p (a c) -> p a c", a=4) for q in range(6)]
        t1, t2, Xr, Xi, Vr, Vi = L2
        Sf = TMP[:, 0:520].rearrange("p (r a c) -> p r a c", r=2, a=4)
        Spb = TMP[:, 520:776].bitcast(BF16).rearrange("p (a r c) -> p a r c", a=4, r=2)
        yv = TMP[:, 776:1032]
        sgS = TMP[:, 1032:1160].bitcast(BF16)
        ini = TMP[:, 1160:1176].rearrange("p (a q) -> p a q", a=4)
        hS = h_blk
        stg0b = stg[0][:].bitcast(BF16)
        wflat = [w_[:].rearrange("p k n -> p (k n)") for w_ in wgb]
        um_set = [[sgt[0][:, 0:256], sgt[0][:, 256:512], sgt[1][:, 0:256], sgt[1][:, 256:512]],
                  [stg0b[:, q * 256:(q + 1) * 256] for q in range(4)]]
        um_buf = [[bf("sgt0"), bf("sgt0"), bf("sgt1"), bf("sgt1")], [bf("stg0")] * 4]
        Dsb_set = [SPF[:, 1024:1536], wflat[0].bitcast(F32)]
        Dsb_buf = [bf("Dsb"), bf("wg0")]
        sh_set = [[SPARE[:, q * 256:(q + 1) * 256] for q in range(8)],
                  [wflat[2][:, q * 256:(q + 1) * 256] for q in range(4)] + [wflat[3][:, q * 256:(q + 1) * 256] for q in range(4)]]
        sh_buf = [[bf(f"sh{q}") for q in range(8)], [bf("wu0")] * 4 + [bf("wu1")] * 4]
        Z3_set = [TMP[:, 1992:2312].bitcast(BF16).rearrange("p (e r n) -> p e r n", e=5, r=2),
                  wflat[1][:, 0:640].rearrange("p (e r n) -> p e r n", e=5, r=2)]
        Z3_buf = [bf("Z3"), bf("wg1")]
        bDs = {}

        def stageA(tau):
            pb_ = tau % 2
            um, ub = um_set[pb_], um_buf[pb_]
            um4 = [u_.rearrange("p (c j) -> p c j", j=4) for u_ in um]
            zb = bf(f"z{tau}")
            for q in range(3):
                act(um[q][q * 32:(q + 1) * 32, :], z_blk[q * 32:(q + 1) * 32, tau, :], AF.Copy, [zb], [ub[q]])
            act(um[3][64:128, :], z_blk[64:128, tau, :], AF.Copy, [zb], [ub[3]])
            P.op("gpsimd", lambda e: e.memset(um[3][64:96, :], 0.0), writes=[ub[3]])
            P.op("gpsimd", lambda e: e.tensor_copy(out=Z3_set[pb_][:, :, :, 32:64], in_=CA[:, 4 * tau + 3, :, :, :]), reads=[bf("CA")], writes=[Z3_buf[pb_]])
            bD = psb(2, 4)
            for q in range(4):
                for ri in range(2):
                    slot = q * 2 + ri
                    for t in range(4):
                        mm(PS[bD][:, slot * 64:(slot + 1) * 64], WB[:, tau, 3 - t, ri, :],
                           um4[q][:, :, t], t == 0, t == 3, [bf("WB"), ub[q]], [bf(f"ps{bD}")])
            act(Dsb_set[pb_], PS[bD][:, 0:512], AF.Copy, [bf(f"ps{bD}")], [Dsb_buf[pb_]])
            for q in range(4):
                for ri in range(2):
                    slot = q * 2 + ri
                    b0 = slot % 2
                    half = (slot // 2) % 2
                    pv = PS[b0][:, half * 256:(half + 1) * 256]
                    pv4 = pv.rearrange("p (c j) -> p c j", j=4)
                    pbuf = bf(f"ps{b0}")
                    for k in range(4):
                        mm(pv4[:, :, k:4], WB[:, tau, k, ri, :], um4[q][:, :, 0:4 - k],
                           k == 0, k == 3, [bf("WB"), ub[q]], [pbuf])
                    act(sh_set[pb_][slot], pv, AF.Copy, [pbuf], [sh_buf[pb_][slot]])

        def stageB(tau):
            pb_ = tau % 2
            Dsb = Dsb_set[pb_].rearrange("p (q r c) -> p q r c", q=4, r=2)
            Dr, Di = Dsb[:, :, 0, :], Dsb[:, :, 1, :]
            Rr, Ri = Rtab[:, 0, 4 * tau:4 * tau + 4, :], Rtab[:, 1, 4 * tau:4 * tau + 4, :]
            db_ = Dsb_buf[pb_]
            Bt1, Bt2, BXr, BXi, BVr, BVi = bf("L2t1"), bf("L2t2"), bf("L2Xr"), bf("L2Xi"), bf("L2Vr"), bf("L2Vi")
            RT = bf("Rtab")
            tt(t1, Rr, Dr, ALU.mult, [RT, db_], [Bt1])
            tt(t2, Ri, Di, ALU.mult, [RT, db_], [Bt2])
            tt(Vr, Rr, Di, ALU.mult, [RT, db_], [BVr])
            tt(Vi, Ri, Dr, ALU.mult, [RT, db_], [BVi])
            scr, sci = Sc[:, 4 * tau:4 * tau + 4, 0], Sc[:, 4 * tau:4 * tau + 4, 1]
            c4, s4 = cs4[:, 4 * tau:4 * tau + 4], sn4[:, 4 * tau:4 * tau + 4]
            ir, ii, ta, tb2 = ini[:, 0, :], ini[:, 1, :], ini[:, 2, :], ini[:, 3, :]
            tc, td = TMP[:, 1176:1180], TMP[:, 1180:1184]
            sp_ = [bf("Sc"), bf("s5p")]
            tt(ta, c4, scr, ALU.mult, sp_, [bf("ita")])
            tt(tb2, s4, sci, ALU.mult, sp_, [bf("itb")])
            tt(tc, s4, scr, ALU.mult, sp_, [bf("itc")])
            tt(td, c4, sci, ALU.mult, sp_, [bf("itd")])
            tt(Xr, t1, t2, ALU.add, [Bt1, Bt2], [BXr])
            tt(Xi, Vr, Vi, ALU.subtract, [BVr, BVi], [BXi])
            tt(ir, ta, tb2, ALU.subtract, [bf("ita"), bf("itb")], [bf("iir")])
            tt(ii, tc, td, ALU.add, [bf("itc"), bf("itd")], [bf("iii")])
            for q in range(4):
                pr_ = 4 * tau + q
                rbc = rho[:, pr_:pr_ + 1].to_broadcast([128, 64])
                V(lambda e, q=q, rbc=rbc: e.tensor_tensor_scan(out=Vr[:, q, :], data0=rbc, data1=Xr[:, q, :], initial=ir[:, q:q + 1],
                                                                op0=ALU.mult, op1=ALU.add), [BXr, bf("iir"), bf("s5p")], [BVr])
            for q in range(4):
                pr_ = 4 * tau + q
                rbc = rho[:, pr_:pr_ + 1].to_broadcast([128, 64])
                V(lambda e, q=q, rbc=rbc: e.tensor_tensor_scan(out=Vi[:, q, :], data0=rbc, data1=Xi[:, q, :], initial=ii[:, q:q + 1],
                                                                op0=ALU.mult, op1=ALU.add), [BXi, bf("iii"), bf("s5p")], [BVi])
            V(lambda e: e.tensor_copy(out=Sf[:, 0, :, 0], in_=scr), [bf("Sc")], [bf("Sf0")])
            V(lambda e: e.tensor_copy(out=Sf[:, 1, :, 0], in_=sci), [bf("Sc")], [bf("Sf1")])
            tt(t1, Rr, Vr, ALU.mult, [RT, BVr], [Bt1])
            tt(t2, Ri, Vi, ALU.mult, [RT, BVi], [Bt2])
            tt(Xr, Rr, Vi, ALU.mult, [RT, BVi], [BXr])
            tt(Xi, Ri, Vr, ALU.mult, [RT, BVr], [BXi])
            tt(Sf[:, 0, :, 1:65], t1, t2, ALU.subtract, [Bt1, Bt2], [bf("Sf0")])
            tt(Sf[:, 1, :, 1:65], Xr, Xi, ALU.add, [BXr, BXi], [bf("Sf1")])
            act(Spb[:, :, 0, :], Sf[:, 0, :, 0:64], AF.Copy, [bf("Sf0")], [bf("Spb")])
            act(Spb[:, :, 1, :], Sf[:, 1, :, 0:64], AF.Copy, [bf("Sf1")], [bf("Spb")])
            V(lambda e: e.tensor_copy(out=scr, in_=Sf[:, 0, :, 64]), [bf("Sf0")], [bf("Sc")])
            V(lambda e: e.tensor_copy(out=sci, in_=Sf[:, 1, :, 64]), [bf("Sf1")], [bf("Sc")])

        def stageC(tau):
            pb_ = tau % 2
            sh, shb = sh_set[pb_], sh_buf[pb_]
            Z3v = Z3_set[pb_]
            by = 4 + (tau % 2)
            for q in (3, 2, 0, 1):
                pr_ = 4 * tau + q
                if q == 3:
                    out = PS[by][64:128, 0:256]
                    lw = lambda e_, r_: Z3v[:, e_, r_, :]
                    wb_ = [Z3_buf[pb_]]
                else:
                    out = PS[by][q * 32:(q + 1) * 32, 0:256]
                    lw = lambda e_, r_, pr_=pr_: CA[:, pr_, e_, r_, :]
                    wb_ = [bf("CA")]
                out4 = out.rearrange("p (c j) -> p c j", j=4)
                first = (q != 2)
                mm(out, lw(0, 0), sh[q * 2], first, False, wb_ + [shb[q * 2]], [bf(f"ps{by}")])
                mm(out, lw(0, 1), sh[q * 2 + 1], False, False, wb_ + [shb[q * 2 + 1]], [bf(f"ps{by}")])
                for j in range(4):
                    mm(out4[:, :, j], lw(j + 1, 0), Spb[:, q, 0, :], False, False, wb_ + [bf("Spb")], [bf(f"ps{by}")])
                    mm(out4[:, :, j], lw(j + 1, 1), Spb[:, q, 1, :], False, j == 3, wb_ + [bf("Spb")], [bf(f"ps{by}")])

        def stageC2(tau):
            by = 4 + (tau % 2)
            stt(yv, z_blk[:, tau, :], cs("dF", i * 6 + tau), PS[by][:, 0:256], ALU.mult, ALU.add, [bf(f"z{tau}"), bf(f"ps{by}"), bf("cst")], [bf("yv")])
            act(hS[:, tau, :], yv, AF.Gelu_apprx_tanh, [bf("yv")], [bf("h_blk")])

        stageA(0)
        stageA(1)
        stageB(0)
        stageC(0)
        for tau in range(1, 6):
            if tau < 5:
                stageA(tau + 1)
            stageB(tau)
            stageC2(tau - 1)
            stageC(tau)
        stageC2(5)
        for m in range(6):
            bg = 6 + (m % 2)
            for k in range(6):
                mm(PS[bg][:, 0:256], wglu_bf[:, k, m * 128:(m + 1) * 128], hS[:, k, :], k == 0, k == 5,
                   [bf("wglu"), bf("h_blk")], [bf(f"ps{bg}")])
            act(sgS, PS[bg][:, 0:256], AF.Sigmoid, [bf(f"ps{bg}")], [bf("sgS")])
            tt(cat_blk[:, m, :], hS[:, m, :], sgS, ALU.mult, [bf("h_blk"), bf("sgS")], [bf(f"cat{m}")])

    import os
    STG = int(os.environ.get("KSTAGE", "99"))
    for l in range(n_layers):
        if STG >= 1:
            layer_start(l)
        if l < 2 and STG >= 2:
            s5_prep(l)
        P.barrier()
        if l >= 2:
            for p_ in range(2):
                zs_ = z_sets[p_]
                P.op("gpsimd", lambda e, zs_=zs_: e.memset(zs_[64:128, 0:6, :], 0.0), writes=[bf(f"z{m_}" + ("" if p_ == 0 else "_1")) for m_ in range(6)])
            P.op("gpsimd", lambda e: e.memset(TMP[0:64, 1280:2048], 0.0), writes=[bf(f"z{m_}") for m_ in range(6)])
            P.op("gpsimd", lambda e: e.memset(SPARE[0:64, 4096:5632], 0.0), writes=[bf(f"z{m_}_1") for m_ in range(6)])
            PAR["p"] = 0
            PAR["mbank"] = None
            win_block(l, 0)
            memattn_block(l, 0)
            for blk in range(NBLK):
                def hook(blk=blk):
                    if blk + 1 < NBLK:
                        PAR["p"] = (blk + 1) % 2
                        PAR["mbank"] = (2, 3)
                        win_block(l, blk + 1)
                        memattn_block(l, blk + 1)
                        PAR["p"] = blk % 2
                PAR["p"] = blk % 2
                diff_block(l, blk, hook)
                wout_block(l, blk)
            PAR["p"] = 0
            PAR["mbank"] = None
        else:
            pre = None
            for blk in range(NBLK):
                win_block(l, blk, pre)
                memattn_block(l, blk)
                pre = xnorm(blk + 1, "gmix", l, None, None, only_p1=True) if blk + 1 < NBLK else None
                s5_block(l, blk)
                wout_block(l, blk)
        if STG >= 9:
            ffn(l)
        if l == 1 and n_layers > 2:
            kv_shared()
    P.barrier()
    ysrc = yT_d.rearrange("(k p) n -> p k n", p=128)
    for k in range(8):
        P.dma(lambda e, k=k: e.dma_start(out=ysrc[:, k, :], in_=xT[:, k, :]), "yst", reads=[bf(f"x{k}_{b}") for b in range(8)])
    P.rec["sync"].append(([(P.sems["yst"], P.dma_cnt["yst"])], None, None, 0))
    return P.finish()


_NC_CACHE = {}


def _prep_inputs(inp, n_cores=8):
    f32 = np.float32
    g = lambda k: np.asarray(inp[k], dtype=f32)
    cst = np.zeros((128, NCST), f32)

    def put(name, arr):
        o, w = CST_OFF[name]
        assert arr.shape == (128, w), (name, arr.shape)
        cst[:, o:o + w] = arr

    fm = lambda a: a.reshape(a.shape[0], 8, 128).transpose(2, 0, 1).reshape(128, -1)
    put("gmix", fm(g("norm_mix")))
    put("gffn", fm(g("norm_ffn")))
    put("gmem", fm(g("norm_mem")))
    put("gkv", fm(g("kv_norm")[None]))
    p64 = np.arange(128) % 64
    put("mqn", g("mem_q_norm")[:, p64].T)
    put("mkn", g("mem_k_norm")[:, p64].T)
    put("dqn", g("diff_q_norm")[:, p64].T)
    put("dkn", g("diff_k_norm")[p64][:, None])
    put("dsn", g("diff_sub_norm").T)
    lam = np.stack([g("diff_lambda_q1"), g("diff_lambda_k1"), g("diff_lambda_q2"), g("diff_lambda_k2")])
    put("lam", np.broadcast_to(lam.reshape(1, 512), (128, 512)))
    put("dF", g("ssm_d").reshape(2, 6, 128).transpose(2, 0, 1).reshape(128, 12))
    G, N, Pp = 48, 64, 16
    a_re, a_im, ldt = g("ssm_a_re"), g("ssm_a_im"), g("ssm_log_dt")
    b_re, b_im, c_re, c_im = g("ssm_b_re"), g("ssm_b_im"), g("ssm_c_re"), g("ssm_c_im")
    s5F = np.zeros((2, 128, 6, 4, 128), f32)
    s5dtF = np.zeros((2, 128, 6), f32)
    s5M = np.zeros((2, 128, 3, 24), f32)
    s5C = np.zeros((2, 128, 2, 24, 32), f32)
    for i in range(2):
        for gg in range(G):
            tau, gl = gg // 8, gg % 8
            rows = slice(gl * 16, gl * 16 + 16)
            half = gg % 2
            cols = slice(half * 64, half * 64 + 64)
            for hh in range(2):
                s5F[i, rows, tau, 0, hh * 64:(hh + 1) * 64] = a_re[i, gg][None, :]
                s5F[i, rows, tau, 1, hh * 64:(hh + 1) * 64] = a_im[i, gg][None, :]
            s5F[i, rows, tau, 2, cols] = b_re[i, gg].T
            s5F[i, rows, tau, 3, cols] = b_im[i, gg].T
            s5dtF[i, rows, tau] = ldt[i, gg]
            pr = gg // 2
            mrows = slice(half * 64, half * 64 + 64)
            s5M[i, mrows, 0, pr] = a_re[i, gg]
            s5M[i, mrows, 1, pr] = a_im[i, gg]
            s5M[i, mrows, 2, pr] = ldt[i, gg]
            s5C[i, mrows, 0, pr, half * 16:(half + 1) * 16] = c_re[i, gg].T
            s5C[i, mrows, 1, pr, half * 16:(half + 1) * 16] = c_im[i, gg].T
    shared = {"cst": cst, "s5F": s5F, "s5dtF": s5dtF, "s5M": s5M, "s5C": s5C}
    for k in ["w_in", "w_out", "mem_wk", "mem_wv", "ffn_w_gate", "ffn_w_up", "ffn_w_down", "ssm_w_glu", "diff_wk", "diff_wv"]:
        shared[k] = np.ascontiguousarray(g(k))
    x, mem = g("x"), g("mem")
    maps = []
    for b in range(n_cores):
        d = dict(shared)
        d["xT"] = np.ascontiguousarray(x[b].T)
        d["memT"] = np.ascontiguousarray(mem[b].T)
        maps.append(d)
    return maps


def kernel(**inputs):
    n_layers = int(inputs.pop("_n_layers", NL))
    maps = _prep_inputs(inputs)
    nc = build(n_layers)
    res = run_bass_kernel_spmd(nc, maps, core_ids=list(range(8)))
    out = np.stack([np.ascontiguousarray(r["yT"].T) for r in res.results], axis=0)
    return out.astype(np.float32)
```

```python
import contextlib
import numpy as np
import concourse.bass as bass
import concourse.mybir as mybir
from concourse.bass_utils import run_bass_kernel_spmd

F32 = mybir.dt.float32
BF16 = mybir.dt.bfloat16
AF = mybir.ActivationFunctionType
ALU = mybir.AluOpType
AX = mybir.AxisListType

SEM_ROT = 20000


class Buf:
    __slots__ = ("name", "w", "r")

    def __init__(self, name=""):
        self.name = name
        self.w = None
        self.r = []


class Prog:
    ENGS = ("tensor", "vector", "scalar", "gpsimd", "sync")

    def __init__(self):
        self.nc = bass.Bass("TRN2", target_bir_lowering=False)
        self.stack = contextlib.ExitStack()
        self.rec = {e: [] for e in self.ENGS}
        self.sems = {}
        self.cur = {}
        self.epoch = {e: 0 for e in self.ENGS}
        self.known = {e: {} for e in self.ENGS}
        self.dma_cnt = {}
        self.relax = False
        for e in self.ENGS:
            self._new_epoch(e)

    def sem(self, key):
        if key not in self.sems:
            self.sems[key] = self.stack.enter_context(self.nc.semaphore(str(key)))
        return self.sems[key]

    def _new_epoch(self, e):
        self.epoch[e] += 1
        key = (e, self.epoch[e])
        self.sem(key)
        self.cur[e] = [key, 0]

    def sbuf(self, name, shape, dt):
        return self.stack.enter_context(self.nc.sbuf_tensor(name, list(shape), dt))

    def psum(self, name, shape, dt=F32):
        return self.stack.enter_context(self.nc.psum_tensor(name, list(shape), dt))

    def dram(self, name, shape, dt, kind):
        return self.nc.dram_tensor(name, list(shape), dt, kind=kind).ap()

    def _deps(self, eng, reads, writes):
        toks = []
        for b in reads:
            if b.w is not None:
                toks.append(b.w)
        for b in writes:
            if b.w is not None:
                toks.append(b.w)
            toks.extend(b.r)
        need = {}
        for (k, v) in toks:
            if eng == "tensor" and k[0] == "tensor":
                continue
            if self.known[eng].get(k, 0) >= v:
                continue
            if need.get(k, 0) < v:
                need[k] = v
        for k, v in need.items():
            self.known[eng][k] = v
        return [(self.sems[k], v) for k, v in need.items()]

    def op(self, eng, fn, reads=(), writes=()):
        waits = self._deps(eng, reads, writes)
        cur = self.cur[eng]
        if cur[1] >= SEM_ROT:
            self._new_epoch(eng)
            cur = self.cur[eng]
        cur[1] += 1
        tok = (cur[0], cur[1])
        self.rec[eng].append((waits, fn, self.sems[cur[0]], 1))
        for b in writes:
            b.w = tok
            b.r = []
        for b in reads:
            if b not in writes:
                b.r.append(tok)
        return tok

    def dma(self, fn, semkey, reads=(), writes=(), eng="sync"):
        waits = self._deps(eng, reads, writes)
        self.sem(semkey)
        self.dma_cnt[semkey] = self.dma_cnt.get(semkey, 0) + 16
        tok = (semkey, self.dma_cnt[semkey])
        self.rec[eng].append((waits, fn, self.sems[semkey], 16))
        for b in writes:
            b.w = tok
            b.r = []
        for b in reads:
            if b not in writes:
                b.r.append(tok)
        return tok

    def wait_all(self, eng, bufs):
        waits = self._deps(eng, bufs, ())
        self.rec[eng].append((waits, None, None, 0))


    def barrier(self):
        toks = []
        for e in self.ENGS:
            for ep in range(1, self.epoch[e] + 1):
                k = (e, ep)
                v = self.cur[e][1] if ep == self.epoch[e] else None
                if v is None:
                    continue
                if v > 0:
                    toks.append((k, v))
        for k, v in self.dma_cnt.items():
            toks.append((k, v))
        for e in self.ENGS:
            waits = []
            for (k, v) in toks:
                if k[0] == e and isinstance(k, tuple) and k[0] in self.ENGS and e != "sync":
                    pass
                if self.known[e].get(k, 0) >= v:
                    continue
                self.known[e][k] = v
                waits.append((self.sems[k], v))
            if waits:
                self.rec[e].append((waits, None, None, 0))

    def finish(self):
        nc = self.nc
        rec = self.rec

        def replay(e, name):
            for (waits, fn, sem, inc) in rec[name]:
                for (s, v) in waits:
                    e.wait_ge(s, v)
                if fn is not None:
                    ins = fn(e)
                    ins.then_inc(sem, inc)

        with nc.Block() as block:
            @block.tensor
            def _(e):
                replay(e, "tensor")

            @block.vector
            def _(e):
                replay(e, "vector")

            @block.scalar
            def _(e):
                replay(e, "scalar")

            @block.gpsimd
            def _(e):
                replay(e, "gpsimd")

            @block.sync
            def _(e):
                replay(e, "sync")
        self.stack.close()
        return nc

import math

EPS = 1e-6
NL = 4
BLK = 256
NBLK = 8
DFF = 2816
NF = 22
FC = 4


def _cst_layout():
    off = {}
    n = 0
    for name, w in [("gmix", 32), ("gffn", 32), ("gmem", 32), ("gkv", 8), ("mqn", 4), ("mkn", 4),
                    ("dqn", 2), ("dkn", 1), ("dsn", 2), ("lam", 512), ("dF", 12)]:
        off[name] = (n, w)
        n += w
    return off, n


CST_OFF, NCST = _cst_layout()


def build(n_layers=NL):
    P = Prog()
    nc = P.nc
    dI = lambda name, shape: P.dram(name, shape, F32, "ExternalInput")
    xT_d = dI("xT", [1024, 2048])
    memT_d = dI("memT", [1024, 256])
    cst_d = dI("cst", [128, NCST])
    w_in_d = dI("w_in", [4, 1024, 1024])
    w_out_d = dI("w_out", [4, 1024, 1024])
    wk_d = dI("mem_wk", [4, 1024, 256])
    wv_d = dI("mem_wv", [4, 1024, 256])
    wg_d = dI("ffn_w_gate", [4, 1024, DFF])
    wu_d = dI("ffn_w_up", [4, 1024, DFF])
    wd_d = dI("ffn_w_down", [4, DFF, 1024])
    wglu_d = dI("ssm_w_glu", [2, 768, 768])
    dwk_d = dI("diff_wk", [1024, 768])
    dwv_d = dI("diff_wv", [1024, 768])
    s5F_d = dI("s5F", [2, 128, 6, 4, 128])
    s5dtF_d = dI("s5dtF", [2, 128, 6])
    s5M_d = dI("s5M", [2, 128, 3, 24])
    s5C_d = dI("s5C", [2, 128, 2, 24, 32])
    yT_d = P.dram("yT", [1024, 2048], F32, "ExternalOutput")

    xT = P.sbuf("xT_sb", [128, 8, 2048], F32)
    MIX = P.sbuf("MIX", [128, 24576], BF16)
    ARENA = P.sbuf("ARENA", [128, 28672], BF16)
    stg = [P.sbuf(f"stg{i}", [128, 1024], F32) for i in range(2)]
    wgb = [P.sbuf(f"wgb{i}", [128, 8, 128], BF16) for i in range(4)]
    cst = P.sbuf("cst_sb", [128, NCST], F32)
    ones_bf = P.sbuf("ones_bf", [128, 128], BF16)
    blk64 = P.sbuf("blk64", [128, 128], BF16)
    epsT = P.sbuf("epsT", [128, 2], F32)
    KmT = P.sbuf("KmT", [128, 2, 256], BF16)
    Vm = P.sbuf("Vm", [128, 2, 256], BF16)
    sq = [P.sbuf(f"sq{i}", [128, 256], BF16) for i in range(2)]
    rstd = [P.sbuf(f"rstd{i}", [128, 256], F32) for i in range(2)]
    sgt = [P.sbuf(f"sgt{i}", [128, 512], BF16) for i in range(2)]
    eM = [P.sbuf(f"eM{i}", [128, 256], BF16) for i in range(2)]
    rrm = P.sbuf("rrm", [128, 256], F32)
    rstdN = P.sbuf("rstdN", [128, 256], F32)
    TMP = P.sbuf("TMP", [128, 2320], F32)
    small = P.sbuf("small", [128, 64], F32)
    PS = [P.psum(f"ps{i}", [128, 512]) for i in range(8)]

    B = {}

    def bf(name):
        if name not in B:
            B[name] = Buf(name)
        return B[name]

    def cs(name, i=0, n=1):
        o, w = CST_OFF[name]
        return cst[:, o + i:o + i + n]

    w_in_bf = ARENA[:, 0:8192].rearrange("p (k m) -> p k m", k=8)
    w_out_bf = ARENA[:, 8192:16384].rearrange("p (k m) -> p k m", k=8)
    h_blk = ARENA[:, 16384:18432].rearrange("p (k n) -> p k n", k=8)
    z_blk = ARENA[:, 18432:20480].rearrange("p (k n) -> p k n", k=8)
    cat_blk = ARENA[:, 20480:22528].rearrange("p (k n) -> p k n", k=8)
    SPARE = ARENA[:, 22528:28672]
    hT = ARENA[:, 0:16384].rearrange("p (k n) -> p k n", k=8)
    aT = ARENA[:, 16384:24576].rearrange("p (f n) -> p f n", f=FC)
    wdb = ARENA[:, 24576:28672].rearrange("p (f n) -> p f n", f=FC)
    KdT = MIX[:, 0:12288].rearrange("p (t n) -> p t n", t=6)
    Vd = MIX[:, 12288:24576].rearrange("p (t n) -> p t n", t=16)
    WB = MIX[:, 0:6144].rearrange("p (t k r n) -> p t k r n", t=6, k=4, r=2)
    CA = MIX[:, 6144:13824].rearrange("p (a e r n) -> p a e r n", a=24, e=5, r=2)
    Rtab = MIX[:, 13824:19968].bitcast(F32).rearrange("p (r a c) -> p r a c", r=2, a=24)
    wglu_bf = MIX[:, 19968:24576].rearrange("p (k m) -> p k m", k=6)
    memT_sb = SPARE[:, 0:4096].bitcast(F32).rearrange("p (k n) -> p k n", k=8)
    mem_n = SPARE[:, 4096:6144].rearrange("p (k n) -> p k n", k=8)
    wk_bf = ARENA[:, 16384:18432].rearrange("p (k n) -> p k n", k=8)
    wv_bf = ARENA[:, 18432:20480].rearrange("p (k n) -> p k n", k=8)

    PAR = {"p": 0, "mbank": None}
    h_sets = [h_blk, SPARE[:, 0:2048].rearrange("p (k n) -> p k n", k=8)]
    z_sets = [z_blk, SPARE[:, 2048:4096].rearrange("p (k n) -> p k n", k=8)]
    catm_sets = [cat_blk[:, 6:8, :], SPARE[:, 5632:6144].rearrange("p (k n) -> p k n", k=2)]

    def HB():
        return h_sets[PAR["p"]]

    def ZB():
        return z_sets[PAR["p"]]

    def CATM():
        return catm_sets[PAR["p"]]

    def sfx():
        return "" if PAR["p"] == 0 else "_1"

    def V(fn, reads=(), writes=(), eng="vector"):
        return P.op(eng, fn, reads=reads, writes=writes)

    def tt(out, a, b, op, reads, writes, eng="vector"):
        return P.op(eng, lambda e: e.tensor_tensor(out=out, in0=a, in1=b, op=op), reads=reads, writes=writes)

    def stt(out, a, s, b, op0, op1, reads, writes):
        return P.op("vector", lambda e: e.scalar_tensor_tensor(out=out, in0=a, scalar=s, in1=b, op0=op0, op1=op1),
                    reads=reads, writes=writes)

    def ts(out, a, s1, s2, op0, op1, reads, writes, eng="vector"):
        return P.op(eng, lambda e: e.tensor_scalar(out=out, in0=a, scalar1=s1, scalar2=s2, op0=op0, op1=op1),
                    reads=reads, writes=writes)

    def act(out, a, func, reads, writes, scale=1.0, bias=None):
        if bias is None:
            return P.op("scalar", lambda e: e.activation(out=out, in_=a, func=func, scale=scale), reads=reads, writes=writes)
        return P.op("scalar", lambda e: e.activation(out=out, in_=a, func=func, scale=scale, bias=bias), reads=reads, writes=writes)

    def mm(out, lhsT, rhs, start, stop, reads, writes):
        return P.op("tensor", lambda e: e.matmul(out, lhsT=lhsT, rhs=rhs, start=start, stop=stop), reads=reads, writes=writes)

    stg_i = [0]

    cast_rot = [0]

    def wload(src, dst, dstbuf, shape3=None, engs=("gpsimd",)):
        i = stg_i[0] % 2
        stg_i[0] += 1
        n = 1
        for d in src.shape[1:]:
            n *= d
        sv = stg[i][:, 0:n]
        if len(src.shape) == 3:
            sv = sv.rearrange("p (a b) -> p a b", a=src.shape[1])
        P.dma(lambda e: e.dma_start(out=sv, in_=src), ("stg", i), writes=[bf(f"stg{i}")])
        ce = engs[cast_rot[0] % len(engs)]
        cast_rot[0] += 1
        if ce == "scalar":
            act(dst, sv, AF.Copy, [bf(f"stg{i}")], [dstbuf])
        else:
            P.op(ce, lambda e: e.tensor_copy(out=dst, in_=sv), reads=[bf(f"stg{i}")], writes=[dstbuf])

    psn = {}

    def psb(lo, hi):
        k = (lo, hi)
        psn[k] = psn.get(k, -1) + 1
        return lo + psn[k] % (hi - lo)

    rot = {}

    def rr(name, n):
        rot[name] = rot.get(name, -1) + 1
        return rot[name] % n

    def rstd_from(ps_ap, ncols, scale, reads, dst=None):
        if dst is None:
            i = rr("rstd", 2)
            r = rstd[i][:, 0:ncols]
            rb = bf(f"rstd{i}")
        else:
            r, rb = dst
        act(r, ps_ap, AF.Ln, reads, [rb], scale=scale, bias=epsT[:, 0:1])
        act(r, r, AF.Exp, [rb], [rb], scale=-0.5)
        return r, rb

    def group_norm_evac(ps_ap, psbuf, ncols, gain, dst, dstbufs, full, split=None, defer=False):
        i = rr("sq", 2)
        s = sq[i][:, 0:ncols]
        act(s, ps_ap, AF.Square, [psbuf], [bf(f"sq{i}")])
        if defer:
            return lambda: _gne2(i, s, ps_ap, psbuf, ncols, gain, dst, dstbufs, full, split)
        _gne2(i, s, ps_ap, psbuf, ncols, gain, dst, dstbufs, full, split)

    def _gne2(i, s, ps_ap, psbuf, ncols, gain, dst, dstbufs, full, split):
        b2 = psb(2, 4)
        mm(PS[b2][:, 0:ncols], ones_bf[:] if full else blk64[:], s, True, True, [bf(f"sq{i}"), bf("consts")], [bf(f"ps{b2}")])
        r, rb = rstd_from(PS[b2][:, 0:ncols], ncols, 1.0 / (128 if full else 64), [bf(f"ps{b2}")])
        if split is None:
            stt(dst, ps_ap, gain, r, ALU.mult, ALU.mult, [psbuf, rb, bf("cst")], dstbufs)
        else:
            lo_, hi_ = split
            stt(lo_[0:64, :], ps_ap[0:64, :], gain[0:64, :], r[0:64, :], ALU.mult, ALU.mult, [psbuf, rb, bf("cst")], dstbufs)
            stt(hi_[64:128, :], ps_ap[64:128, :], gain[64:128, :], r[64:128, :], ALU.mult, ALU.mult, [psbuf, rb, bf("cst")], dstbufs)

    zhi_sets = [[TMP[:, 1280 + 128 * m_:1408 + 128 * m_].bitcast(BF16) for m_ in range(6)],
                [SPARE[:, 4096 + 256 * m_:4352 + 256 * m_] for m_ in range(6)]]

    def ZHI(m_):
        return zhi_sets[PAR["p"]][m_]

    def xnorm(blk, gname, gl, dst, dstbuf, pre=None, only_p1=False):
        c0 = blk * BLK
        if pre is not None:
            r, rb = pre
            for k in range(8):
                stt(dst[:, k, :], xT[:, k, c0:c0 + BLK], cs(gname, gl * 8 + k), r, ALU.mult, ALU.mult,
                    [bf(f"x{k}_{blk}"), rb, bf("cst")], [dstbuf])
            return None
        b2 = psb(2, 4)
        for k in range(8):
            i = rr("sq", 2)
            act(sq[i][:], xT[:, k, c0:c0 + BLK], AF.Square, [bf(f"x{k}_{blk}")], [bf(f"sq{i}")])
            mm(PS[b2][:, 0:BLK], ones_bf[:], sq[i][:], k == 0, k == 7, [bf(f"sq{i}"), bf("consts")], [bf(f"ps{b2}")])
        if only_p1:
            return rstd_from(PS[b2][:, 0:BLK], BLK, 1.0 / 1024, [bf(f"ps{b2}")], dst=(rstdN[:, 0:BLK], bf("rstdN")))
        r, rb = rstd_from(PS[b2][:, 0:BLK], BLK, 1.0 / 1024, [bf(f"ps{b2}")])
        for k in range(8):
            stt(dst[:, k, :], xT[:, k, c0:c0 + BLK], cs(gname, gl * 8 + k), r, ALU.mult, ALU.mult,
                [bf(f"x{k}_{blk}"), rb, bf("cst")], [dstbuf])

    P.dma(lambda e: e.dma_start(out=cst[:], in_=cst_d[:, :]), "cstld", writes=[bf("cst")])
    V(lambda e: e.memset(ones_bf[:], 1.0), writes=[bf("consts")])
    V(lambda e: e.memset(blk64[:], 0.0), writes=[bf("consts")])
    V(lambda e: e.memset(blk64[0:64, 0:64], 1.0), writes=[bf("consts")])
    V(lambda e: e.memset(blk64[64:128, 64:128], 1.0), writes=[bf("consts")])
    V(lambda e: e.memset(epsT[:, 0:1], EPS), writes=[bf("consts")])
    V(lambda e: e.memset(epsT[:, 1:2], math.pi / 2), writes=[bf("consts")])
    xsrc = xT_d.rearrange("(k p) n -> p k n", p=128)
    for k in range(8):
        for hh in range(2):
            P.dma(lambda e, k=k, hh=hh: e.dma_start(out=xT[:, k, hh * 1024:(hh + 1) * 1024], in_=xsrc[:, k, hh * 1024:(hh + 1) * 1024]),
                  "xld", writes=[bf(f"x{k}_{b}") for b in range(hh * 4, hh * 4 + 4)])

    def lam_prep():
        lamv = cs("lam", 0, 512).rearrange("p (a j d) -> p a j d", a=4, j=2)
        tmp = TMP[:, 0:64]
        for j in range(2):
            layer = 2 + j
            linit = 0.8 - 0.6 * math.exp(-0.3 * layer)
            for a in range(2):
                tt(tmp, lamv[:, 2 * a, j, :], lamv[:, 2 * a + 1, j, :], ALU.mult, [bf("cst")], [bf("lamtmp")])
                V(lambda e, a=a, j=j: e.reduce_sum(out=small[:, 8 + a:9 + a], in_=tmp, axis=AX.X), [bf("lamtmp")], [bf("small")])
                act(small[:, 8 + a:9 + a], small[:, 8 + a:9 + a], AF.Exp, [bf("small")], [bf("small")])
            tt(small[:, 10:11], small[:, 9:10], small[:, 8:9], ALU.subtract, [bf("small")], [bf("small")])
            ts(small[:, j:j + 1], small[:, 10:11], -linit, None, ALU.add, ALU.bypass, [bf("small")], [bf("small")])
            ts(small[:, 2 + j:3 + j], cs("dsn", j), 1.0 - linit, None, ALU.mult, ALU.bypass, [bf("cst")], [bf("small")])

    lam_prep()

    def layer_start(l):
        wi = w_in_d[l].rearrange("(k p) m -> p k m", p=128)
        wo = w_out_d[l].rearrange("(k p) m -> p k m", p=128)
        for m in range(8):
            wload(wi[:, :, m * 128:(m + 1) * 128], w_in_bf[:, :, m * 128:(m + 1) * 128], bf("w_in"), engs=(("scalar", "gpsimd") if l < 2 else ("scalar", "vector", "gpsimd")))
        for m in range(8):
            wload(wo[:, :, m * 128:(m + 1) * 128], w_out_bf[:, :, m * 128:(m + 1) * 128], bf("w_out"), engs=(("scalar", "gpsimd") if l < 2 else ("scalar", "vector", "gpsimd")))
        msrc = memT_d.rearrange("(k p) n -> p k n", p=128)
        P.dma(lambda e: e.dma_start(out=memT_sb, in_=msrc), "memld", writes=[bf("memT")])
        wks = wk_d[l].rearrange("(k p) m -> p k m", p=128)
        wvs = wv_d[l].rearrange("(k p) m -> p k m", p=128)
        for j in range(2):
            wload(wks[:, :, j * 128:(j + 1) * 128], wk_bf[:, :, j * 128:(j + 1) * 128], bf("wk"), engs=(("scalar",) if l < 2 else ("scalar", "vector")))
            wload(wvs[:, :, j * 128:(j + 1) * 128], wv_bf[:, :, j * 128:(j + 1) * 128], bf("wv"), engs=(("scalar",) if l < 2 else ("scalar", "vector")))
        b2 = psb(2, 4)
        for k in range(8):
            i = rr("sq", 2)
            act(sq[i][:], memT_sb[:, k, :], AF.Square, [bf("memT")], [bf(f"sq{i}")])
            mm(PS[b2][:, 0:256], ones_bf[:], sq[i][:], k == 0, k == 7, [bf(f"sq{i}"), bf("consts")], [bf(f"ps{b2}")])
        r, rb = rstd_from(PS[b2][:, 0:256], 256, 1.0 / 1024, [bf(f"ps{b2}")])
        for k in range(8):
            stt(mem_n[:, k, :], memT_sb[:, k, :], cs("gmem", l * 8 + k), r, ALU.mult, ALU.mult,
                [bf("memT"), rb, bf("cst")], [bf("mem_n")])
        for j in range(2):
            b0 = psb(0, 2)
            for k in range(8):
                mm(PS[b0][:, 0:256], wk_bf[:, k, j * 128:(j + 1) * 128], mem_n[:, k, :], k == 0, k == 7,
                   [bf("wk"), bf("mem_n")], [bf(f"ps{b0}")])
            group_norm_evac(PS[b0][:, 0:256], bf(f"ps{b0}"), 256, cs("mkn", l), KmT[:, j, :], [bf("KmT")], False)
        for i2 in range(2):
            b0 = psb(0, 2)
            for k in range(8):
                mm(PS[b0][:, 0:256], mem_n[:, k, i2 * 128:(i2 + 1) * 128], wv_bf[:, k, :], k == 0, k == 7,
                   [bf("wv"), bf("mem_n")], [bf(f"ps{b0}")])
            act(Vm[:, i2, :], PS[b0][:, 0:256], AF.Copy, [bf(f"ps{b0}")], [bf("Vm")])

    def win_block(l, blk, pre=None):
        hb_, zb_ = HB(), ZB()
        hbuf = bf("h_blk" + sfx())
        xnorm(blk, "gmix", l, hb_, hbuf, pre=pre)
        for m in range(8):
            b0 = psb(0, 2)
            for k in range(8):
                mm(PS[b0][:, 0:BLK], w_in_bf[:, k, m * 128:(m + 1) * 128], hb_[:, k, :], k == 0, k == 7,
                   [bf("w_in"), hbuf], [bf(f"ps{b0}")])
            zbuf = bf(f"z{m}" + sfx())
            if m >= 6:
                group_norm_evac(PS[b0][:, 0:BLK], bf(f"ps{b0}"), BLK, cs("mqn", l), zb_[:, m, :], [zbuf], False)
            elif l >= 2:
                group_norm_evac(PS[b0][:, 0:BLK], bf(f"ps{b0}"), BLK, cs("dqn", l - 2), None, [zbuf], False,
                                split=(zb_[:, m, :], ZHI(m)))
            else:
                act(zb_[:, m, :], PS[b0][:, 0:BLK], AF.Copy, [bf(f"ps{b0}")], [zbuf])

    def memattn_block(l, blk):
        zb_, cm_ = ZB(), CATM()
        for h in range(4):
            j = h // 2
            po = (h % 2) * 64
            if PAR["mbank"] is None:
                nb, dbk = 4 + (h % 2), 6 + (h % 2)
            else:
                nb, dbk = PAR["mbank"]
            zbuf = bf(f"z{6 + j}" + sfx())
            eis = []
            for i2 in range(2):
                b0 = psb(0, 2)
                mm(PS[b0][:, 0:BLK], KmT[po:po + 64, j, i2 * 128:(i2 + 1) * 128], zb_[po:po + 64, 6 + j, :], True, True,
                   [bf("KmT"), zbuf], [bf(f"ps{b0}")])
                ei = rr("eM", 2)
                act(eM[ei][:], PS[b0][:, 0:BLK], AF.Exp, [bf(f"ps{b0}")], [bf(f"eM{ei}")], scale=0.125)
                eis.append(ei)
            for i2 in range(2):
                ei = eis[i2]
                mm(PS[nb][po:po + 64, 0:BLK], Vm[:, i2, h * 64:(h + 1) * 64], eM[ei][:], i2 == 0, i2 == 1,
                   [bf("Vm"), bf(f"eM{ei}")], [bf(f"ps{nb}")])
                mm(PS[dbk][po:po + 64, 0:BLK], ones_bf[:, 0:64], eM[ei][:], i2 == 0, i2 == 1,
                   [bf("consts"), bf(f"eM{ei}")], [bf(f"ps{dbk}")])
            V(lambda e, po=po, dbk=dbk: e.reciprocal(out=rrm[po:po + 64, :], in_=PS[dbk][po:po + 64, 0:BLK]), [bf(f"ps{dbk}")], [bf(f"rrm{po}")])
            tt(cm_[po:po + 64, j, :], PS[nb][po:po + 64, 0:BLK], rrm[po:po + 64, :], ALU.mult,
               [bf(f"ps{nb}"), bf(f"rrm{po}")], [bf(f"cat{6 + j}" + sfx())])

    def wout_block(l, blk):
        c0 = blk * BLK
        cm_ = CATM()
        for m in range(8):
            b0 = psb(0, 2)
            for k in range(8):
                rhs = cat_blk[:, k, :] if k < 6 else cm_[:, k - 6, :]
                cbuf = bf(f"cat{k}") if k < 6 else bf(f"cat{k}" + sfx())
                mm(PS[b0][:, 0:BLK], w_out_bf[:, k, m * 128:(m + 1) * 128], rhs, k == 0, k == 7,
                   [bf("w_out"), cbuf], [bf(f"ps{b0}")])
            tt(xT[:, m, c0:c0 + BLK], xT[:, m, c0:c0 + BLK], PS[b0][:, 0:BLK], ALU.add,
               [bf(f"ps{b0}"), bf(f"x{m}_{blk}")], [bf(f"x{m}_{blk}")])

    def ffn(l):
        P.barrier()
        gsrc = wg_d[l].rearrange("(k p) f -> p k f", p=128)
        usrc = wu_d[l].rearrange("(k p) f -> p k f", p=128)
        dsrc = wd_d[l].rearrange("(f p) m -> p f m", p=128)
        f = 0
        while f < NF:
            nfc = min(FC, NF - f)
            for fi in range(nfc):
                ff = f + fi
                gi = rr("wgb", 2)
                wload(gsrc[:, :, ff * 128:(ff + 1) * 128], wgb[gi][:], bf(f"wg{gi}"))
                wload(usrc[:, :, ff * 128:(ff + 1) * 128], wgb[2 + gi][:], bf(f"wu{gi}"))
                wload(dsrc[:, ff, :], wdb[:, fi, :], bf(f"wd{fi}"))
                for tb in range(4):
                    if ff == 0:
                        for blk in (2 * tb, 2 * tb + 1):
                            xnorm(blk, "gffn", l, hT[:, :, blk * BLK:(blk + 1) * BLK], bf(f"hT{blk}"))
                    bg = psb(0, 2)
                    bu = psb(2, 4)
                    for k in range(8):
                        mm(PS[bg][:], wgb[gi][:, k, :], hT[:, k, tb * 512:(tb + 1) * 512], k == 0, k == 7,
                           [bf(f"wg{gi}"), bf(f"hT{2 * tb}"), bf(f"hT{2 * tb + 1}")], [bf(f"ps{bg}")])
                    for k in range(8):
                        mm(PS[bu][:], wgb[2 + gi][:, k, :], hT[:, k, tb * 512:(tb + 1) * 512], k == 0, k == 7,
                           [bf(f"wu{gi}"), bf(f"hT{2 * tb}"), bf(f"hT{2 * tb + 1}")], [bf(f"ps{bu}")])
                    si = rr("sgt", 2)
                    act(sgt[si][:], PS[bg][:], AF.Silu, [bf(f"ps{bg}")], [bf(f"sgt{si}")])
                    tt(aT[:, fi, tb * 512:(tb + 1) * 512], sgt[si][:], PS[bu][:], ALU.mult,
                       [bf(f"sgt{si}"), bf(f"ps{bu}")], [bf(f"aT{fi}_{tb}")])
            for tb in range(4):
                for m in range(8):
                    bd = psb(4, 8)
                    for fi in range(nfc):
                        mm(PS[bd][:], wdb[:, fi, m * 128:(m + 1) * 128], aT[:, fi, tb * 512:(tb + 1) * 512], fi == 0, fi == nfc - 1,
                           [bf(f"wd{fi}"), bf(f"aT{fi}_{tb}")], [bf(f"ps{bd}")])
                    tt(xT[:, m, tb * 512:(tb + 1) * 512], xT[:, m, tb * 512:(tb + 1) * 512], PS[bd][:], ALU.add,
                       [bf(f"ps{bd}"), bf(f"x{m}_{2 * tb}"), bf(f"x{m}_{2 * tb + 1}")], [bf(f"x{m}_{2 * tb}"), bf(f"x{m}_{2 * tb + 1}")])
            f += nfc
        P.barrier()

    def kv_shared():
        for blk in range(NBLK):
            xnorm(blk, "gkv", 0, hT[:, :, blk * BLK:(blk + 1) * BLK], bf(f"hT{blk // 2}"))
        ksrc = dwk_d.rearrange("(k p) f -> p k f", p=128)
        vsrc = dwv_d.rearrange("(k p) f -> p k f", p=128)
        for t in range(6):
            gi = rr("wgb", 2)
            wload(ksrc[:, :, t * 128:(t + 1) * 128], wgb[gi][:], bf(f"wg{gi}"), engs=("scalar", "gpsimd"))
            wload(vsrc[:, :, t * 128:(t + 1) * 128], wgb[2 + gi][:], bf(f"wu{gi}"), engs=("scalar", "gpsimd"))
            for blk in range(NBLK):
                b0 = psb(0, 2)
                for k in range(8):
                    mm(PS[b0][:, 0:BLK], wgb[gi][:, k, :], hT[:, k, blk * BLK:(blk + 1) * BLK], k == 0, k == 7,
                       [bf(f"wg{gi}"), bf(f"hT{blk // 2}")], [bf(f"ps{b0}")])
                group_norm_evac(PS[b0][:, 0:BLK], bf(f"ps{b0}"), BLK, cs("dkn", 0), KdT[:, t, blk * BLK:(blk + 1) * BLK],
                                [bf(f"KdT{blk}")], False)
                for tt_ in (2 * blk, 2 * blk + 1):
                    b4 = psb(4, 8)
                    for k in range(8):
                        mm(PS[b4][:, 0:128], hT[:, k, tt_ * 128:(tt_ + 1) * 128], wgb[2 + gi][:, k, :], k == 0, k == 7,
                           [bf(f"wu{gi}"), bf(f"hT{tt_ // 4}")], [bf(f"ps{b4}")])
                    V(lambda e, tt_=tt_, t=t, b4=b4: e.tensor_copy(out=Vd[:, tt_, t * 128:(t + 1) * 128], in_=PS[b4][:, 0:128]),
                      [bf(f"ps{b4}")], [bf(f"Vd{tt_}")])
        P.barrier()

    def diff_block(l, blk, hook=None):
        j = l - 2
        zb_ = ZB()
        zhi_ = [ZHI(m_) for m_ in range(6)]
        sf_ = sfx()
        TF = TMP
        r0 = TF[:, 0:256]
        r1 = TF[:, 256:512]
        t0 = TF[:, 512:768]
        t1 = TF[:, 768:1024]
        eT = [TF[:, 1024:1152].bitcast(BF16), TF[:, 1152:1280].bitcast(BF16)]
        nkt = 2 * blk + 2

        def loops(h, c):
            hp = h % 2
            idx = c * 6 + h
            zt = idx // 2
            po = (idx % 2) * 64
            ob, db = 4 + hp, 6 + hp
            pendq = []
            Es = [sgt[0], sgt[1], TF[:, 1024:1280].bitcast(BF16)]
            Eb = [bf("sgt0"), bf("sgt1"), bf("e3")]

            def pv(unit, ei):
                for ik, kt in enumerate(unit):
                    n0 = max(0, kt - 2 * blk) * 128
                    N = BLK - n0
                    rhs = Es[ei][:, ik * 256:ik * 256 + N]
                    mm(PS[ob][:, c * 256 + n0:c * 256 + BLK], Vd[:, kt, h * 128:(h + 1) * 128], rhs, kt == 0, kt == nkt - 1,
                       [bf(f"Vd{kt}"), Eb[ei]], [bf(f"ps{ob}")])
                    mm(PS[db][:, c * 256 + n0:c * 256 + BLK], ones_bf[:], rhs, kt == 0, kt == nkt - 1,
                       [bf("consts"), Eb[ei]], [bf(f"ps{db}")])

            units = [(2 * p_, 2 * p_ + 1) for p_ in range(blk)] + [(2 * blk, 2 * blk + 1)]
            zq = zb_[:, zt, :] if po == 0 else zhi_[zt]
            for unit in units:
                diag = unit[0] >= 2 * blk
                b0 = psb(0, 2)
                for ik, kt in enumerate(unit):
                    n0 = max(0, kt - 2 * blk) * 128
                    N = BLK - n0
                    mm(PS[b0][:, ik * 256:ik * 256 + N], KdT[:, zt, kt * 128:(kt + 1) * 128], zq[:, n0:BLK], True, True,
                       [bf(f"KdT{kt // 2}"), bf(f"z{zt}" + sf_)], [bf(f"ps{b0}")])
                W = 384 if diag else 512
                ei = rr("eT3", 3)
                act(Es[ei][:, 0:W], PS[b0][:, 0:W], AF.Exp, [bf(f"ps{b0}")], [Eb[ei]], scale=0.125)
                if diag:
                    P.op("gpsimd", lambda e, ei=ei: e.memset(Es[ei][64:128, 0:64], 0.0), writes=[Eb[ei]])
                    P.op("gpsimd", lambda e, ei=ei: e.memset(Es[ei][64:128, 256:320], 0.0), writes=[Eb[ei]])
                pendq.append((unit, ei))
                if len(pendq) > 2:
                    pv(*pendq.pop(0))
            while pendq:
                pv(*pendq.pop(0))

        st = {}

        def postA(h):
            hp = h % 2
            ob, db = 4 + hp, 6 + hp
            V(lambda e: e.reciprocal(out=r0, in_=PS[db][:, 0:256]), [bf(f"ps{db}")], [bf("r0")])
            V(lambda e: e.reciprocal(out=r1, in_=PS[db][:, 256:512]), [bf(f"ps{db}")], [bf("r1")])
            tt(t0, PS[ob][:, 0:256], r0, ALU.mult, [bf(f"ps{ob}"), bf("r0")], [bf("t0")])
            tt(t1, PS[ob][:, 256:512], r1, ALU.mult, [bf(f"ps{ob}"), bf("r1")], [bf("t1")])
            stt(t0, t1, small[:, j:j + 1], t0, ALU.mult, ALU.add, [bf("t0"), bf("t1"), bf("small")], [bf("t0")])

        def postM(h):
            i = rr("sq", 2)
            act(sq[i][:], t0, AF.Square, [bf("t0")], [bf(f"sq{i}")])
            b2 = psb(2, 4)
            mm(PS[b2][:, 0:BLK], ones_bf[:], sq[i][:], True, True, [bf(f"sq{i}"), bf("consts")], [bf(f"ps{b2}")])
            st["b2"] = b2

        def postB(h):
            b2 = st["b2"]
            r, rb = rstd_from(PS[b2][:, 0:BLK], BLK, 1.0 / 128, [bf(f"ps{b2}")])
            stt(cat_blk[:, h, :], t0, small[:, 2 + j:3 + j], r, ALU.mult, ALU.mult, [bf("t0"), rb, bf("small")], [bf(f"cat{h}")])

        loops(0, 0)
        loops(0, 1)
        for h in range(6):
            postA(h)
            if h < 5:
                loops(h + 1, 0)
            postM(h)
            if h < 5:
                loops(h + 1, 1)
            postB(h)
            if h == 2 and hook is not None:
                hook()

    SPF = SPARE.bitcast(F32)
    rho = TMP[:, 1200:1224]
    cs4 = TMP[:, 1224:1248]
    sn4 = TMP[:, 1248:1272]
    Sc = small[:, 16:64].rearrange("p (a r) -> p a r", r=2)

    def cmul(dr, di, ar, ai, br, bi, t1, t2, R, W):
        tt(t1, ar, br, ALU.mult, R, [bf("s5t")])
        tt(t2, ai, bi, ALU.mult, R, [bf("s5t")])
        tt(dr, t1, t2, ALU.subtract, [bf("s5t")], W)
        tt(t1, ar, bi, ALU.mult, R, [bf("s5t")])
        tt(t2, ai, br, ALU.mult, R, [bf("s5t")])
        tt(di, t1, t2, ALU.add, [bf("s5t")], W)

    def double_angle(c, s, t1, t2, n):
        for _ in range(n):
            tt(t1, c, c, ALU.mult, [bf("s5p")], [bf("s5t")])
            tt(t2, s, s, ALU.mult, [bf("s5p")], [bf("s5t")])
            stt(s, c, 2.0, s, ALU.mult, ALU.mult, [bf("s5p")], [bf("s5p")])
            tt(c, t1, t2, ALU.subtract, [bf("s5t")], [bf("s5p")])

    def s5_prep_F(l):
        i = l
        pb = [bf("s5p")]
        tb_ = [bf("s5t")]
        dtF = TMP[:, 1300:1306]
        dt64 = TMP[:, 1306:1312]
        P.dma(lambda e: e.dma_start(out=dtF, in_=s5dtF_d[i]), "s5ld", writes=pb)
        act(dtF, dtF, AF.Exp, pb, pb)
        ts(dt64, dtF, 1.0 / 64, None, ALU.mult, ALU.bypass, pb, pb)
        MIXF = MIX[:, 6144:24576].bitcast(F32)
        sl = [MIXF[:, 768 * q:768 * (q + 1)].rearrange("p (t n) -> p t n", t=6) for q in range(12)]
        A, I, mag, c, s, t1, t2, nr, cr, ci, Wr, Wi = sl
        rden, Br, Bi, Wr2, Wi2 = mag, A, I, mag, nr
        dtb = dtF.unsqueeze(2).to_broadcast([128, 6, 128])
        dt64b = dt64.unsqueeze(2).to_broadcast([128, 6, 128])
        P.dma(lambda e: e.dma_start(out=A, in_=s5F_d[i][:, :, 0, :]), "s5ld", writes=pb)
        P.dma(lambda e: e.dma_start(out=I, in_=s5F_d[i][:, :, 1, :]), "s5ld", writes=pb)
        tt(t1, A, dtb, ALU.mult, pb, tb_)
        act(mag, t1, AF.Exp, tb_, pb)
        tt(t1, I, dt64b, ALU.mult, pb, tb_)
        act(s, t1, AF.Sin, tb_, pb)
        act(c, t1, AF.Sin, tb_, pb, bias=epsT[:, 1:2])
        double_angle(c, s, t1, t2, 6)
        tt(c, mag, c, ALU.mult, pb, pb)
        tt(s, mag, s, ALU.mult, pb, pb)
        tt(t1, A, A, ALU.mult, pb, tb_)
        tt(t2, I, I, ALU.mult, pb, tb_)
        tt(t1, t1, t2, ALU.add, tb_, tb_)
        V(lambda e: e.reciprocal(out=rden, in_=t1), tb_, pb)
        ts(nr, c, -1.0, None, ALU.add, ALU.bypass, pb, pb)
        tt(t1, nr, A, ALU.mult, pb, tb_)
        tt(t2, s, I, ALU.mult, pb, tb_)
        tt(t1, t1, t2, ALU.add, tb_, tb_)
        tt(cr, t1, rden, ALU.mult, tb_ + pb, pb)
        tt(t1, s, A, ALU.mult, pb, tb_)
        tt(t2, nr, I, ALU.mult, pb, tb_)
        tt(t1, t1, t2, ALU.subtract, tb_, tb_)
        tt(ci, t1, rden, ALU.mult, tb_ + pb, pb)
        P.dma(lambda e: e.dma_start(out=Br, in_=s5F_d[i][:, :, 2, :]), "s5ld", reads=tb_, writes=pb)
        P.dma(lambda e: e.dma_start(out=Bi, in_=s5F_d[i][:, :, 3, :]), "s5ld", reads=tb_, writes=pb)
        cmul(Wr, Wi, cr, ci, Br, Bi, t1, t2, pb, pb)
        for k in range(4):
            V(lambda e, k=k, Wr=Wr: e.tensor_copy(out=WB[:, :, k, 0, :], in_=Wr), pb, [bf("WB")])
            V(lambda e, k=k, Wi=Wi: e.tensor_copy(out=WB[:, :, k, 1, :], in_=Wi), pb, [bf("WB")])
            if k < 3:
                cmul(Wr2, Wi2, c, s, Wr, Wi, t1, t2, pb, pb)
                Wr, Wr2 = Wr2, Wr
                Wi, Wi2 = Wi2, Wi
        V(lambda e: e.memset(small[:, 12:13], 0.0), pb + tb_, [bf("CA"), bf("Rtab"), bf("wglu"), bf("small12")])

    def s5_prep_M(l):
        i = l
        pb = [bf("s5p")]
        tb_ = [bf("s5t")]
        inM = TMP[:, 1400:1472].rearrange("p (a n) -> p a n", a=3)
        P.dma(lambda e: e.dma_start(out=inM, in_=s5M_d[i]), "s5ld", writes=pb)
        Am, Im, dtm = inM[:, 0, :], inM[:, 1, :], inM[:, 2, :]
        mt = [TMP[:, 1480 + 24 * q:1504 + 24 * q] for q in range(16)]
        xm, cm, sm, m1, m2, magm = mt[0:6]
        pr = [None, mt[6], mt[8], mt[10], mt[12]]
        pi_ = [None, mt[7], mt[9], mt[11], mt[13]]
        act(dtm, dtm, AF.Exp, pb, pb)
        tt(xm, Am, dtm, ALU.mult, pb, pb)
        act(rho, xm, AF.Exp, pb, pb, scale=4.0)
        act(magm, xm, AF.Exp, pb, pb)
        tt(xm, Im, dtm, ALU.mult, pb, pb)
        act(sm, xm, AF.Sin, pb, pb, scale=1.0 / 64)
        act(cm, xm, AF.Sin, pb, pb, scale=1.0 / 64, bias=epsT[:, 1:2])
        double_angle(cm, sm, m1, m2, 6)
        tt(pr[1], magm, cm, ALU.mult, pb, pb)
        tt(pi_[1], magm, sm, ALU.mult, pb, pb)
        for e_ in range(2, 5):
            cmul(pr[e_], pi_[e_], pr[e_ - 1], pi_[e_ - 1], pr[1], pi_[1], m1, m2, pb, pb)
        double_angle(cm, sm, m1, m2, 2)
        V(lambda e: e.tensor_copy(out=cs4, in_=cm), pb, pb)
        V(lambda e: e.tensor_copy(out=sn4, in_=sm), pb, pb)
        Cin = SPF[:, 0:1536].rearrange("p (r a n) -> p r a n", r=2, a=24)
        P.dma(lambda e: e.dma_start(out=Cin, in_=s5C_d[i]), "s5ld", writes=pb + tb_ + [bf("memT"), bf("mem_n")])
        Cr, Ci = Cin[:, 0], Cin[:, 1]
        c1 = SPF[:, 1536:2304].rearrange("p (a n) -> p a n", a=24)
        c2 = SPF[:, 2304:3072].rearrange("p (a n) -> p a n", a=24)
        V(lambda e: e.tensor_copy(out=CA[:, :, 0, 0, :], in_=Cr), pb, [bf("CA")])
        ts(CA[:, :, 0, 1, :], Ci, -1.0, None, ALU.mult, ALU.bypass, pb, [bf("CA")])
        for e_ in range(1, 5):
            prb = pr[e_].unsqueeze(2).to_broadcast([128, 24, 32])
            pib = pi_[e_].unsqueeze(2).to_broadcast([128, 24, 32])
            tt(c1, Cr, prb, ALU.mult, pb, tb_)
            tt(c2, Ci, pib, ALU.mult, pb, tb_)
            tt(CA[:, :, e_, 0, :], c1, c2, ALU.subtract, tb_, [bf("CA")])
            tt(c1, Cr, pib, ALU.mult, pb, tb_)
            tt(c2, Ci, prb, ALU.mult, pb, tb_)
            tt(c1, c1, c2, ALU.add, tb_, tb_)
            ts(CA[:, :, e_, 1, :], c1, -1.0, None, ALU.mult, ALU.bypass, tb_, [bf("CA")])
        Rr, Ri = Rtab[:, 0], Rtab[:, 1]
        V(lambda e: e.memset(Rr[:, :, 0:1], 1.0), writes=[bf("Rtab")])
        V(lambda e: e.memset(Ri[:, :, 0:1], 0.0), writes=[bf("Rtab")])
        qr, qi = cm, sm
        n = 1
        while n < 64:
            qrb = qr.unsqueeze(2).to_broadcast([128, 24, n])
            qib = qi.unsqueeze(2).to_broadcast([128, 24, n])
            a1 = c1[:, :, 0:n]
            a2 = c2[:, :, 0:n]
            tt(a1, Rr[:, :, 0:n], qrb, ALU.mult, pb + [bf("Rtab")], tb_)
            tt(a2, Ri[:, :, 0:n], qib, ALU.mult, pb + [bf("Rtab")], tb_)
            tt(Rr[:, :, n:2 * n], a1, a2, ALU.subtract, tb_, [bf("Rtab")])
            tt(a1, Rr[:, :, 0:n], qib, ALU.mult, pb + [bf("Rtab")], tb_)
            tt(a2, Ri[:, :, 0:n], qrb, ALU.mult, pb + [bf("Rtab")], tb_)
            tt(Ri[:, :, n:2 * n], a1, a2, ALU.add, tb_, [bf("Rtab")])
            double_angle(qr, qi, m1, m2, 1)
            n *= 2
        V(lambda e: e.memset(small[:, 16:64], 0.0), writes=[bf("Sc")])
        V(lambda e: e.memset(TMP[:, 1992:2312], 0.0), writes=[bf("Z3")])

        gsrc = wglu_d[i].rearrange("(k p) m -> p k m", p=128)
        for m in range(6):
            wload(gsrc[:, :, m * 128:(m + 1) * 128], wglu_bf[:, :, m * 128:(m + 1) * 128], bf("wglu"))
        P.op("gpsimd", lambda e: e.memset(sgt[0][:], 0.0), writes=[bf("sgt0")])
        P.op("gpsimd", lambda e: e.memset(sgt[1][:], 0.0), writes=[bf("sgt1")])
        P.op("gpsimd", lambda e: e.memset(stg[0][:], 0.0), writes=[bf("stg0")])
        P.op("gpsimd", lambda e: e.memset(wgb[1][:], 0.0), writes=[bf("wg1")])

    def s5_block(l, blk):
        i = l
        L2 = [SPF[:, 1536 + 256 * q:1792 + 256 * q].rearrange("p (a c) -> p a c", a=4) for q in range(6)]
        t1, t2, Xr, Xi, Vr, Vi = L2
        Sf = TMP[:, 0:520].rearrange("p (r a c) -> p r a c", r=2, a=4)
        Spb = TMP[:, 520:776].bitcast(BF16).rearrange("p (a r c) -> p a r c", a=4, r=2)
        yv = TMP[:, 776:1032]
        sgS = TMP[:, 1032:1160].bitcast(BF16)
        ini = TMP[:, 1160:1176].rearrange("p (a q) -> p a q", a=4)
        hS = h_blk
        stg0b = stg[0][:].bitcast(BF16)
        wflat = [w_[:].rearrange("p k n -> p (k n)") for w_ in wgb]
        um_set = [[sgt[0][:, 0:256], sgt[0][:, 256:512], sgt[1][:, 0:256], sgt[1][:, 256:512]],
                  [stg0b[:, q * 256:(q + 1) * 256] for q in range(4)]]
        um_buf = [[bf("sgt0"), bf("sgt0"), bf("sgt1"), bf("sgt1")], [bf("stg0")] * 4]
        Dsb_set = [SPF[:, 1024:1536], wflat[0].bitcast(F32)]
        Dsb_buf = [bf("Dsb"), bf("wg0")]
        sh_set = [[SPARE[:, q * 256:(q + 1) * 256] for q in range(8)],
                  [wflat[2][:, q * 256:(q + 1) * 256] for q in range(4)] + [wflat[3][:, q * 256:(q + 1) * 256] for q in range(4)]]
        sh_buf = [[bf(f"sh{q}") for q in range(8)], [bf("wu0")] * 4 + [bf("wu1")] * 4]
        Z3_set = [TMP[:, 1992:2312].bitcast(BF16).rearrange("p (e r n) -> p e r n", e=5, r=2),
                  wflat[1][:, 0:640].rearrange("p (e r n) -> p e r n", e=5, r=2)]
        Z3_buf = [bf("Z3"), bf("wg1")]
        bDs = {}

        def stageA(tau):
            pb_ = tau % 2
            um, ub = um_set[pb_], um_buf[pb_]
            um4 = [u_.rearrange("p (c j) -> p c j", j=4) for u_ in um]
            zb = bf(f"z{tau}")
            for q in range(3):
                act(um[q][q * 32:(q + 1) * 32, :], z_blk[q * 32:(q + 1) * 32, tau, :], AF.Copy, [zb], [ub[q]])
            act(um[3][64:128, :], z_blk[64:128, tau, :], AF.Copy, [zb], [ub[3]])
            P.op("gpsimd", lambda e: e.memset(um[3][64:96, :], 0.0), writes=[ub[3]])
            P.op("gpsimd", lambda e: e.tensor_copy(out=Z3_set[pb_][:, :, :, 32:64], in_=CA[:, 4 * tau + 3, :, :, :]), reads=[bf("CA")], writes=[Z3_buf[pb_]])
            bD = psb(2, 4)
            for q in range(4):
                for ri in range(2):
                    slot = q * 2 + ri
                    for t in range(4):
                        mm(PS[bD][:, slot * 64:(slot + 1) * 64], WB[:, tau, 3 - t, ri, :],
                           um4[q][:, :, t], t == 0, t == 3, [bf("WB"), ub[q]], [bf(f"ps{bD}")])
            act(Dsb_set[pb_], PS[bD][:, 0:512], AF.Copy, [bf(f"ps{bD}")], [Dsb_buf[pb_]])
            for q in range(4):
                for ri in range(2):
                    slot = q * 2 + ri
                    b0 = slot % 2
                    half = (slot // 2) % 2
                    pv = PS[b0][:, half * 256:(half + 1) * 256]
                    pv4 = pv.rearrange("p (c j) -> p c j", j=4)
                    pbuf = bf(f"ps{b0}")
                    for k in range(4):
                        mm(pv4[:, :, k:4], WB[:, tau, k, ri, :], um4[q][:, :, 0:4 - k],
                           k == 0, k == 3, [bf("WB"), ub[q]], [pbuf])
                    act(sh_set[pb_][slot], pv, AF.Copy, [pbuf], [sh_buf[pb_][slot]])

        def stageB(tau):
            pb_ = tau % 2
            Dsb = Dsb_set[pb_].rearrange("p (q r c) -> p q r c", q=4, r=2)
            Dr, Di = Dsb[:, :, 0, :], Dsb[:, :, 1, :]
            Rr, Ri = Rtab[:, 0, 4 * tau:4 * tau + 4, :], Rtab[:, 1, 4 * tau:4 * tau + 4, :]
            db_ = Dsb_buf[pb_]
            Bt1, Bt2, BXr, BXi, BVr, BVi = bf("L2t1"), bf("L2t2"), bf("L2Xr"), bf("L2Xi"), bf("L2Vr"), bf("L2Vi")
            RT = bf("Rtab")
            tt(t1, Rr, Dr, ALU.mult, [RT, db_], [Bt1])
            tt(t2, Ri, Di, ALU.mult, [RT, db_], [Bt2])
            tt(Vr, Rr, Di, ALU.mult, [RT, db_], [BVr])
            tt(Vi, Ri, Dr, ALU.mult, [RT, db_], [BVi])
            scr, sci = Sc[:, 4 * tau:4 * tau + 4, 0], Sc[:, 4 * tau:4 * tau + 4, 1]
            c4, s4 = cs4[:, 4 * tau:4 * tau + 4], sn4[:, 4 * tau:4 * tau + 4]
            ir, ii, ta, tb2 = ini[:, 0, :], ini[:, 1, :], ini[:, 2, :], ini[:, 3, :]
            tc, td = TMP[:, 1176:1180], TMP[:, 1180:1184]
            sp_ = [bf("Sc"), bf("s5p")]
            tt(ta, c4, scr, ALU.mult, sp_, [bf("ita")])
            tt(tb2, s4, sci, ALU.mult, sp_, [bf("itb")])
            tt(tc, s4, scr, ALU.mult, sp_, [bf("itc")])
            tt(td, c4, sci, ALU.mult, sp_, [bf("itd")])
            tt(Xr, t1, t2, ALU.add, [Bt1, Bt2], [BXr])
            tt(Xi, Vr, Vi, ALU.subtract, [BVr, BVi], [BXi])
            tt(ir, ta, tb2, ALU.subtract, [bf("ita"), bf("itb")], [bf("iir")])
            tt(ii, tc, td, ALU.add, [bf("itc"), bf("itd")], [bf("iii")])
            for q in range(4):
                pr_ = 4 * tau + q
                rbc = rho[:, pr_:pr_ + 1].to_broadcast([128, 64])
                V(lambda e, q=q, rbc=rbc: e.tensor_tensor_scan(out=Vr[:, q, :], data0=rbc, data1=Xr[:, q, :], initial=ir[:, q:q + 1],
                                                                op0=ALU.mult, op1=ALU.add), [BXr, bf("iir"), bf("s5p")], [BVr])
            for q in range(4):
                pr_ = 4 * tau + q
                rbc = rho[:, pr_:pr_ + 1].to_broadcast([128, 64])
                V(lambda e, q=q, rbc=rbc: e.tensor_tensor_scan(out=Vi[:, q, :], data0=rbc, data1=Xi[:, q, :], initial=ii[:, q:q + 1],
                                                                op0=ALU.mult, op1=ALU.add), [BXi, bf("iii"), bf("s5p")], [BVi])
            V(lambda e: e.tensor_copy(out=Sf[:, 0, :, 0], in_=scr), [bf("Sc")], [bf("Sf0")])
            V(lambda e: e.tensor_copy(out=Sf[:, 1, :, 0], in_=sci), [bf("Sc")], [bf("Sf1")])
            tt(t1, Rr, Vr, ALU.mult, [RT, BVr], [Bt1])
            tt(t2, Ri, Vi, ALU.mult, [RT, BVi], [Bt2])
            tt(Xr, Rr, Vi, ALU.mult, [RT, BVi], [BXr])
            tt(Xi, Ri, Vr, ALU.mult, [RT, BVr], [BXi])
            tt(Sf[:, 0, :, 1:65], t1, t2, ALU.subtract, [Bt1, Bt2], [bf("Sf0")])
            tt(Sf[:, 1, :, 1:65], Xr, Xi, ALU.add, [BXr, BXi], [bf("Sf1")])
            act(Spb[:, :, 0, :], Sf[:, 0, :, 0:64], AF.Copy, [bf("Sf0")], [bf("Spb")])
            act(Spb[:, :, 1, :], Sf[:, 1, :, 0:64], AF.Copy, [bf("Sf1")], [bf("Spb")])
            V(lambda e: e.tensor_copy(out=scr, in_=Sf[:, 0, :, 64]), [bf("Sf0")], [bf("Sc")])
            V(lambda e: e.tensor_copy(out=sci, in_=Sf[:, 1, :, 64]), [bf("Sf1")], [bf("Sc")])

        def stageC(tau):
            pb_ = tau % 2
            sh, shb = sh_set[pb_], sh_buf[pb_]
            Z3v = Z3_set[pb_]
            by = 4 + (tau % 2)
            for q in (3, 2, 0, 1):
                pr_ = 4 * tau + q
                if q == 3:
                    out = PS[by][64:128, 0:256]
                    lw = lambda e_, r_: Z3v[:, e_, r_, :]
                    wb_ = [Z3_buf[pb_]]
                else:
                    out = PS[by][q * 32:(q + 1) * 32, 0:256]
                    lw = lambda e_, r_, pr_=pr_: CA[:, pr_, e_, r_, :]
                    wb_ = [bf("CA")]
                out4 = out.rearrange("p (c j) -> p c j", j=4)
                first = (q != 2)
                mm(out, lw(0, 0), sh[q * 2], first, False, wb_ + [shb[q * 2]], [bf(f"ps{by}")])
                mm(out, lw(0, 1), sh[q * 2 + 1], False, False, wb_ + [shb[q * 2 + 1]], [bf(f"ps{by}")])
                for j in range(4):
                    mm(out4[:, :, j], lw(j + 1, 0), Spb[:, q, 0, :], False, False, wb_ + [bf("Spb")], [bf(f"ps{by}")])
                    mm(out4[:, :, j], lw(j + 1, 1), Spb[:, q, 1, :], False, j == 3, wb_ + [bf("Spb")], [bf(f"ps{by}")])

        def stageC2(tau):
            by = 4 + (tau % 2)
            stt(yv, z_blk[:, tau, :], cs("dF", i * 6 + tau), PS[by][:, 0:256], ALU.mult, ALU.add, [bf(f"z{tau}"), bf(f"ps{by}"), bf("cst")], [bf("yv")])
            act(hS[:, tau, :], yv, AF.Gelu_apprx_tanh, [bf("yv")], [bf("h_blk")])

        stageA(0)
        stageA(1)
        stageB(0)
        stageC(0)
        for tau in range(1, 6):
            if tau < 5:
                stageA(tau + 1)
            stageB(tau)
            stageC2(tau - 1)
            stageC(tau)
        stageC2(5)
        for m in range(6):
            bg = 6 + (m % 2)
            for k in range(6):
                mm(PS[bg][:, 0:256], wglu_bf[:, k, m * 128:(m + 1) * 128], hS[:, k, :], k == 0, k == 5,
                   [bf("wglu"), bf("h_blk")], [bf(f"ps{bg}")])
            act(sgS, PS[bg][:, 0:256], AF.Sigmoid, [bf(f"ps{bg}")], [bf("sgS")])
            tt(cat_blk[:, m, :], hS[:, m, :], sgS, ALU.mult, [bf("h_blk"), bf("sgS")], [bf(f"cat{m}")])

    import os
    STG = int(os.environ.get("KSTAGE", "99"))
    for l in range(n_layers):
        if l < 2:
            s5_prep_F(l)
        layer_start(l)
        if l < 2:
            s5_prep_M(l)
        P.barrier()
        if l >= 2:
            for p_ in range(2):
                zs_ = z_sets[p_]
                P.op("gpsimd", lambda e, zs_=zs_: e.memset(zs_[64:128, 0:6, :], 0.0), writes=[bf(f"z{m_}" + ("" if p_ == 0 else "_1")) for m_ in range(6)])
            P.op("gpsimd", lambda e: e.memset(TMP[0:64, 1280:2048], 0.0), writes=[bf(f"z{m_}") for m_ in range(6)])
            P.op("gpsimd", lambda e: e.memset(SPARE[0:64, 4096:5632], 0.0), writes=[bf(f"z{m_}_1") for m_ in range(6)])
            PAR["p"] = 0
            PAR["mbank"] = None
            win_block(l, 0)
            memattn_block(l, 0)
            for blk in range(NBLK):
                def hook(blk=blk):
                    if blk + 1 < NBLK:
                        PAR["p"] = (blk + 1) % 2
                        PAR["mbank"] = (2, 3)
                        win_block(l, blk + 1)
                        memattn_block(l, blk + 1)
                        PAR["p"] = blk % 2
                PAR["p"] = blk % 2
                diff_block(l, blk, hook)
                wout_block(l, blk)
            PAR["p"] = 0
            PAR["mbank"] = None
        else:
            pre = None
            for blk in range(NBLK):
                win_block(l, blk, pre)
                memattn_block(l, blk)
                pre = xnorm(blk + 1, "gmix", l, None, None, only_p1=True) if blk + 1 < NBLK else None
                s5_block(l, blk)
                wout_block(l, blk)
        if STG >= 9:
            ffn(l)
        if l == 1 and n_layers > 2:
            kv_shared()
    P.barrier()
    ysrc = yT_d.rearrange("(k p) n -> p k n", p=128)
    for k in range(8):
        P.dma(lambda e, k=k: e.dma_start(out=ysrc[:, k, :], in_=xT[:, k, :]), "yst", reads=[bf(f"x{k}_{b}") for b in range(8)])
    P.rec["sync"].append(([(P.sems["yst"], P.dma_cnt["yst"])], None, None, 0))
    return P.finish()


_NC_CACHE = {}


def _prep_inputs(inp, n_cores=8):
    f32 = np.float32
    g = lambda k: np.asarray(inp[k], dtype=f32)
    cst = np.zeros((128, NCST), f32)

    def put(name, arr):
        o, w = CST_OFF[name]
        assert arr.shape == (128, w), (name, arr.shape)
        cst[:, o:o + w] = arr

    fm = lambda a: a.reshape(a.shape[0], 8, 128).transpose(2, 0, 1).reshape(128, -1)
    put("gmix", fm(g("norm_mix")))
    put("gffn", fm(g("norm_ffn")))
    put("gmem", fm(g("norm_mem")))
    put("gkv", fm(g("kv_norm")[None]))
    p64 = np.arange(128) % 64
    put("mqn", g("mem_q_norm")[:, p64].T)
    put("mkn", g("mem_k_norm")[:, p64].T)
    put("dqn", g("diff_q_norm")[:, p64].T)
    put("dkn", g("diff_k_norm")[p64][:, None])
    put("dsn", g("diff_sub_norm").T)
    lam = np.stack([g("diff_lambda_q1"), g("diff_lambda_k1"), g("diff_lambda_q2"), g("diff_lambda_k2")])
    put("lam", np.broadcast_to(lam.reshape(1, 512), (128, 512)))
    put("dF", g("ssm_d").reshape(2, 6, 128).transpose(2, 0, 1).reshape(128, 12))
    G, N, Pp = 48, 64, 16
    a_re, a_im, ldt = g("ssm_a_re"), g("ssm_a_im"), g("ssm_log_dt")
    b_re, b_im, c_re, c_im = g("ssm_b_re"), g("ssm_b_im"), g("ssm_c_re"), g("ssm_c_im")
    s5F = np.zeros((2, 128, 6, 4, 128), f32)
    s5dtF = np.zeros((2, 128, 6), f32)
    s5M = np.zeros((2, 128, 3, 24), f32)
    s5C = np.zeros((2, 128, 2, 24, 32), f32)
    for i in range(2):
        for gg in range(G):
            tau, gl = gg // 8, gg % 8
            rows = slice(gl * 16, gl * 16 + 16)
            half = gg % 2
            cols = slice(half * 64, half * 64 + 64)
            for hh in range(2):
                s5F[i, rows, tau, 0, hh * 64:(hh + 1) * 64] = a_re[i, gg][None, :]
                s5F[i, rows, tau, 1, hh * 64:(hh + 1) * 64] = a_im[i, gg][None, :]
            s5F[i, rows, tau, 2, cols] = b_re[i, gg].T
            s5F[i, rows, tau, 3, cols] = b_im[i, gg].T
            s5dtF[i, rows, tau] = ldt[i, gg]
            pr = gg // 2
            mrows = slice(half * 64, half * 64 + 64)
            s5M[i, mrows, 0, pr] = a_re[i, gg]
            s5M[i, mrows, 1, pr] = a_im[i, gg]
            s5M[i, mrows, 2, pr] = ldt[i, gg]
            s5C[i, mrows, 0, pr, half * 16:(half + 1) * 16] = c_re[i, gg].T
            s5C[i, mrows, 1, pr, half * 16:(half + 1) * 16] = c_im[i, gg].T
    shared = {"cst": cst, "s5F": s5F, "s5dtF": s5dtF, "s5M": s5M, "s5C": s5C}
    for k in ["w_in", "w_out", "mem_wk", "mem_wv", "ffn_w_gate", "ffn_w_up", "ffn_w_down", "ssm_w_glu", "diff_wk", "diff_wv"]:
        shared[k] = np.ascontiguousarray(g(k))
    x, mem = g("x"), g("mem")
    maps = []
    for b in range(n_cores):
        d = dict(shared)
        d["xT"] = np.ascontiguousarray(x[b].T)
        d["memT"] = np.ascontiguousarray(mem[b].T)
        maps.append(d)
    return maps


def kernel(**inputs):
    n_layers = int(inputs.pop("_n_layers", NL))
    maps = _prep_inputs(inputs)
    nc = build(n_layers)
    res = run_bass_kernel_spmd(nc, maps, core_ids=list(range(8)))
    out = np.stack([np.ascontiguousarray(r["yT"].T) for r in res.results], axis=0)
    return out.astype(np.float32)
```
